# Optimizing a Trainium2 kernel written in Bass

```python
import jax, jax.numpy as jnp
from jax import lax
import numpy as np

D_MODEL = 2048
BATCH = 4
SEQ = 4096
DEPTH = 2

MEM_LEN = 256
EPS = 1e-6

GLA_HEADS = 4
GLA_DK = 128
GLA_DV = 256
GLA_WIDTH = GLA_HEADS * GLA_DV
GLA_KWIDTH = GLA_HEADS * GLA_DK
GLA_CHUNK = 64
GK_RANK = 16
GK_NORMALIZER = 16.0

FOX_HEADS = 8
FOX_DH = 64
FOX_WIDTH = FOX_HEADS * FOX_DH
FOX_BLOCK = 128
FORGET_BIAS_INIT = 3.0

MEM_HEADS = 4
MEM_DH = 128
MEM_WIDTH = MEM_HEADS * MEM_DH

N_BRANCH = 3
MIX_WIDTH = GLA_WIDTH + FOX_WIDTH + MEM_WIDTH

IN_SIZES = (
    GLA_KWIDTH, GLA_KWIDTH, GLA_WIDTH, GLA_WIDTH, GK_RANK,
    FOX_WIDTH, FOX_WIDTH, FOX_WIDTH, FOX_HEADS, FOX_WIDTH,
    MEM_WIDTH, MEM_WIDTH,
    N_BRANCH * D_MODEL,
)
IN_COLS = int(sum(IN_SIZES))
IN_OFFSETS = tuple(int(v) for v in np.cumsum(IN_SIZES)[:-1])
BRANCH_ROWS = (0, GLA_WIDTH, GLA_WIDTH + FOX_WIDTH, MIX_WIDTH)

kernel_name = "hybrid_gla_fox_mem_gated_merge"


def rmsnorm(x, gain):
    xf = x.astype(jnp.float32)
    out = xf * lax.rsqrt(jnp.mean(xf * xf, axis=-1, keepdims=True) + EPS)
    return (out * gain.astype(jnp.float32)).astype(x.dtype)


def gla_chunked(q, k, v, g):
    B, S, H, DK = q.shape
    DV = v.shape[-1]
    C = GLA_CHUNK
    N = S // C

    def to_chunks(t):
        return t.astype(jnp.float32).reshape(B, N, C, H, -1).transpose(1, 0, 3, 2, 4)

    qc, kc, vc, gc = to_chunks(q * (DK ** -0.5)), to_chunks(k), to_chunks(v), to_chunks(g)
    causal = jnp.tril(jnp.ones((C, C), dtype=bool))

    def step(state, inp):
        qi, ki, vi, gi = inp
        b = jnp.cumsum(gi, axis=2)
        o_inter = jnp.einsum('bhcd,bhde->bhce', qi * jnp.exp(b), state)
        diff = b[:, :, :, None, :] - b[:, :, None, :, :]
        decay = jnp.exp(jnp.where(causal[None, None, :, :, None], diff, -jnp.inf))
        scores = jnp.einsum('bhid,bhjd,bhijd->bhij', qi, ki, decay)
        o_intra = jnp.einsum('bhij,bhje->bhie', scores, vi)
        b_last = b[:, :, -1:, :]
        k_dec = ki * jnp.exp(b_last - b)
        state = state * jnp.exp(b_last[:, :, 0, :, None]) + jnp.einsum('bhcd,bhce->bhde', k_dec, vi)
        return state, o_inter + o_intra

    state0 = jnp.zeros((B, H, DK, DV), jnp.float32)
    _, outs = lax.scan(step, state0, (qc, kc, vc, gc))
    return outs.transpose(1, 0, 3, 2, 4).reshape(B, S, H, DV)


def forgetting_attention(q, k, v, log_f):
    B, S, H, Dh = q.shape
    nb = S // FOX_BLOCK
    c = jnp.cumsum(log_f, axis=1).transpose(0, 2, 1)
    kh = k.transpose(0, 2, 1, 3)
    vh = v.transpose(0, 2, 1, 3)
    q_blocks = q.transpose(0, 2, 1, 3).reshape(B, H, nb, FOX_BLOCK, Dh).transpose(2, 0, 1, 3, 4)
    cq_blocks = c.reshape(B, H, nb, FOX_BLOCK).transpose(2, 0, 1, 3)
    key_pos = jnp.arange(S)
    scale = Dh ** -0.5

    def block(args):
        qb, cqb, idx = args
        s = jnp.einsum('bhqd,bhkd->bhqk', qb, kh).astype(jnp.float32) * scale
        s = s + cqb[..., None] - c[:, :, None, :]
        q_pos = idx * FOX_BLOCK + jnp.arange(FOX_BLOCK)
        s = jnp.where(key_pos[None, :] <= q_pos[:, None], s, -jnp.inf)
        p = jax.nn.softmax(s, axis=-1)
        return jnp.einsum('bhqk,bhkd->bhqd', p.astype(vh.dtype), vh)

    out = lax.map(block, (q_blocks, cq_blocks, jnp.arange(nb)))
    return out.transpose(1, 0, 3, 2, 4).reshape(B, S, H, Dh)


def memory_attention(q, mem_k, mem_v):
    s = jnp.einsum('bqhd,bkhd->bhqk', q, mem_k).astype(jnp.float32) * (q.shape[-1] ** -0.5)
    p = jax.nn.softmax(s, axis=-1)
    return jnp.einsum('bhqk,bkhd->bqhd', p.astype(mem_v.dtype), mem_v)


def setup_inputs(seed: int = 0) -> dict:
    key = jax.random.key(seed)
    ks = jax.random.split(key, 16)
    f32 = jnp.float32
    x = jax.random.normal(ks[0], (BATCH, SEQ, D_MODEL), f32)
    mem = jax.random.normal(ks[1], (BATCH, MEM_LEN, D_MODEL), f32)
    norm_gain = 1.0 + 0.02 * jax.random.normal(ks[2], (DEPTH, D_MODEL), f32)
    w_in = jax.random.normal(ks[3], (DEPTH, D_MODEL, IN_COLS), f32) * D_MODEL ** -0.5
    w_gk_up = jax.random.normal(ks[4], (DEPTH, GK_RANK, GLA_KWIDTH), f32) * GK_RANK ** -0.5
    b_gk = 0.1 * jax.random.normal(ks[5], (DEPTH, GLA_KWIDTH), f32)
    gla_norm_gain = 1.0 + 0.02 * jax.random.normal(ks[6], (DEPTH, GLA_DV), f32)
    b_f = FORGET_BIAS_INIT + 0.1 * jax.random.normal(ks[7], (DEPTH, FOX_HEADS), f32)
    mem_norm_gain = 1.0 + 0.02 * jax.random.normal(ks[8], (DEPTH, D_MODEL), f32)
    w_mem_kv = jax.random.normal(ks[9], (DEPTH, D_MODEL, 2 * MEM_WIDTH), f32) * D_MODEL ** -0.5
    kb = jax.random.split(ks[10], 3)
    w_branch = jnp.concatenate([
        jax.random.normal(kb[0], (DEPTH, GLA_WIDTH, D_MODEL), f32) * GLA_WIDTH ** -0.5,
        jax.random.normal(kb[1], (DEPTH, FOX_WIDTH, D_MODEL), f32) * FOX_WIDTH ** -0.5,
        jax.random.normal(kb[2], (DEPTH, MEM_WIDTH, D_MODEL), f32) * MEM_WIDTH ** -0.5,
    ], axis=1)
    w_out = jax.random.normal(ks[11], (DEPTH, D_MODEL, D_MODEL), f32) * D_MODEL ** -0.5
    final_gain = 1.0 + 0.02 * jax.random.normal(ks[12], (D_MODEL,), f32)
    return {"x": x, "mem": mem, "norm_gain": norm_gain, "w_in": w_in, "w_gk_up": w_gk_up,
            "b_gk": b_gk, "gla_norm_gain": gla_norm_gain, "b_f": b_f,
            "mem_norm_gain": mem_norm_gain, "w_mem_kv": w_mem_kv, "w_branch": w_branch,
            "w_out": w_out, "final_gain": final_gain}


def reference(x, mem, norm_gain, w_in, w_gk_up, b_gk, gla_norm_gain, b_f,
              mem_norm_gain, w_mem_kv, w_branch, w_out, final_gain):
    B, S, D = x.shape
    M = mem.shape[1]
    for l in range(DEPTH):
        h = rmsnorm(x, norm_gain[l])
        z = h @ w_in[l]
        (gq, gk_, gv, ggate, gdown, fq, fk, fv, flogit, fgate,
         mq, mgate, merge) = jnp.split(z, IN_OFFSETS, axis=-1)

        gk_log = jax.nn.log_sigmoid((gdown @ w_gk_up[l] + b_gk[l]).astype(jnp.float32)) / GK_NORMALIZER
        o_a = gla_chunked(gq.reshape(B, S, GLA_HEADS, GLA_DK), gk_.reshape(B, S, GLA_HEADS, GLA_DK),
                          gv.reshape(B, S, GLA_HEADS, GLA_DV), gk_log.reshape(B, S, GLA_HEADS, GLA_DK))
        o_a = rmsnorm(o_a, gla_norm_gain[l]).reshape(B, S, GLA_WIDTH).astype(x.dtype)
        o_a = o_a * jax.nn.silu(ggate)

        log_f = jax.nn.log_sigmoid((flogit + b_f[l]).astype(jnp.float32))
        o_b = forgetting_attention(fq.reshape(B, S, FOX_HEADS, FOX_DH), fk.reshape(B, S, FOX_HEADS, FOX_DH),
                                   fv.reshape(B, S, FOX_HEADS, FOX_DH), log_f)
        o_b = o_b.reshape(B, S, FOX_WIDTH) * jax.nn.silu(fgate)

        mkv = rmsnorm(mem, mem_norm_gain[l]) @ w_mem_kv[l]
        mk, mv = jnp.split(mkv, 2, axis=-1)
        o_c = memory_attention(mq.reshape(B, S, MEM_HEADS, MEM_DH), mk.reshape(B, M, MEM_HEADS, MEM_DH),
                               mv.reshape(B, M, MEM_HEADS, MEM_DH))
        o_c = o_c.reshape(B, S, MEM_WIDTH) * jax.nn.silu(mgate)

        gates = jax.nn.sigmoid(merge.reshape(B, S, N_BRANCH, D))
        wb = w_branch[l]
        y = (gates[:, :, 0] * (o_a @ wb[BRANCH_ROWS[0]:BRANCH_ROWS[1]])
             + gates[:, :, 1] * (o_b @ wb[BRANCH_ROWS[1]:BRANCH_ROWS[2]])
             + gates[:, :, 2] * (o_c @ wb[BRANCH_ROWS[2]:BRANCH_ROWS[3]]))
        x = x + y @ w_out[l]
    return rmsnorm(x, final_gain)
```

```python
import contextlib
import numpy as np
import concourse.bass as bass
import concourse.mybir as mybir
from concourse.bass_utils import run_bass_kernel_spmd

F32 = mybir.dt.float32
BF16 = mybir.dt.bfloat16
AF = mybir.ActivationFunctionType
ALU = mybir.AluOpType

D = 2048
TOK = 2048
NT = 16
KT = 16
NG = 4
DEPTH = 2
INC = 12312
MEM = 256
O_GQ, O_GK, O_GV, O_GG, O_GD, O_FQ, O_FK, O_FV, O_FL, O_FG, O_MQ, O_MG, O_MERGE = (
    0, 512, 1024, 2048, 3072, 3088, 3600, 4112, 4624, 4632, 5144, 5656, 6168)
EPS = 1e-6
SAME_ENGINE_SYNC = True


class Buf:
    __slots__ = ("name", "w", "r")

    def __init__(self, name):
        self.name = name
        self.w = None
        self.r = {}


class V:
    def __init__(self, ap, name, buf=None):
        self.ap = ap
        self.buf = buf or Buf(name)

    def __getitem__(self, k):
        return self.ap[k]


def _b(x):
    return getattr(x, "buf", x)


class Prog:
    ENGS = ("pe", "act", "dve", "pool", "sp")
    CENGS = ("pe", "act", "dve", "pool")

    def __init__(self, nc, es):
        self.nc = nc
        self.stream = {k: [] for k in self.ENGS}
        self.ecount = {k: 0 for k in self.ENGS}
        self.known = {k: {} for k in self.ENGS}
        self.sems = {}
        for k in self.CENGS:
            self.sems[("e", k)] = es.enter_context(nc.semaphore("es_" + k))
        self.dkeys = {}
        self.dcount = {}
        self.dnext = {}
        for q, n in {"sp": 12, "pool": 8}.items():
            self.dkeys[q] = []
            for i in range(n):
                key = ("d", q, i)
                self.sems[key] = es.enter_context(nc.semaphore(f"ds_{q}{i}"))
                self.dkeys[q].append(key)
                self.dcount[key] = 0
            self.dnext[q] = 0
        self.cckey = ("c", "cc")
        self.sems[self.cckey] = es.enter_context(nc.semaphore("cc_sem"))
        self.dcount[self.cckey] = 0
        self.signaled = {k: set() for k in self.CENGS}

    def _dep(self, eng, tok):
        if tok is None:
            return
        key, idx = tok
        if key[0] == "e" and key[1] == eng:
            if eng == "pe" or not SAME_ENGINE_SYNC:
                return
        if self.known[eng].get(key, 0) >= idx:
            return
        self.known[eng][key] = idx
        if key[0] == "e":
            self.signaled[key[1]].add(idx)
        self.stream[eng].append(("w", key, idx))

    def _deps(self, eng, reads, writes):
        for b in reads:
            self._dep(eng, _b(b).w)
        for b in writes:
            b = _b(b)
            self._dep(eng, b.w)
            for k, v in b.r.items():
                self._dep(eng, (k, v))

    def _mark(self, tok, reads, writes):
        for b in writes:
            b = _b(b)
            b.w = tok
            b.r = {}
        for b in reads:
            b = _b(b)
            if b.r.get(tok[0], 0) < tok[1]:
                b.r[tok[0]] = tok[1]

    def op(self, eng, fn, reads=(), writes=()):
        self._deps(eng, reads, writes)
        self.ecount[eng] += 1
        tok = (("e", eng), self.ecount[eng])
        self.stream[eng].append(("i", fn, tok))
        self._mark(tok, reads, writes)
        return tok

    def dma(self, q, out, in_, reads=(), writes=()):
        i = self.dnext[q]
        self.dnext[q] = (i + 1) % len(self.dkeys[q])
        key = self.dkeys[q][i]
        if self.dcount[key] > 0:
            self._dep(q, (key, self.dcount[key]))
        self._deps(q, reads, writes)
        self.dcount[key] += 16
        tok = (key, self.dcount[key])
        self.stream[q].append(("d", (out, in_), tok))
        self._mark(tok, reads, writes)
        return tok

    def cc(self, fn, reads=(), writes=()):
        q = "pool"
        key = self.cckey
        self._deps(q, reads, writes)
        self.dcount[key] += 1
        tok = (key, self.dcount[key])
        self.stream[q].append(("c", fn, tok))
        self._mark(tok, reads, writes)
        return tok

    def barrier(self):
        for e in self.ENGS:
            for k in self.CENGS:
                if k != e and self.ecount[k] > 0:
                    self._dep(e, (("e", k), self.ecount[k]))
            for key, cnt in self.dcount.items():
                if cnt > 0:
                    self._dep(e, (key, cnt))

    def emit(self):
        nc = self.nc
        rank = {}
        for k, s in self.signaled.items():
            rank[k] = {idx: r + 1 for r, idx in enumerate(sorted(s))}
        prog = self

        def run(engname, e):
            for ent in prog.stream[engname]:
                if ent[0] == "w":
                    _, key, idx = ent
                    val = rank[key[1]][idx] if key[0] == "e" else idx
                    e.wait_ge(prog.sems[key], val)
                elif ent[0] == "i":
                    _, fn, tok = ent
                    ins = fn(e)
                    if tok[1] in prog.signaled[engname]:
                        ins.then_inc(prog.sems[tok[0]], 1)
                elif ent[0] == "d":
                    _, (out, in_), tok = ent
                    e.dma_start(out=out, in_=in_).then_inc(prog.sems[tok[0]], 16)
                elif ent[0] == "c":
                    _, fn, tok = ent
                    fn(e).then_inc(prog.sems[tok[0]])

        with nc.Block() as block:
            @block.sync
            def _(e):
                run("sp", e)

            @block.tensor
            def _(e):
                run("pe", e)

            @block.scalar
            def _(e):
                run("act", e)

            @block.vector
            def _(e):
                run("dve", e)

            @block.gpsimd
            def _(e):
                run("pool", e)
        return {k: len(v) for k, v in self.stream.items()}


def build(dbg=None, nlayers=DEPTH, stop_after=None, ncores=8, nocc=False):
    nc = bass.Bass("TRN2", target_bir_lowering=False)
    es = contextlib.ExitStack()
    dbg = dbg or []
    with es:
        p = Prog(nc, es)

        def din(name, shape, dt=F32):
            return nc.dram_tensor(name, list(shape), dt, kind="ExternalInput").ap()

        def dscr(name, shape, dt, force_internal=False):
            kind = "ExternalOutput" if (name in dbg and not force_internal) else "Internal"
            return nc.dram_tensor(name, list(shape), dt, kind=kind).ap()

        x_d = din("x", [TOK, D])
        mem_d = din("mem", [MEM, D])
        ng_d = din("norm_gain", [DEPTH, D])
        win_d = din("w_in", [DEPTH, D, INC])
        wup_d = din("w_gk_up", [DEPTH, 16, 512])
        bgk_d = din("b_gk", [DEPTH, 512])
        gng_d = din("gla_norm_gain", [DEPTH, 256])
        bf_d = din("b_f", [DEPTH, 8])
        mng_d = din("mem_norm_gain", [DEPTH, D])
        wkv_d = din("w_mem_kv", [DEPTH, D, 1024])
        wbr_d = din("w_branch", [DEPTH, D, D])
        wout_d = din("w_out", [DEPTH, D, D])
        fg_d = din("final_gain", [1, D])
        flags_d = din("flags", [128, 2])
        out_d = nc.dram_tensor("out", [TOK, D], F32, kind="ExternalOutput").ap()

        xres = [dscr("xresA", [TOK, D], F32), dscr("xresB", [TOK, D], F32)]
        gqT_d = dscr("gqT", [4, 128, TOK], BF16)
        gkT_d = dscr("gkT", [4, 128, TOK], BF16)
        ktok_d = dscr("ktok", [TOK, 512], BF16)
        vtok_d = dscr("vtok", [TOK, 1024], BF16)
        gp_d = dscr("gp", [TOK, 512], F32)
        kdec_d = dscr("kdec", [TOK, 512], BF16)
        fqT_d = dscr("fqT", [4, 128, TOK], BF16)
        fkT_d = dscr("fkT", [512, TOK], BF16, True)
        fva_d = [dscr(f"fva{i}", [TOK // 2, 1024], BF16, True) for i in range(2)]
        cinU_d = dscr("cinU", [128, 1024], F32, True)
        cinS_d = dscr("cinS", [128, 128], F32, True)
        fkT_all = dscr("fkT_all", [1024, TOK], BF16, True)
        fva_all = [dscr(f"fva_all{i}", [TOK, 1024], BF16, True) for i in range(2)]
        coutU_d = dscr("coutU", [256, 1024], F32, True)
        coutS_d = dscr("coutS", [256, 128], F32, True)
        yT_d = dscr("yT", [16, 128, TOK], BF16)
        wconv_d = dscr("wconv", [16, 128, KT * 512], BF16)
        dbg_omix = dscr("dbg_omix", [16, 128, TOK], BF16) if "dbg_omix" in dbg else None
        dbg_hT = dscr("dbg_hT", [16, 128, TOK], BF16) if "dbg_hT" in dbg else None
        exB = {n: Buf(n) for n in ("fkT", "fva", "cinU", "cinS", "fkT_all", "fva_all", "coutU", "coutS")}

        def sb(name, shape, dt):
            h = es.enter_context(nc.sbuf_tensor(name, list(shape), dt))
            return V(h, name)

        hT = sb("hT", [128, KT, TOK], BF16)
        omix = sb("omix", [128, 16, TOK], BF16)
        ident = sb("ident", [128, 128], BF16)
        maskG = sb("maskG", [128, 128], BF16)
        maskGf = sb("maskGf", [128, 128], F32)
        Lrev = sb("Lrev", [128, 128], F32)
        Ucum = sb("Ucum", [128, 128], F32)
        SU = sb("SU", [128, 128], F32)
        onesf = sb("onesf", [128, 128], F32)
        onesb = sb("onesb", [128, 128], BF16)
        neg16 = sb("neg16", [128, 1], F32)
        c_eps = sb("c_eps", [128, 1], F32)
        c_one = sb("c_one", [128, 1], F32)
        flags = sb("flags_sb", [128, 2], F32)
        gdT = sb("gdT", [32, TOK], BF16)
        memKT = sb("memKT", [128, 4, MEM], BF16)
        memV = sb("memV", [128, 2, 512], BF16)
        lfp = sb("lfp", [128, NT, 8], F32)
        SUF = sb("SUF", [128, NT, 8], F32)
        SUFo = sb("SUFo", [128, NT, 8], F32)
        Tn = sb("Tn", [128, NG, 8], F32)
        PREn = sb("PREn", [128, NG, 8], F32)
        bias_all = sb("bias_all", [128, NG, 32, 8], F32)
        Sst = sb("Sst", [128, 4, 256], F32)
        Sbf = sb("Sbf", [128, 4, 256], BF16)
        wupb = sb("wupb", [32, 512], BF16)
        gng_bc = sb("gng_bc", [128, 256], F32)
        bf_bc = sb("bf_bc", [128, 8], F32)
        small = sb("small", [128, 64], F32)

        ARENA_COLS = 12800
        arena_h = es.enter_context(nc.sbuf_tensor("arena", [128, ARENA_COLS], F32))

        class Arena:
            def __init__(self):
                self.off = 0
                self.gen = 0

            def reset(self):
                self.off = 0
                self.gen += 1

            def alloc(self, name, shape, dt, parts=128):
                n = int(np.prod(shape[1:]))
                ncol = (n * (2 if dt == BF16 else 4) + 3) // 4
                ncol = (ncol + 7) // 8 * 8
                assert self.off + ncol <= ARENA_COLS, (name, self.off, ncol)
                ap = arena_h[0:shape[0], self.off:self.off + ncol]
                self.off += ncol
                if dt == BF16:
                    ap = ap.bitcast(BF16)
                ap = ap[:, 0:n]
                if len(shape) == 3:
                    ap = ap.rearrange("p (a b) -> p a b", a=shape[1])
                elif len(shape) == 4:
                    ap = ap.rearrange("p (a b c) -> p a b c", a=shape[1], b=shape[2])
                return V(ap, f"{name}_{self.gen}")

        ar = Arena()

        psb = []
        for i in range(8):
            h = es.enter_context(nc.psum_tensor(f"psb{i}", [128, 512], F32))
            psb.append(V(h, f"psb{i}"))
        ps_rr = [0]

        def nextps():
            i = ps_rr[0]
            ps_rr[0] = (i + 1) % 8
            return psb[i]

        def psbf(ps):
            return ps.ap[:, :].bitcast(BF16)

        def A(eng, f, r=(), w=()):
            return p.op(eng, f, reads=r, writes=w)

        def mm(out, lhsT, rhs, start, stop, r, w):
            A("pe", lambda e, out=out, lhsT=lhsT, rhs=rhs, start=start, stop=stop:
              e.matmul(out, lhsT=lhsT, rhs=rhs, start=start, stop=stop), r, w)

        evac_rr = [0]

        def evac(out, in_, r, w, func=None, scale=1.0, eng=None):
            if func is None and eng is None:
                evac_rr[0] ^= 1
                eng = "act" if evac_rr[0] else "dve"
            if func is not None or eng == "act":
                f = func if func is not None else AF.Copy
                A("act", lambda e, out=out, in_=in_, f=f, scale=scale: e.activation(out=out, in_=in_, func=f, scale=scale), r, w)
            else:
                if scale == 1.0:
                    A("dve", lambda e, out=out, in_=in_: e.tensor_copy(out=out, in_=in_), r, w)
                else:
                    A("dve", lambda e, out=out, in_=in_, scale=scale: e.tensor_scalar(out=out, in0=in_, scalar1=scale, scalar2=None, op0=ALU.mult), r, w)

        def fill_tri(t, val, kind):
            A("pool", lambda e: e.memset(t[:], val), (), [t])
            if kind == "gt":
                kw = dict(pattern=[[-1, 128]], compare_op=ALU.is_gt, base=0, channel_multiplier=1)
            elif kind == "le":
                kw = dict(pattern=[[1, 128]], compare_op=ALU.is_gt, base=1, channel_multiplier=-1)
            else:
                kw = dict(pattern=[[-1, 128]], compare_op=ALU.is_equal, base=0, channel_multiplier=1)
            A("pool", lambda e: e.affine_select(out=t[:], in_=t[:], fill=0.0, **kw), [t], [t])

        fill_tri(maskGf, 1.0, "le")
        fill_tri(Lrev, -1.0 / 16.0, "gt")
        fill_tri(Ucum, -1.0 / 16.0, "le")
        fill_tri(SU, 1.0, "gt")
        fill_tri(onesf, 1.0, "eq")
        A("dve", lambda e: e.tensor_copy(out=ident[:], in_=onesf[:]), [onesf], [ident])
        A("dve", lambda e: e.tensor_copy(out=maskG[:], in_=maskGf[:]), [maskGf], [maskG])
        A("pool", lambda e: e.memset(onesf[:], 1.0), (), [onesf])
        A("pool", lambda e: e.memset(onesb[:], 1.0), (), [onesb])
        A("pool", lambda e: e.memset(neg16[:], -1.0 / 16.0), (), [neg16])
        A("pool", lambda e: e.memset(c_eps[:], EPS), (), [c_eps])
        A("pool", lambda e: e.memset(c_one[:], 1.0), (), [c_one])
        A("pool", lambda e: e.memset(gdT[:], 1.0), (), [gdT])
        p.dma("sp", flags[:], flags_d[:, :], (), [flags])
        p.barrier()

        def norm_transpose(src_fn, ntiles, gain_ap, dst, dst_is_hT=True):
            ar.reset()
            xin = [ar.alloc(f"xin{i}", [128, D], F32) for i in range(2)]
            xs = [ar.alloc(f"xs{i}", [128, D], BF16) for i in range(2)]
            gbc = ar.alloc("gbc", [128, D], F32)
            junk = ar.alloc("junk", [128, D], BF16)
            st = [ar.alloc(f"nst{i}", [128, 4], F32) for i in range(2)]
            p.dma("sp", gbc[:], gain_ap.partition_broadcast(128), (), [gbc])
            def stats(i):
                xt, xb, s = xin[i % 2], xs[i % 2], st[i % 2]
                p.dma("sp", xt[:], src_fn(i), (), [xt])
                A("act", lambda e, xt=xt, s=s: e.activation(out=junk[:], in_=xt[:], func=AF.Square, accum_out=s[:, 0:1]), [xt], [junk, s])
                A("act", lambda e, s=s: e.activation(out=s[:, 1:2], in_=s[:, 0:1], func=AF.Sqrt, scale=1.0 / D, bias=c_eps[:]), [s, c_eps], [s])
                A("dve", lambda e, s=s: e.reciprocal(out=s[:, 2:3], in_=s[:, 1:2]), [s], [s])
                A("dve", lambda e, xt=xt, xb=xb, s=s: e.scalar_tensor_tensor(out=xb[:], in0=xt[:], scalar=s[:, 2:3], in1=gbc[:],
                                                                              op0=ALU.mult, op1=ALU.mult), [xt, s, gbc], [xb])

            def trans(i):
                xb = xs[i % 2]
                for g in range(4):
                    ps = nextps()
                    pv = psbf(ps)
                    for j in range(4):
                        kt = g * 4 + j
                        A("pe", lambda e, pv=pv, j=j, xb=xb, kt=kt: e.transpose(pv[:, j * 128:(j + 1) * 128], xb[:, kt * 128:(kt + 1) * 128], ident[:]),
                          [xb, ident], [ps])
                    evac(dst[:, g * 4:(g + 1) * 4, i * 128:(i + 1) * 128], pv[:, 0:512].rearrange("p (a b) -> p a b", a=4), [ps], [dst])

            stats(0)
            for i in range(ntiles):
                if i + 1 < ntiles:
                    stats(i + 1)
                trans(i)
            p.barrier()

        def load_w(wsrc, c0, ncols, wb_t, c_dst=0):
            src = wsrc[:, c0:c0 + ncols].rearrange("(kt p) c -> p kt c", p=128)
            for hf in range(2):
                p.dma("pool", wb_t[:, hf * 8:(hf + 1) * 8, c_dst:c_dst + ncols], src[:, hf * 8:(hf + 1) * 8, :], (), [wb_t])

        def proj_F(wb_t, m0, msz, act_T, ntok, consume):
            for n in range(ntok // 512 if ntok >= 512 else 1):
                nn = min(512, ntok)
                ps = nextps()
                for kt in range(KT):
                    mm(ps[0:msz, 0:nn], wb_t[:, kt, m0:m0 + msz], act_T[:, kt, n * 512:n * 512 + nn], kt == 0, kt == KT - 1, [wb_t, act_T], [ps])
                consume(n, ps)

        def proj_T(wb_t, c0, ncols, act_T, ntiles, consume):
            for i in range(ntiles):
                ps = nextps()
                for kt in range(KT):
                    mm(ps[:, 0:ncols], act_T[:, kt, i * 128:(i + 1) * 128], wb_t[:, kt, c0:c0 + ncols], kt == 0, kt == KT - 1, [wb_t, act_T], [ps])
                consume(i, ps)

        for l in range(nlayers):
            x_src = x_d if l == 0 else xres[(l - 1) % 2]
            x_dst = xres[l % 2]
            win_l, wbr_l, wout_l, wkv_l = win_d[l], wbr_d[l], wout_d[l], wkv_d[l]

            p.dma("pool", wupb[0:16, :], wup_d[l], (), [wupb])
            p.dma("pool", wupb[16:17, :], bgk_d[l:l + 1, :], (), [wupb])
            p.dma("sp", gng_bc[:], gng_d[l:l + 1, :].partition_broadcast(128), (), [gng_bc])
            p.dma("sp", bf_bc[:], bf_d[l:l + 1, :].partition_broadcast(128), (), [bf_bc])

            norm_transpose(lambda i: mem_d[i * 128:(i + 1) * 128, :], 2, mng_d[l:l + 1, :], hT)
            ar.reset()
            wb2 = [ar.alloc(f"wb{i}", [128, KT, 512], BF16) for i in range(2)]
            load_w(wkv_l, 0, 512, wb2[0])
            load_w(wkv_l, 512, 512, wb2[1])
            for h in range(4):
                def cons(n, ps, h=h):
                    evac(memKT[:, h, :], ps[:, 0:MEM], [ps], [memKT])
                proj_F(wb2[0], h * 128, 128, hT, MEM, cons)

            def cons(i, ps):
                evac(memV[:, i, :], ps[:, :], [ps], [memV])
            proj_T(wb2[1], 0, 512, hT, 2, cons)
            p.barrier()

            norm_transpose(lambda i: x_src[i * 128:(i + 1) * 128, :], NT, ng_d[l:l + 1, :], hT)
            if dbg_hT is not None and l == 0:
                for kt in range(KT):
                    p.dma("sp", dbg_hT[kt], hT[:, kt, :], [hT], ())
                p.barrier()

            ar.reset()
            wb2 = [ar.alloc(f"wb{i}", [128, KT, 512], BF16) for i in range(2)]
            stg = [ar.alloc(f"stg{i}", [128, 4, 512], BF16) for i in range(3)]
            stv = [ar.alloc(f"stv{i}", [128, 8, 128], BF16) for i in range(2)]
            lft = ar.alloc("lft", [128, 16], F32)
            for t in stv:
                A("pool", lambda e, t=t: e.memset(t[:], 1.0), (), [t])
            groups = [("gq", O_GQ), ("gk", O_GK), ("gv0", O_GV), ("gv1", O_GV + 512), ("small", None),
                      ("fq", O_FQ), ("fk", O_FK), ("fv", O_FV)]

            def issue_load(gi):
                name, c0 = groups[gi]
                wt = wb2[gi % 2]
                if name == "small":
                    load_w(win_l, O_GD, 16, wt, 0)
                    load_w(win_l, O_FL, 8, wt, 16)
                else:
                    load_w(win_l, c0, 512, wt)
            issue_load(0)
            sg = [0]

            def F_to_dram(wt, dst3, scale=1.0):
                for n in range(NG):
                    s = stg[sg[0] % 3]
                    sg[0] += 1
                    for m in range(4):
                        ps = nextps()
                        for kt in range(KT):
                            mm(ps[:, :], wt[:, kt, m * 128:(m + 1) * 128], hT[:, kt, n * 512:(n + 1) * 512], kt == 0, kt == KT - 1, [wt, hT], [ps])
                        evac(s[:, m, :], ps[:, :], [ps], [s], scale=scale)
                    p.dma("sp", dst3[:, :, n * 512:(n + 1) * 512].rearrange("m p t -> p m t"), s[:], [s], ())

            def T_to_dram(wt, dst2, c_dst):
                for i4 in range(4):
                    s = stg[sg[0] % 3]
                    sg[0] += 1
                    for j in range(4):
                        i = i4 * 4 + j
                        ps = nextps()
                        for kt in range(KT):
                            mm(ps[:, :], hT[:, kt, i * 128:(i + 1) * 128], wt[:, kt, 0:512], kt == 0, kt == KT - 1, [wt, hT], [ps])
                        evac(s[:, j, :], ps[:, :], [ps], [s])
                    p.dma("sp", dst2[i4 * 512:(i4 + 1) * 512, c_dst:c_dst + 512].rearrange("(j p) c -> p j c", p=128), s[:], [s], ())

            for gi, (name, c0) in enumerate(groups):
                if gi + 1 < len(groups):
                    issue_load(gi + 1)
                wt = wb2[gi % 2]
                if name == "gq":
                    F_to_dram(wt, gqT_d, scale=128.0 ** -0.5)
                elif name == "gk":
                    F_to_dram(wt, gkT_d)
                    T_to_dram(wt, ktok_d, 0)
                elif name == "gv0":
                    T_to_dram(wt, vtok_d, 0)
                elif name == "gv1":
                    T_to_dram(wt, vtok_d, 512)
                elif name == "fq":
                    F_to_dram(wt, fqT_d)
                elif name == "fk":
                    F_to_dram(wt, fkT_d.rearrange("(m p) t -> m p t", p=128))
                elif name == "fv":
                    for i in range(NT):
                        s = stv[i % 2]
                        ps = nextps()
                        for kt in range(KT):
                            mm(ps[:, :], hT[:, kt, i * 128:(i + 1) * 128], wt[:, kt, 0:512], kt == 0, kt == KT - 1, [wt, hT], [ps])
                        pv = ps[:, :].rearrange("p (a b c) -> p a b c", a=4, b=2)
                        sv = s[:, :, :].rearrange("p (a b) c -> p a b c", b=2)
                        A("dve", lambda e, sv=sv, pv=pv: e.tensor_copy(out=sv[:, :, 0, 0:64], in_=pv[:, :, 0, :]), [ps], [s])
                        A("dve", lambda e, sv=sv, pv=pv: e.tensor_copy(out=sv[:, :, 1, 64:128], in_=pv[:, :, 1, :]), [ps], [s])
                        p.dma("sp", fva_d[i // 8][(i % 8) * 128:(i % 8 + 1) * 128, :], s[:, :, :].rearrange("p a b -> p (a b)"), [s], [exB["fva"]])
                elif name == "small":
                    def cons(n, ps):
                        evac(gdT[0:16, n * 512:(n + 1) * 512], ps[0:16, :], [ps], [gdT])
                    proj_F(wt, 0, 16, hT, TOK, cons)
                    for i in range(NT):
                        ps = nextps()
                        for kt in range(KT):
                            mm(ps[:, 0:8], hT[:, kt, i * 128:(i + 1) * 128], wt[:, kt, 16:24], kt == 0, kt == KT - 1, [wt, hT], [ps])
                        A("dve", lambda e, ps=ps: e.tensor_tensor(out=lft[:, 0:8], in0=ps[:, 0:8], in1=bf_bc[:], op=ALU.add), [ps, bf_bc], [lft])
                        A("act", lambda e: e.activation(out=lft[:, 8:16], in_=lft[:, 0:8], func=AF.Exp, scale=-1.0), [lft], [lft])
                        A("act", lambda e, i=i: e.activation(out=lfp[:, i, :], in_=lft[:, 8:16], func=AF.Ln, bias=c_one[:], scale=1.0), [lft, c_one], [lfp])
            p.barrier()
            if stop_after == "A2":
                break

            ar.reset()
            kt_t = [ar.alloc(f"ktok{i}", [128, 512], BF16) for i in range(2)]
            vt_t = [ar.alloc(f"vtok{i}", [128, 1024], BF16) for i in range(2)]
            t1 = [ar.alloc(f"t1{i}", [128, 512], F32) for i in range(2)]
            gp_t = [ar.alloc(f"gp{i}", [128, 512], F32) for i in range(2)]
            eR = [ar.alloc(f"eR{i}", [128, 512], F32) for i in range(2)]
            kd_t = [ar.alloc(f"kd{i}", [128, 512], BF16) for i in range(2)]
            dec = [ar.alloc(f"dec{i}", [128, 4], F32) for i in range(2)]
            A("pool", lambda e: e.memset(Sst[:], 0.0), (), [Sst])
            for i in range(NT):
                k_, v_, t_, g_, r_, d_, dc = kt_t[i % 2], vt_t[i % 2], t1[i % 2], gp_t[i % 2], eR[i % 2], kd_t[i % 2], dec[i % 2]
                p.dma("sp", k_[:], ktok_d[i * 128:(i + 1) * 128, :], (), [k_])
                p.dma("sp", v_[:], vtok_d[i * 128:(i + 1) * 128, :], (), [v_])
                ps = nextps()
                mm(ps[:, :], gdT[0:17, i * 128:(i + 1) * 128], wupb[0:17, :], True, True, [gdT, wupb], [ps])
                A("act", lambda e, t_=t_, ps=ps: e.activation(out=t_[:], in_=ps[:, :], func=AF.Exp, scale=-1.0), [ps], [t_])
                A("act", lambda e, t_=t_, g_=g_: e.activation(out=g_[:], in_=t_[:], func=AF.Ln, bias=c_one[:], scale=1.0), [t_, c_one], [g_])
                p.dma("sp", gp_d[i * 128:(i + 1) * 128, :], g_[:], [g_], ())
                ps2 = nextps()
                mm(ps2[:, :], Lrev[:], g_[:], True, True, [Lrev, g_], [ps2])
                A("act", lambda e, r_=r_, ps2=ps2: e.activation(out=r_[:], in_=ps2[:, :], func=AF.Exp), [ps2], [r_])
                A("dve", lambda e, d_=d_, k_=k_, r_=r_: e.tensor_tensor(out=d_[:], in0=k_[:], in1=r_[:], op=ALU.mult), [k_, r_], [d_])
                p.dma("sp", kdec_d[i * 128:(i + 1) * 128, :], d_[:], [d_], ())
                ps3 = nextps()
                for h in range(4):
                    mm(ps3[:, h:h + 1], g_[:, h * 128:(h + 1) * 128], neg16[:], True, True, [g_, neg16], [ps3])
                A("act", lambda e, dc=dc, ps3=ps3: e.activation(out=dc[:], in_=ps3[:, 0:4], func=AF.Exp), [ps3], [dc])
                for hp in range(2):
                    ps4 = nextps()
                    for hh in range(2):
                        h = hp * 2 + hh
                        mm(ps4[:, hh * 256:(hh + 1) * 256], d_[:, h * 128:(h + 1) * 128], v_[:, h * 256:(h + 1) * 256], True, True, [d_, v_], [ps4])
                    for hh in range(2):
                        h = hp * 2 + hh
                        A("dve", lambda e, h=h, hh=hh, dc=dc, ps4=ps4: e.scalar_tensor_tensor(
                            out=Sst[:, h, :], in0=Sst[:, h, :], scalar=dc[:, h:h + 1], in1=ps4[:, hh * 256:(hh + 1) * 256],
                            op0=ALU.mult, op1=ALU.add), [Sst, dc, ps4], [Sst])
            p.dma("sp", cinU_d[:, :], Sst[:, :, :].rearrange("p a b -> p (a b)"), [Sst], [exB["cinU"]])

            rs_sb = ar.alloc("rs_sb", [128, NT, 8], F32)
            tot_sb = ar.alloc("tot_sb", [128, NT, 8], F32)
            acc = ar.alloc("acc", [128, 8], F32)
            psr, pst = nextps(), nextps()
            for j in range(NT):
                mm(psr[:, j * 8:(j + 1) * 8], SU[:], lfp[:, j, :], True, True, [SU, lfp], [psr])
                mm(pst[:, j * 8:(j + 1) * 8], onesf[:], lfp[:, j, :], True, True, [onesf, lfp], [pst])
            evac(rs_sb[:, :, :].rearrange("p a b -> p (a b)"), psr[:, 0:128], [psr], [rs_sb], eng="dve")
            evac(tot_sb[:, :, :].rearrange("p a b -> p (a b)"), pst[:, 0:128], [pst], [tot_sb], eng="dve")
            A("dve", lambda e: e.memset(acc[:], 0.0), (), [acc])
            A("dve", lambda e: e.memset(Tn[:, 3, :], 0.0), (), [Tn])
            for j in range(NT - 1, -1, -1):
                A("dve", lambda e, j=j: e.tensor_tensor(out=SUF[:, j, :], in0=rs_sb[:, j, :], in1=acc[:], op=ALU.add), [rs_sb, acc], [SUF])
                A("dve", lambda e, j=j: e.tensor_tensor(out=acc[:], in0=acc[:], in1=tot_sb[:, j, :], op=ALU.add), [acc, tot_sb], [acc])
                if j % 4 == 0 and j > 0:
                    n = j // 4 - 1
                    A("dve", lambda e, n=n: e.tensor_copy(out=Tn[:, n, :], in_=acc[:]), [acc], [Tn])
            for n in range(NG):
                A("dve", lambda e, n=n: e.tensor_tensor(out=PREn[:, n, :], in0=acc[:], in1=Tn[:, n, :], op=ALU.subtract), [acc, Tn], [PREn])
            p.dma("sp", cinS_d[:, :], SUF[:, :, :].rearrange("p a b -> p (a b)"), [SUF], [exB["cinS"]])
            p.barrier()

            rg = [[2 * i, 2 * i + 1] for i in range(ncores // 2)]
            for src, dst, sn, dn in ((fkT_d, fkT_all, "fkT", "fkT_all"), (fva_d[0], fva_all[0], "fva", "fva_all"), (fva_d[1], fva_all[1], "fva", "fva_all"),
                                     (cinU_d, coutU_d, "cinU", "coutU"), (cinS_d, coutS_d, "cinS", "coutS")):
                if nocc:
                    continue
                p.cc(lambda e, src=src, dst=dst: e.collective_compute("AllGather", ALU.bypass, replica_groups=rg,
                                                                      ins=[src[:, :]], outs=[dst[:, :]]), [exB[sn]], [exB[dn]])
            ar.reset()
            wb2 = [ar.alloc(f"wbg{i}", [128, KT, 512], BF16) for i in range(2)]
            ggroups = [(O_GG, 0), (O_GG + 512, 4), (O_FG, 8), (O_MG, 12)]
            load_w(win_l, ggroups[0][0], 512, wb2[0])
            for gi, (c0, ch0) in enumerate(ggroups):
                if gi + 1 < len(ggroups):
                    load_w(win_l, ggroups[gi + 1][0], 512, wb2[(gi + 1) % 2])
                for c in range(4):
                    def cons(n, ps, ch=ch0 + c):
                        evac(omix[:, ch, n * 512:(n + 1) * 512], ps[:, :], [ps], [omix], func=AF.Silu)
                    proj_F(wb2[gi % 2], c * 128, 128, hT, TOK, cons)
            p.barrier()
            if stop_after == "A3":
                break

            ar.reset()
            qT_t = [ar.alloc(f"qT{i}", [128, 4, 128], BF16) for i in range(2)]
            kT_t = [ar.alloc(f"kT{i}", [128, 4, 128], BF16) for i in range(2)]
            kd_t = [ar.alloc(f"kd{i}", [128, 512], BF16) for i in range(2)]
            vt_t = [ar.alloc(f"vtok{i}", [128, 1024], BF16) for i in range(2)]
            gp_t = [ar.alloc(f"gp{i}", [128, 512], F32) for i in range(2)]
            NS = 3
            E_t = [[ar.alloc(f"E{k}_{i}", [128, 128], F32) for i in range(4)] for k in range(NS)]
            Ei_t = [[ar.alloc(f"Ei{k}_{i}", [128, 128], F32) for i in range(4)] for k in range(NS)]
            qd_t = [[ar.alloc(f"qd{k}_{i}", [128, 128], BF16) for i in range(4)] for k in range(NS)]
            kdd_t = [[ar.alloc(f"kdd{k}_{i}", [128, 128], BF16) for i in range(4)] for k in range(NS)]
            at_t = [[ar.alloc(f"at{k}_{i}", [128, 128], BF16) for i in range(4)] for k in range(NS)]
            on_t = [ar.alloc(f"on{i}", [128, 1024], BF16) for i in range(2)]
            nst = [ar.alloc(f"gst{i}", [128, 16], F32) for i in range(2)]
            junkg = ar.alloc("junkg", [128, 256], BF16)
            Uin = ar.alloc("Uin", [128, 1024], F32)
            p.dma("sp", Uin[:], coutU_d[0:128, :], [exB["coutU"]], [Uin])
            A("dve", lambda e: e.tensor_scalar(out=Sst[:, :, :].rearrange("p a b -> p (a b)"), in0=Uin[:], scalar1=flags[:, 1:2], scalar2=None, op0=ALU.mult),
              [Uin, flags], [Sst])
            A("act", lambda e: e.activation(out=Sbf[:, :, :].rearrange("p a b -> p (a b)"), in_=Sst[:, :, :].rearrange("p a b -> p (a b)"), func=AF.Copy), [Sst], [Sbf])
            g_rr = [0]

            def gps():
                g_rr[0] = (g_rr[0] + 1) % 5
                return psb[3 + g_rr[0]]

            def gla_A1(i):
                q_, k_, g_ = qT_t[i % 2], kT_t[i % 2], gp_t[i % 2]
                tsl = slice(i * 128, (i + 1) * 128)
                p.dma("sp", q_[:], gqT_d[:, :, tsl].rearrange("m p t -> p m t"), (), [q_])
                p.dma("sp", k_[:], gkT_d[:, :, tsl].rearrange("m p t -> p m t"), (), [k_])
                p.dma("sp", g_[:], gp_d[tsl, :], (), [g_])
                pbs = []
                for h in range(4):
                    psb_ = gps()
                    mm(psb_[:, 0:128], g_[:, h * 128:(h + 1) * 128], Ucum[:], True, True, [g_, Ucum], [psb_])
                    E, Ei = E_t[i % NS][h], Ei_t[i % NS][h]
                    A("act", lambda e, E=E, psb_=psb_: e.activation(out=E[:], in_=psb_[:, 0:128], func=AF.Exp), [psb_], [E])
                    A("act", lambda e, Ei=Ei, psb_=psb_: e.activation(out=Ei[:], in_=psb_[:, 0:128], func=AF.Exp, scale=-1.0), [psb_], [Ei])
                for h in range(4):
                    E, Ei, qd, kdd = E_t[i % NS][h], Ei_t[i % NS][h], qd_t[i % NS][h], kdd_t[i % NS][h]
                    A("dve", lambda e, qd=qd, q_=q_, h=h, E=E: e.tensor_tensor(out=qd[:], in0=q_[:, h, :], in1=E[:], op=ALU.mult), [q_, E], [qd])
                    A("dve", lambda e, kdd=kdd, k_=k_, h=h, Ei=Ei: e.tensor_tensor(out=kdd[:], in0=k_[:, h, :], in1=Ei[:], op=ALU.mult), [k_, Ei], [kdd])

            def gla_A2(i):
                d_, v_ = kd_t[i % 2], vt_t[i % 2]
                tsl = slice(i * 128, (i + 1) * 128)
                p.dma("sp", d_[:], kdec_d[tsl, :], (), [d_])
                p.dma("sp", v_[:], vtok_d[tsl, :], (), [v_])
                for h in range(4):
                    qd, kdd, at = qd_t[i % NS][h], kdd_t[i % NS][h], at_t[i % NS][h]
                    psa = gps()
                    mm(psa[:, 0:128], kdd[:], qd[:], True, True, [kdd, qd], [psa])
                    A("dve", lambda e, at=at, psa=psa: e.tensor_tensor(out=at[:], in0=psa[:, 0:128], in1=maskGf[:], op=ALU.mult), [psa, maskGf], [at])

            def gla_B(i):
                d_, v_ = kd_t[i % 2], vt_t[i % 2]
                on, s = on_t[i % 2], nst[i % 2]
                tsl = slice(i * 128, (i + 1) * 128)
                pso = [psb[0], psb[1]]
                for h in range(4):
                    E, qd, at = E_t[i % NS][h], qd_t[i % NS][h], at_t[i % NS][h]
                    hs = slice(h * 128, (h + 1) * 128)
                    vs = slice(h * 256, (h + 1) * 256)
                    po = pso[h // 2]
                    pos = slice((h % 2) * 256, (h % 2 + 1) * 256)
                    mm(po[:, pos], qd[:], Sbf[:, h, :], True, False, [qd, Sbf], [po])
                    mm(po[:, pos], at[:], v_[:, vs], False, True, [at, v_], [po])
                    pss = gps()
                    mm(pss[:, 0:256], d_[:, hs], v_[:, vs], True, True, [d_, v_], [pss])
                    A("dve", lambda e, h=h, E=E, pss=pss: e.scalar_tensor_tensor(out=Sst[:, h, :], in0=Sst[:, h, :], scalar=E[:, 127:128], in1=pss[:, 0:256],
                                                                              op0=ALU.mult, op1=ALU.add), [Sst, E, pss], [Sst])
                    A("act", lambda e, h=h: e.activation(out=Sbf[:, h, :], in_=Sst[:, h, :], func=AF.Copy), [Sst], [Sbf])
                for h in range(4):
                    po = pso[h // 2]
                    pos = slice((h % 2) * 256, (h % 2 + 1) * 256)
                    A("act", lambda e, po=po, pos=pos, s=s, h=h: e.activation(out=junkg[:], in_=po[:, pos], func=AF.Square, accum_out=s[:, h:h + 1]), [po], [junkg, s])
                A("act", lambda e, s=s: e.activation(out=s[:, 4:8], in_=s[:, 0:4], func=AF.Sqrt, scale=1.0 / 256.0, bias=c_eps[:]), [s, c_eps], [s])
                A("dve", lambda e, s=s: e.reciprocal(out=s[:, 8:12], in_=s[:, 4:8]), [s], [s])
                for h in range(4):
                    po = pso[h // 2]
                    pos = slice((h % 2) * 256, (h % 2 + 1) * 256)
                    A("dve", lambda e, on=on, po=po, pos=pos, s=s, h=h: e.scalar_tensor_tensor(
                        out=on[:, h * 256:(h + 1) * 256], in0=po[:, pos], scalar=s[:, 8 + h:9 + h], in1=gng_bc[:], op0=ALU.mult, op1=ALU.mult),
                      [po, s, gng_bc], [on])
                pt = psb[2]
                ptv = psbf(pt)
                for c in range(8):
                    A("pe", lambda e, ptv=ptv, c=c, on=on: e.transpose(ptv[:, c * 128:(c + 1) * 128], on[:, c * 128:(c + 1) * 128], ident[:]), [on, ident], [pt])
                A("dve", lambda e, ptv=ptv, tsl=tsl: e.tensor_tensor(out=omix[:, 0:8, tsl], in0=ptv[:, :].rearrange("p (a b) -> p a b", a=8),
                                                                      in1=omix[:, 0:8, tsl], op=ALU.mult), [pt, omix], [omix])

            gla_A1(0)
            gla_A1(1)
            gla_A2(0)
            for i in range(NT):
                if i + 2 < NT:
                    gla_A1(i + 2)
                if i + 1 < NT:
                    gla_A2(i + 1)
                gla_B(i)
            p.barrier()
            if stop_after == "B1a":
                break

            ar.reset()
            p.dma("sp", SUFo[:, :, :].rearrange("p a b -> p (a b)"), coutS_d[0:128, :], [exB["coutS"]], [SUFo])
            for n in range(NG):
                A("dve", lambda e, n=n: e.tensor_scalar(out=small[:, 0:8], in0=PREn[:, n, :], scalar1=-1.0, scalar2=flags[:, 0:1], op0=ALU.mult, op1=ALU.add),
                  [PREn, flags], [small])
                for j in range(NT):
                    A("dve", lambda e, n=n, j=j: e.tensor_tensor(out=bias_all[:, n, j, :], in0=small[:, 0:8], in1=SUFo[:, j, :], op=ALU.subtract),
                      [small, SUFo], [bias_all])
                    A("dve", lambda e, n=n, j=j: e.tensor_tensor(out=bias_all[:, n, 16 + j, :], in0=Tn[:, n, :], in1=SUF[:, j, :], op=ALU.subtract),
                      [Tn, SUF], [bias_all])
            QT = [ar.alloc(f"QT{i}", [128, 8, 512], BF16) for i in range(2)]
            for t in QT:
                A("pool", lambda e, t=t: e.memset(t[:], 0.0), (), [t])
            NKV = 4
            KTt = [ar.alloc(f"KTt{i}", [128, 4, 128], BF16) for i in range(NKV)]
            Vt = [ar.alloc(f"Vt{i}", [128, 8, 128], BF16) for i in range(NKV)]
            PT = [ar.alloc(f"PT{i}", [128, 512], BF16) for i in range(4)]
            rc = [ar.alloc(f"rc{i}", [128, 512], F32) for i in range(2)]
            tmpo = [ar.alloc(f"tmpo{i}", [128, 512], F32) for i in range(2)]
            acs = [ar.alloc(f"acs{i}", [128, 512], F32) for i in range(4)]
            conv_list = [(f, t, hf) for f in range(16) for t in range(4) for hf in range(2)]

            def conv_dma(f, t, hf):
                c0 = f * 128
                srcw = wbr_l[:, c0:c0 + 128] if t == 0 else win_l[:, O_MERGE + (t - 1) * D + c0:O_MERGE + (t - 1) * D + c0 + 128]
                src = srcw.rearrange("(kt p) c -> p kt c", p=128)[:, hf * 8:(hf + 1) * 8, :]
                dst = wconv_d[f].rearrange("p (kt c) -> p kt c", c=512)[:, hf * 8:(hf + 1) * 8, t * 128:(t + 1) * 128]
                p.dma("pool", dst, src, (), ())
            kvc = [0]
            ptc = [0]
            LA = 2
            PF = 2
            for n in range(NG):
                qt = QT[n % 2]
                for par in range(2):
                    rsl = slice(par * 64, par * 64 + 64)
                    p.dma("sp", qt.ap[rsl, par::2, :], fqT_d[:, rsl, n * 512:(n + 1) * 512].rearrange("m p t -> p m t"), (), [qt])
                keys = [("o", j) for j in range(NT)] + [("s", j) for j in range(4 * n + 4)]
                nk = len(keys)
                for hb in range(2):
                    accs = [psb[k] for k in range(4)]
                    kv = {}

                    def load_kv(ki):
                        kind, j = keys[ki]
                        kt_ = KTt[kvc[0] % NKV]
                        vt_ = Vt[kvc[0] % NKV]
                        kvc[0] += 1
                        if kind == "o":
                            p.dma("sp", kt_[:], fkT_all[0:512, j * 128:(j + 1) * 128].rearrange("(m p) t -> p m t", p=128), [exB["fkT_all"]], [kt_])
                            p.dma("sp", vt_[:, :, :].rearrange("p a b -> p (a b)"), fva_all[j // 8][(j % 8) * 128:(j % 8 + 1) * 128, :], [exB["fva_all"]], [vt_])
                            kv[ki] = (kt_, vt_, j, 0)
                        else:
                            p.dma("sp", kt_[:], fkT_d[:, j * 128:(j + 1) * 128].rearrange("(m p) t -> p m t", p=128), [exB["fkT"]], [kt_])
                            p.dma("sp", vt_[:, :, :].rearrange("p a b -> p (a b)"), fva_d[j // 8][(j % 8) * 128:(j % 8 + 1) * 128, :], [exB["fva"]], [vt_])
                            kv[ki] = (kt_, vt_, 16 + j, max(0, j - 4 * n) * 128)

                    units = [(ki, hh) for ki in range(nk) for hh in range(4)]
                    ust = {}

                    def front(u):
                        ki, hh = units[u]
                        kind, j = keys[ki]
                        kt_, vt_, bj, q0 = kv[ki]
                        h = hb * 4 + hh
                        rows = slice((h % 2) * 64, (h % 2) * 64 + 64)
                        pr = h // 2
                        pss = psb[4 + (ptc[0] % 4)]
                        pt_ = PT[ptc[0] % 4]
                        ptc[0] += 1
                        mm(pss[:, q0:512], kt_[:, pr, :], qt[:, h, q0:512], True, True, [kt_, qt], [pss])
                        A("act", lambda e, pt_=pt_, pss=pss, q0=q0, n=n, bj=bj, h=h: e.activation(
                            out=pt_[:, q0:512], in_=pss[:, q0:512], func=AF.Exp, scale=0.125, bias=bias_all[:, n, bj, h:h + 1]),
                          [pss, bias_all], [pt_])
                        if kind == "s" and j >= 4 * n:
                            A("dve", lambda e, pt_=pt_, q0=q0: e.tensor_tensor(out=pt_[:, q0:q0 + 128], in0=pt_[:, q0:q0 + 128], in1=maskG[:], op=ALU.mult),
                              [pt_, maskG], [pt_])
                        ust[u] = pt_

                    def back(u):
                        ki, hh = units[u]
                        kt_, vt_, bj, q0 = kv[ki]
                        h = hb * 4 + hh
                        pt_ = ust.pop(u)
                        mm(accs[hh][:, q0:512], vt_[:, h, :], pt_[:, q0:512], ki == 0, ki == nk - 1, [vt_, pt_], [accs[hh]])

                    for _ in range(16):
                        if conv_list:
                            conv_dma(*conv_list.pop(0))
                    for ki in range(min(PF, nk)):
                        load_kv(ki)
                    for idx in range(len(units) + LA):
                        if idx < len(units):
                            ki, hh = units[idx]
                            if hh == 0 and ki + PF < nk:
                                load_kv(ki + PF)
                            front(idx)
                        if idx >= LA:
                            back(idx - LA)
                    for hh in range(4):
                        ac = acs[hh]
                        A("dve", lambda e, ac=ac, acc_=accs[hh]: e.tensor_copy(out=ac[:], in_=acc_[:, :]), [accs[hh]], [ac])
                    for hh in range(4):
                        h = hb * 4 + hh
                        orow = slice((h % 2) * 64, (h % 2) * 64 + 64)
                        srow = slice((1 - h % 2) * 64, (1 - h % 2) * 64 + 64)
                        rc_, tm_ = rc[hh % 2], tmpo[hh % 2]
                        ac = acs[hh]
                        A("dve", lambda e, rc_=rc_, ac=ac, orow=orow, srow=srow: e.reciprocal(out=rc_[orow, :], in_=ac[srow, :]), [ac], [rc_])
                        A("dve", lambda e, tm_=tm_, rc_=rc_, ac=ac, orow=orow: e.tensor_tensor(out=tm_[orow, :], in0=ac[orow, :], in1=rc_[orow, :], op=ALU.mult),
                          [ac, rc_], [tm_])
                        A("dve", lambda e, tm_=tm_, orow=orow, h=h, n=n: e.tensor_tensor(out=omix[orow, 8 + h // 2, n * 512:(n + 1) * 512], in0=tm_[orow, :],
                                                                                      in1=omix[orow, 8 + h // 2, n * 512:(n + 1) * 512], op=ALU.mult),
                          [tm_, omix], [omix])
            p.barrier()
            if stop_after == "B1b":
                break

            ar.reset()
            mqT = ar.alloc("mqT", [128, 4, TOK], BF16)
            wb2 = [ar.alloc(f"wb{i}", [128, KT, 512], BF16) for i in range(2)]
            load_w(win_l, O_MQ, 512, wb2[1])
            for c in range(4):
                def cons2(n, ps, c=c):
                    evac(mqT[:, c, n * 512:(n + 1) * 512], ps[:, :], [ps], [mqT])
                proj_F(wb2[1], c * 128, 128, hT, TOK, cons2)
            p.barrier()
            ar.off -= 2 * (KT * 512 // 2)
            PTm = [ar.alloc(f"PTm{i}", [128, 512], BF16) for i in range(4)]
            rc = [ar.alloc(f"rc{i}", [128, 512], F32) for i in range(2)]
            tmpo = [ar.alloc(f"tmpo{i}", [128, 512], F32) for i in range(2)]
            pc = [0]
            for n in range(NG):
                for h in range(4):
                    pts = []
                    for mt in range(2):
                        pss = nextps()
                        pt_ = PTm[pc[0] % 4]
                        pc[0] += 1
                        mm(pss[:, :], memKT[:, h, mt * 128:(mt + 1) * 128], mqT[:, h, n * 512:(n + 1) * 512], True, True, [memKT, mqT], [pss])
                        A("act", lambda e, pt_=pt_, pss=pss: e.activation(out=pt_[:], in_=pss[:, :], func=AF.Exp, scale=128.0 ** -0.5), [pss], [pt_])
                        pts.append(pt_)
                    pso_, psm = nextps(), nextps()
                    for mt in range(2):
                        mm(pso_[:, :], memV[:, mt, h * 128:(h + 1) * 128], pts[mt][:], mt == 0, mt == 1, [memV, pts[mt]], [pso_])
                    for mt in range(2):
                        mm(psm[:, :], onesb[:], pts[mt][:], mt == 0, mt == 1, [onesb, pts[mt]], [psm])
                    rc_, tm_ = rc[h % 2], tmpo[h % 2]
                    A("act", lambda e, rc_=rc_, psm=psm: e.activation(out=rc_[:], in_=psm[:, :], func=AF.Ln), [psm], [rc_])
                    A("act", lambda e, rc_=rc_: e.activation(out=rc_[:], in_=rc_[:], func=AF.Exp, scale=-1.0), [rc_], [rc_])
                    A("dve", lambda e, tm_=tm_, rc_=rc_, pso_=pso_: e.tensor_tensor(out=tm_[:], in0=pso_[:, :], in1=rc_[:], op=ALU.mult), [pso_, rc_], [tm_])
                    A("dve", lambda e, tm_=tm_, h=h, n=n: e.tensor_tensor(out=omix[:, 12 + h, n * 512:(n + 1) * 512], in0=tm_[:],
                                                                           in1=omix[:, 12 + h, n * 512:(n + 1) * 512], op=ALU.mult), [tm_, omix], [omix])
            p.barrier()
            if dbg_omix is not None and l == 0:
                for c in range(16):
                    p.dma("sp", dbg_omix[c], omix[:, c, :], [omix], ())
                p.barrier()
            if stop_after == "B1c":
                break

            ar.reset()
            wall_t = [ar.alloc(f"wall{i}", [128, KT, 512], BF16) for i in range(2)]
            G_t = [ar.alloc(f"G{i}", [128, 512], F32) for i in range(4)]
            y_t = [ar.alloc(f"y{i}", [128, 512], F32) for i in range(2)]
            t_t = [ar.alloc(f"t{i}", [128, 512], F32) for i in range(2)]
            yb_t = [ar.alloc(f"yb{i}", [128, 512], BF16) for i in range(2)]
            branches = [(0, 8), (8, 12), (12, 16)]

            def load_f(f):
                wa = wall_t[f % 2]
                srcv = wconv_d[f].rearrange("p (kt c) -> p kt c", c=512)
                for hf in range(2):
                    p.dma("sp", wa[:, hf * 8:(hf + 1) * 8, :], srcv[:, hf * 8:(hf + 1) * 8, :], (), [wa])
            load_f(0)
            gc = [0]
            for f in range(16):
                if f + 1 < 16:
                    load_f(f + 1)
                wa = wall_t[f % 2]
                for n in range(NG):
                    ns = slice(n * 512, (n + 1) * 512)
                    Gs = []
                    for br in range(3):
                        ps = nextps()
                        for kt in range(KT):
                            mm(ps[:, :], wa[:, kt, 128 + br * 128:128 + (br + 1) * 128], hT[:, kt, ns], kt == 0, kt == KT - 1, [wa, hT], [ps])
                        G = G_t[gc[0] % 4]
                        gc[0] += 1
                        A("act", lambda e, G=G, ps=ps: e.activation(out=G[:], in_=ps[:, :], func=AF.Sigmoid), [ps], [G])
                        Gs.append(G)
                    y, t, yb = y_t[n % 2], t_t[n % 2], yb_t[n % 2]
                    for br, (k0, k1) in enumerate(branches):
                        ps = nextps()
                        for kc in range(k0, k1):
                            mm(ps[:, :], wa[:, kc, 0:128], omix[:, kc, ns], kc == k0, kc == k1 - 1, [wa, omix], [ps])
                        if br == 0:
                            A("dve", lambda e, y=y, ps=ps, G=Gs[0]: e.tensor_tensor(out=y[:], in0=ps[:, :], in1=G[:], op=ALU.mult), [ps, Gs[0]], [y])
                        else:
                            A("dve", lambda e, t=t, ps=ps, G=Gs[br]: e.tensor_tensor(out=t[:], in0=ps[:, :], in1=G[:], op=ALU.mult), [ps, Gs[br]], [t])
                            if br == 1:
                                A("dve", lambda e, y=y, t=t: e.tensor_tensor(out=y[:], in0=y[:], in1=t[:], op=ALU.add), [y, t], [y])
                            else:
                                A("dve", lambda e, y=y, t=t, yb=yb: e.tensor_tensor(out=yb[:], in0=y[:], in1=t[:], op=ALU.add), [y, t], [yb])
                    p.dma("sp", yT_d[f, :, ns], yb[:], [yb], ())
            p.barrier()

            ar.reset()
            wb2 = [ar.alloc(f"wb{i}", [128, KT, 512], BF16) for i in range(2)]
            xt_t = [ar.alloc(f"xt{i}", [128, 512], F32) for i in range(3)]
            xo_t = [ar.alloc(f"xo{i}", [128, 512], F32) for i in range(3)]
            yT = omix
            for f in range(16):
                p.dma("sp", yT[:, f, :], yT_d[f], (), [yT])
            load_w(wout_l, 0, 512, wb2[0])
            xc = [0]
            for cg in range(4):
                if cg + 1 < 4:
                    load_w(wout_l, (cg + 1) * 512, 512, wb2[(cg + 1) % 2])
                wt = wb2[cg % 2]
                cs = slice(cg * 512, (cg + 1) * 512)
                for i in range(NT):
                    xt, xo = xt_t[xc[0] % 3], xo_t[xc[0] % 3]
                    xc[0] += 1
                    p.dma("sp", xt[:], x_src[i * 128:(i + 1) * 128, cs], (), [xt])
                    ps = nextps()
                    for f in range(16):
                        mm(ps[:, :], yT[:, f, i * 128:(i + 1) * 128], wt[:, f, :], f == 0, f == 15, [yT, wt], [ps])
                    A("dve", lambda e, xo=xo, ps=ps, xt=xt: e.tensor_tensor(out=xo[:], in0=ps[:, :], in1=xt[:], op=ALU.add), [ps, xt], [xo])
                    p.dma("sp", x_dst[i * 128:(i + 1) * 128, cs], xo[:], [xo], ())
            p.barrier()

        if stop_after is None:
            ar.reset()
            xinF = [ar.alloc(f"xinF{i}", [128, D], F32) for i in range(2)]
            xoF = [ar.alloc(f"xoF{i}", [128, D], F32) for i in range(2)]
            gbcF = ar.alloc("gbcF", [128, D], F32)
            junkF = ar.alloc("junkF", [128, D], BF16)
            stF = [ar.alloc(f"nst{i}", [128, 4], F32) for i in range(2)]
            xf = xres[(nlayers - 1) % 2]
            p.dma("sp", gbcF[:], fg_d[0:1, :].partition_broadcast(128), (), [gbcF])
            for i in range(NT):
                xt, o_, s = xinF[i % 2], xoF[i % 2], stF[i % 2]
                p.dma("sp", xt[:], xf[i * 128:(i + 1) * 128, :], (), [xt])
                A("act", lambda e, xt=xt, s=s: e.activation(out=junkF[:], in_=xt[:], func=AF.Square, accum_out=s[:, 0:1]), [xt], [junkF, s])
                A("act", lambda e, s=s: e.activation(out=s[:, 1:2], in_=s[:, 0:1], func=AF.Sqrt, scale=1.0 / D, bias=c_eps[:]), [s, c_eps], [s])
                A("dve", lambda e, s=s: e.reciprocal(out=s[:, 2:3], in_=s[:, 1:2]), [s], [s])
                A("dve", lambda e, xt=xt, o_=o_, s=s: e.scalar_tensor_tensor(out=o_[:], in0=xt[:], scalar=s[:, 2:3], in1=gbcF[:],
                                                                              op0=ALU.mult, op1=ALU.mult), [xt, s, gbcF], [o_])
                p.dma("sp", out_d[i * 128:(i + 1) * 128, :], o_[:], [o_], ())
        p.barrier()
        counts = p.emit()
        _NC_CACHE['prog'] = p
    return nc, counts


_NC_CACHE = {}


def make_in_maps(inputs):
    f32 = np.float32
    x = np.asarray(inputs["x"], dtype=f32)
    mem = np.asarray(inputs["mem"], dtype=f32)
    shared = {k: np.ascontiguousarray(np.asarray(inputs[k], dtype=f32)) for k in
              ("norm_gain", "w_in", "w_gk_up", "b_gk", "gla_norm_gain", "b_f", "mem_norm_gain", "w_mem_kv", "w_branch", "w_out")}
    shared["final_gain"] = np.ascontiguousarray(np.asarray(inputs["final_gain"], dtype=f32).reshape(1, D))
    in_maps = []
    for c in range(8):
        b, half = c // 2, c % 2
        flags = np.zeros((128, 2), f32)
        flags[:, 0] = 0.0 if half == 1 else -30000.0
        flags[:, 1] = 1.0 if half == 1 else 0.0
        m = dict(shared)
        m["x"] = np.ascontiguousarray(x[b, half * TOK:(half + 1) * TOK])
        m["mem"] = np.ascontiguousarray(mem[b])
        m["flags"] = flags
        in_maps.append(m)
    return in_maps


def kernel(**inputs):
    if "nc" not in _NC_CACHE:
        _NC_CACHE["nc"] = build()[0]
    nc = _NC_CACHE["nc"]
    in_maps = make_in_maps(inputs)
    res = run_bass_kernel_spmd(nc, in_maps, core_ids=list(range(8)))
    out = np.empty((4, 4096, D), np.float32)
    for c in range(8):
        b, half = c // 2, c % 2
        out[b, half * TOK:(half + 1) * TOK] = np.asarray(res.results[c]["out"], dtype=np.float32)
    return out
```

```python
import contextlib
import numpy as np
import concourse.bass as bass
import concourse.mybir as mybir
from concourse.bass_utils import run_bass_kernel_spmd

F32 = mybir.dt.float32
BF16 = mybir.dt.bfloat16
AF = mybir.ActivationFunctionType
ALU = mybir.AluOpType

D = 2048
TOK = 2048
NT = 16
KT = 16
NG = 4
DEPTH = 2
INC = 12312
MEM = 256
O_GQ, O_GK, O_GV, O_GG, O_GD, O_FQ, O_FK, O_FV, O_FL, O_FG, O_MQ, O_MG, O_MERGE = (
    0, 512, 1024, 2048, 3072, 3088, 3600, 4112, 4624, 4632, 5144, 5656, 6168)
EPS = 1e-6
SAME_ENGINE_SYNC = True


class Buf:
    __slots__ = ("name", "w", "r")

    def __init__(self, name):
        self.name = name
        self.w = None
        self.r = {}


class V:
    def __init__(self, ap, name, buf=None):
        self.ap = ap
        self.buf = buf or Buf(name)

    def __getitem__(self, k):
        return self.ap[k]


def _b(x):
    return getattr(x, "buf", x)


class Prog:
    ENGS = ("pe", "act", "dve", "pool", "sp")
    CENGS = ("pe", "act", "dve", "pool")

    def __init__(self, nc, es):
        self.nc = nc
        self.stream = {k: [] for k in self.ENGS}
        self.ecount = {k: 0 for k in self.ENGS}
        self.known = {k: {} for k in self.ENGS}
        self.sems = {}
        for k in self.CENGS:
            self.sems[("e", k)] = es.enter_context(nc.semaphore("es_" + k))
        self.dkeys = {}
        self.dcount = {}
        self.dnext = {}
        for q, n in {"sp": 12, "pool": 8}.items():
            self.dkeys[q] = []
            for i in range(n):
                key = ("d", q, i)
                self.sems[key] = es.enter_context(nc.semaphore(f"ds_{q}{i}"))
                self.dkeys[q].append(key)
                self.dcount[key] = 0
            self.dnext[q] = 0
        self.cckey = ("c", "cc")
        self.sems[self.cckey] = es.enter_context(nc.semaphore("cc_sem"))
        self.dcount[self.cckey] = 0
        self.signaled = {k: set() for k in self.CENGS}

    def _dep(self, eng, tok):
        if tok is None:
            return
        key, idx = tok
        if key[0] == "e" and key[1] == eng:
            if eng == "pe" or not SAME_ENGINE_SYNC:
                return
        if self.known[eng].get(key, 0) >= idx:
            return
        self.known[eng][key] = idx
        if key[0] == "e":
            self.signaled[key[1]].add(idx)
        self.stream[eng].append(("w", key, idx))

    def _deps(self, eng, reads, writes):
        for b in reads:
            self._dep(eng, _b(b).w)
        for b in writes:
            b = _b(b)
            self._dep(eng, b.w)
            for k, v in b.r.items():
                self._dep(eng, (k, v))

    def _mark(self, tok, reads, writes):
        for b in writes:
            b = _b(b)
            b.w = tok
            b.r = {}
        for b in reads:
            b = _b(b)
            if b.r.get(tok[0], 0) < tok[1]:
                b.r[tok[0]] = tok[1]

    def op(self, eng, fn, reads=(), writes=()):
        self._deps(eng, reads, writes)
        self.ecount[eng] += 1
        tok = (("e", eng), self.ecount[eng])
        self.stream[eng].append(("i", fn, tok))
        self._mark(tok, reads, writes)
        return tok

    def dma(self, q, out, in_, reads=(), writes=()):
        i = self.dnext[q]
        self.dnext[q] = (i + 1) % len(self.dkeys[q])
        key = self.dkeys[q][i]
        if self.dcount[key] > 0:
            self._dep(q, (key, self.dcount[key]))
        self._deps(q, reads, writes)
        self.dcount[key] += 16
        tok = (key, self.dcount[key])
        self.stream[q].append(("d", (out, in_), tok))
        self._mark(tok, reads, writes)
        return tok

    def cc(self, fn, reads=(), writes=()):
        q = "pool"
        key = self.cckey
        self._deps(q, reads, writes)
        self.dcount[key] += 1
        tok = (key, self.dcount[key])
        self.stream[q].append(("c", fn, tok))
        self._mark(tok, reads, writes)
        return tok

    def barrier(self):
        for e in self.ENGS:
            for k in self.CENGS:
                if k != e and self.ecount[k] > 0:
                    self._dep(e, (("e", k), self.ecount[k]))
            for key, cnt in self.dcount.items():
                if cnt > 0:
                    self._dep(e, (key, cnt))

    def emit(self):
        nc = self.nc
        rank = {}
        for k, s in self.signaled.items():
            rank[k] = {idx: r + 1 for r, idx in enumerate(sorted(s))}
        prog = self

        def run(engname, e):
            for ent in prog.stream[engname]:
                if ent[0] == "w":
                    _, key, idx = ent
                    val = rank[key[1]][idx] if key[0] == "e" else idx
                    e.wait_ge(prog.sems[key], val)
                elif ent[0] == "i":
                    _, fn, tok = ent
                    ins = fn(e)
                    if tok[1] in prog.signaled[engname]:
                        ins.then_inc(prog.sems[tok[0]], 1)
                elif ent[0] == "d":
                    _, (out, in_), tok = ent
                    e.dma_start(out=out, in_=in_).then_inc(prog.sems[tok[0]], 16)
                elif ent[0] == "c":
                    _, fn, tok = ent
                    fn(e).then_inc(prog.sems[tok[0]])

        with nc.Block() as block:
            @block.sync
            def _(e):
                run("sp", e)

            @block.tensor
            def _(e):
                run("pe", e)

            @block.scalar
            def _(e):
                run("act", e)

            @block.vector
            def _(e):
                run("dve", e)

            @block.gpsimd
            def _(e):
                run("pool", e)
        return {k: len(v) for k, v in self.stream.items()}


def build(dbg=None, nlayers=DEPTH, stop_after=None, ncores=8, nocc=False):
    nc = bass.Bass("TRN2", target_bir_lowering=False)
    es = contextlib.ExitStack()
    dbg = dbg or []
    with es:
        p = Prog(nc, es)

        def din(name, shape, dt=F32):
            return nc.dram_tensor(name, list(shape), dt, kind="ExternalInput").ap()

        def dscr(name, shape, dt, force_internal=False):
            kind = "ExternalOutput" if (name in dbg and not force_internal) else "Internal"
            return nc.dram_tensor(name, list(shape), dt, kind=kind).ap()

        x_d = din("x", [TOK, D])
        mem_d = din("mem", [MEM, D])
        ng_d = din("norm_gain", [DEPTH, D])
        win_d = din("w_in", [DEPTH, D, INC])
        wup_d = din("w_gk_up", [DEPTH, 16, 512])
        bgk_d = din("b_gk", [DEPTH, 512])
        gng_d = din("gla_norm_gain", [DEPTH, 256])
        bf_d = din("b_f", [DEPTH, 8])
        mng_d = din("mem_norm_gain", [DEPTH, D])
        wkv_d = din("w_mem_kv", [DEPTH, D, 1024])
        wbr_d = din("w_branch", [DEPTH, D, D])
        wout_d = din("w_out", [DEPTH, D, D])
        fg_d = din("final_gain", [1, D])
        flags_d = din("flags", [128, 2])
        out_d = nc.dram_tensor("out", [TOK, D], F32, kind="ExternalOutput").ap()

        xres = [dscr("xresA", [TOK, D], F32), dscr("xresB", [TOK, D], F32)]
        gqT_d = dscr("gqT", [4, 128, TOK], BF16)
        gkT_d = dscr("gkT", [4, 128, TOK], BF16)
        ktok_d = dscr("ktok", [TOK, 512], BF16)
        vtok_d = dscr("vtok", [TOK, 1024], BF16)
        gp_d = dscr("gp", [TOK, 512], F32)
        kdec_d = dscr("kdec", [TOK, 512], BF16)
        fqT_d = dscr("fqT", [4, 128, TOK], BF16)
        fkT_d = dscr("fkT", [512, TOK], BF16, True)
        fva_d = [dscr(f"fva{i}", [TOK // 2, 1024], BF16, True) for i in range(2)]
        cinU_d = dscr("cinU", [128, 1024], F32, True)
        cinS_d = dscr("cinS", [128, 128], F32, True)
        fkT_all = dscr("fkT_all", [1024, TOK], BF16, True)
        fva_all = [dscr(f"fva_all{i}", [TOK, 1024], BF16, True) for i in range(2)]
        coutU_d = dscr("coutU", [256, 1024], F32, True)
        coutS_d = dscr("coutS", [256, 128], F32, True)
        yT_d = dscr("yT", [16, 128, TOK], BF16)
        wconv_d = dscr("wconv", [16, 128, KT * 512], BF16)
        dbg_omix = dscr("dbg_omix", [16, 128, TOK], BF16) if "dbg_omix" in dbg else None
        dbg_hT = dscr("dbg_hT", [16, 128, TOK], BF16) if "dbg_hT" in dbg else None
        exB = {n: Buf(n) for n in ("fkT", "fva", "cinU", "cinS", "fkT_all", "fva_all", "coutU", "coutS")}

        def sb(name, shape, dt):
            h = es.enter_context(nc.sbuf_tensor(name, list(shape), dt))
            return V(h, name)

        hT = sb("hT", [128, KT, TOK], BF16)
        omix = sb("omix", [128, 16, TOK], BF16)
        ident = sb("ident", [128, 128], BF16)
        maskG = sb("maskG", [128, 128], BF16)
        maskGf = sb("maskGf", [128, 128], F32)
        Lrev = sb("Lrev", [128, 128], F32)
        Ucum = sb("Ucum", [128, 128], F32)
        SU = sb("SU", [128, 128], F32)
        onesf = sb("onesf", [128, 128], F32)
        onesb = sb("onesb", [128, 128], BF16)
        neg16 = sb("neg16", [128, 1], F32)
        c_eps = sb("c_eps", [128, 1], F32)
        c_one = sb("c_one", [128, 1], F32)
        flags = sb("flags_sb", [128, 2], F32)
        gdT = sb("gdT", [32, TOK], BF16)
        memKT = sb("memKT", [128, 4, MEM], BF16)
        memV = sb("memV", [128, 2, 512], BF16)
        lfp = sb("lfp", [128, NT, 8], F32)
        SUF = sb("SUF", [128, NT, 8], F32)
        SUFo = sb("SUFo", [128, NT, 8], F32)
        Tn = sb("Tn", [128, NG, 8], F32)
        PREn = sb("PREn", [128, NG, 8], F32)
        bias_all = sb("bias_all", [128, NG, 32, 8], F32)
        Sst = sb("Sst", [128, 4, 256], F32)
        Sbf = sb("Sbf", [128, 4, 256], BF16)
        wupb = sb("wupb", [32, 512], BF16)
        gng_bc = sb("gng_bc", [128, 256], F32)
        bf_bc = sb("bf_bc", [128, 8], F32)
        small = sb("small", [128, 64], F32)

        ARENA_COLS = 12800
        arena_h = es.enter_context(nc.sbuf_tensor("arena", [128, ARENA_COLS], F32))

        class Arena:
            def __init__(self):
                self.off = 0
                self.gen = 0

            def reset(self):
                self.off = 0
                self.gen += 1

            def alloc(self, name, shape, dt, parts=128):
                n = int(np.prod(shape[1:]))
                ncol = (n * (2 if dt == BF16 else 4) + 3) // 4
                ncol = (ncol + 7) // 8 * 8
                assert self.off + ncol <= ARENA_COLS, (name, self.off, ncol)
                ap = arena_h[0:shape[0], self.off:self.off + ncol]
                self.off += ncol
                if dt == BF16:
                    ap = ap.bitcast(BF16)
                ap = ap[:, 0:n]
                if len(shape) == 3:
                    ap = ap.rearrange("p (a b) -> p a b", a=shape[1])
                elif len(shape) == 4:
                    ap = ap.rearrange("p (a b c) -> p a b c", a=shape[1], b=shape[2])
                return V(ap, f"{name}_{self.gen}")

        ar = Arena()

        psb = []
        for i in range(8):
            h = es.enter_context(nc.psum_tensor(f"psb{i}", [128, 512], F32))
            psb.append(V(h, f"psb{i}"))
        ps_rr = [0]

        def nextps():
            i = ps_rr[0]
            ps_rr[0] = (i + 1) % 8
            return psb[i]

        def psbf(ps):
            return ps.ap[:, :].bitcast(BF16)

        def A(eng, f, r=(), w=()):
            return p.op(eng, f, reads=r, writes=w)

        def mm(out, lhsT, rhs, start, stop, r, w):
            A("pe", lambda e, out=out, lhsT=lhsT, rhs=rhs, start=start, stop=stop:
              e.matmul(out, lhsT=lhsT, rhs=rhs, start=start, stop=stop), r, w)

        evac_rr = [0]

        def evac(out, in_, r, w, func=None, scale=1.0, eng=None):
            if func is None and eng is None:
                evac_rr[0] ^= 1
                eng = "act" if evac_rr[0] else "dve"
            if func is not None or eng == "act":
                f = func if func is not None else AF.Copy
                A("act", lambda e, out=out, in_=in_, f=f, scale=scale: e.activation(out=out, in_=in_, func=f, scale=scale), r, w)
            else:
                if scale == 1.0:
                    A("dve", lambda e, out=out, in_=in_: e.tensor_copy(out=out, in_=in_), r, w)
                else:
                    A("dve", lambda e, out=out, in_=in_, scale=scale: e.tensor_scalar(out=out, in0=in_, scalar1=scale, scalar2=None, op0=ALU.mult), r, w)

        def fill_tri(t, val, kind):
            A("pool", lambda e: e.memset(t[:], val), (), [t])
            if kind == "gt":
                kw = dict(pattern=[[-1, 128]], compare_op=ALU.is_gt, base=0, channel_multiplier=1)
            elif kind == "le":
                kw = dict(pattern=[[1, 128]], compare_op=ALU.is_gt, base=1, channel_multiplier=-1)
            else:
                kw = dict(pattern=[[-1, 128]], compare_op=ALU.is_equal, base=0, channel_multiplier=1)
            A("pool", lambda e: e.affine_select(out=t[:], in_=t[:], fill=0.0, **kw), [t], [t])

        fill_tri(maskGf, 1.0, "le")
        fill_tri(Lrev, -1.0 / 16.0, "gt")
        fill_tri(Ucum, -1.0 / 16.0, "le")
        fill_tri(SU, 1.0, "gt")
        fill_tri(onesf, 1.0, "eq")
        A("dve", lambda e: e.tensor_copy(out=ident[:], in_=onesf[:]), [onesf], [ident])
        A("dve", lambda e: e.tensor_copy(out=maskG[:], in_=maskGf[:]), [maskGf], [maskG])
        A("pool", lambda e: e.memset(onesf[:], 1.0), (), [onesf])
        A("pool", lambda e: e.memset(onesb[:], 1.0), (), [onesb])
        A("pool", lambda e: e.memset(neg16[:], -1.0 / 16.0), (), [neg16])
        A("pool", lambda e: e.memset(c_eps[:], EPS), (), [c_eps])
        A("pool", lambda e: e.memset(c_one[:], 1.0), (), [c_one])
        A("pool", lambda e: e.memset(gdT[:], 1.0), (), [gdT])
        p.dma("sp", flags[:], flags_d[:, :], (), [flags])
        p.barrier()

        def norm_transpose(src_fn, ntiles, gain_ap, dst, dst_is_hT=True, extra=None):
            ar.reset()
            xin = [ar.alloc(f"xin{i}", [128, D], F32) for i in range(2)]
            xs = [ar.alloc(f"xs{i}", [128, D], BF16) for i in range(2)]
            gbc = ar.alloc("gbc", [128, D], F32)
            junk = ar.alloc("junk", [128, D], BF16)
            st = [ar.alloc(f"nst{i}", [128, 4], F32) for i in range(2)]
            p.dma("sp", gbc[:], gain_ap.partition_broadcast(128), (), [gbc])
            def stats(i):
                xt, xb, s = xin[i % 2], xs[i % 2], st[i % 2]
                p.dma("sp", xt[:], src_fn(i), (), [xt])
                A("act", lambda e, xt=xt, s=s: e.activation(out=junk[:], in_=xt[:], func=AF.Square, accum_out=s[:, 0:1]), [xt], [junk, s])
                A("act", lambda e, s=s: e.activation(out=s[:, 1:2], in_=s[:, 0:1], func=AF.Sqrt, scale=1.0 / D, bias=c_eps[:]), [s, c_eps], [s])
                A("dve", lambda e, s=s: e.reciprocal(out=s[:, 2:3], in_=s[:, 1:2]), [s], [s])
                A("dve", lambda e, xt=xt, xb=xb, s=s: e.scalar_tensor_tensor(out=xb[:], in0=xt[:], scalar=s[:, 2:3], in1=gbc[:],
                                                                              op0=ALU.mult, op1=ALU.mult), [xt, s, gbc], [xb])

            def trans(i):
                xb = xs[i % 2]
                for g in range(4):
                    ps = nextps()
                    pv = psbf(ps)
                    for j in range(4):
                        kt = g * 4 + j
                        A("pe", lambda e, pv=pv, j=j, xb=xb, kt=kt: e.transpose(pv[:, j * 128:(j + 1) * 128], xb[:, kt * 128:(kt + 1) * 128], ident[:]),
                          [xb, ident], [ps])
                    evac(dst[:, g * 4:(g + 1) * 4, i * 128:(i + 1) * 128], pv[:, 0:512].rearrange("p (a b) -> p a b", a=4), [ps], [dst])

            stats(0)
            for i in range(ntiles):
                if i + 1 < ntiles:
                    stats(i + 1)
                trans(i)
                if extra is not None:
                    extra(i)
            p.barrier()

        def load_w(wsrc, c0, ncols, wb_t, c_dst=0):
            src = wsrc[:, c0:c0 + ncols].rearrange("(kt p) c -> p kt c", p=128)
            for hf in range(2):
                p.dma("pool", wb_t[:, hf * 8:(hf + 1) * 8, c_dst:c_dst + ncols], src[:, hf * 8:(hf + 1) * 8, :], (), [wb_t])

        def proj_F(wb_t, m0, msz, act_T, ntok, consume):
            for n in range(ntok // 512 if ntok >= 512 else 1):
                nn = min(512, ntok)
                ps = nextps()
                for kt in range(KT):
                    mm(ps[0:msz, 0:nn], wb_t[:, kt, m0:m0 + msz], act_T[:, kt, n * 512:n * 512 + nn], kt == 0, kt == KT - 1, [wb_t, act_T], [ps])
                consume(n, ps)

        def proj_T(wb_t, c0, ncols, act_T, ntiles, consume):
            for i in range(ntiles):
                ps = nextps()
                for kt in range(KT):
                    mm(ps[:, 0:ncols], act_T[:, kt, i * 128:(i + 1) * 128], wb_t[:, kt, c0:c0 + ncols], kt == 0, kt == KT - 1, [wb_t, act_T], [ps])
                consume(i, ps)

        for l in range(nlayers):
            x_src = x_d if l == 0 else xres[(l - 1) % 2]
            x_dst = xres[l % 2]
            win_l, wbr_l, wout_l, wkv_l = win_d[l], wbr_d[l], wout_d[l], wkv_d[l]

            conv_list = [(f, t, hf) for f in range(16) for t in range(4) for hf in range(2)]

            def conv_some(k, win_l=win_l, wbr_l=wbr_l, conv_list=conv_list):
                for _ in range(k):
                    if not conv_list:
                        return
                    f, t, hf = conv_list.pop(0)
                    c0 = f * 128
                    srcw = wbr_l[:, c0:c0 + 128] if t == 0 else win_l[:, O_MERGE + (t - 1) * D + c0:O_MERGE + (t - 1) * D + c0 + 128]
                    src = srcw.rearrange("(kt p) c -> p kt c", p=128)[:, hf * 8:(hf + 1) * 8, :]
                    dst = wconv_d[f].rearrange("p (kt c) -> p kt c", c=512)[:, hf * 8:(hf + 1) * 8, t * 128:(t + 1) * 128]
                    p.dma("pool", dst, src, (), ())

            p.dma("pool", wupb[0:16, :], wup_d[l], (), [wupb])
            p.dma("pool", wupb[16:17, :], bgk_d[l:l + 1, :], (), [wupb])
            p.dma("sp", gng_bc[:], gng_d[l:l + 1, :].partition_broadcast(128), (), [gng_bc])
            p.dma("sp", bf_bc[:], bf_d[l:l + 1, :].partition_broadcast(128), (), [bf_bc])

            norm_transpose(lambda i: mem_d[i * 128:(i + 1) * 128, :], 2, mng_d[l:l + 1, :], hT)
            ar.reset()
            wb2 = [ar.alloc(f"wb{i}", [128, KT, 512], BF16) for i in range(2)]
            load_w(wkv_l, 0, 512, wb2[0])
            load_w(wkv_l, 512, 512, wb2[1])
            for h in range(4):
                def cons(n, ps, h=h):
                    evac(memKT[:, h, :], ps[:, 0:MEM], [ps], [memKT])
                proj_F(wb2[0], h * 128, 128, hT, MEM, cons)

            def cons(i, ps):
                evac(memV[:, i, :], ps[:, :], [ps], [memV])
            proj_T(wb2[1], 0, 512, hT, 2, cons)
            p.barrier()

            norm_transpose(lambda i: x_src[i * 128:(i + 1) * 128, :], NT, ng_d[l:l + 1, :], hT, extra=lambda i: conv_some(1))
            if dbg_hT is not None and l == 0:
                for kt in range(KT):
                    p.dma("sp", dbg_hT[kt], hT[:, kt, :], [hT], ())
                p.barrier()

            ar.reset()
            wb2 = [ar.alloc(f"wb{i}", [128, KT, 512], BF16) for i in range(2)]
            stg = [ar.alloc(f"stg{i}", [128, 4, 512], BF16) for i in range(3)]
            stv = [ar.alloc(f"stv{i}", [128, 8, 128], BF16) for i in range(2)]
            lft = ar.alloc("lft", [128, 16], F32)
            for t in stv:
                A("pool", lambda e, t=t: e.memset(t[:], 1.0), (), [t])
            groups = [("gq", O_GQ), ("gk", O_GK), ("gv0", O_GV), ("gv1", O_GV + 512), ("small", None),
                      ("fq", O_FQ), ("fk", O_FK), ("fv", O_FV)]

            def issue_load(gi):
                name, c0 = groups[gi]
                wt = wb2[gi % 2]
                if name == "small":
                    load_w(win_l, O_GD, 16, wt, 0)
                    load_w(win_l, O_FL, 8, wt, 16)
                else:
                    load_w(win_l, c0, 512, wt)
            issue_load(0)
            sg = [0]

            def F_to_dram(wt, dst3, scale=1.0):
                for n in range(NG):
                    s = stg[sg[0] % 3]
                    sg[0] += 1
                    for m in range(4):
                        ps = nextps()
                        for kt in range(KT):
                            mm(ps[:, :], wt[:, kt, m * 128:(m + 1) * 128], hT[:, kt, n * 512:(n + 1) * 512], kt == 0, kt == KT - 1, [wt, hT], [ps])
                        evac(s[:, m, :], ps[:, :], [ps], [s], scale=scale)
                    p.dma("sp", dst3[:, :, n * 512:(n + 1) * 512].rearrange("m p t -> p m t"), s[:], [s], ())

            def T_to_dram(wt, dst2, c_dst):
                for i4 in range(4):
                    s = stg[sg[0] % 3]
                    sg[0] += 1
                    for j in range(4):
                        i = i4 * 4 + j
                        ps = nextps()
                        for kt in range(KT):
                            mm(ps[:, :], hT[:, kt, i * 128:(i + 1) * 128], wt[:, kt, 0:512], kt == 0, kt == KT - 1, [wt, hT], [ps])
                        evac(s[:, j, :], ps[:, :], [ps], [s])
                    p.dma("sp", dst2[i4 * 512:(i4 + 1) * 512, c_dst:c_dst + 512].rearrange("(j p) c -> p j c", p=128), s[:], [s], ())

            for gi, (name, c0) in enumerate(groups):
                if gi + 1 < len(groups):
                    issue_load(gi + 1)
                wt = wb2[gi % 2]
                if name == "gq":
                    F_to_dram(wt, gqT_d, scale=128.0 ** -0.5)
                elif name == "gk":
                    F_to_dram(wt, gkT_d)
                    T_to_dram(wt, ktok_d, 0)
                elif name == "gv0":
                    T_to_dram(wt, vtok_d, 0)
                elif name == "gv1":
                    T_to_dram(wt, vtok_d, 512)
                elif name == "fq":
                    F_to_dram(wt, fqT_d)
                elif name == "fk":
                    F_to_dram(wt, fkT_d.rearrange("(m p) t -> m p t", p=128))
                elif name == "fv":
                    for i in range(NT):
                        s = stv[i % 2]
                        ps = nextps()
                        for kt in range(KT):
                            mm(ps[:, :], hT[:, kt, i * 128:(i + 1) * 128], wt[:, kt, 0:512], kt == 0, kt == KT - 1, [wt, hT], [ps])
                        pv = ps[:, :].rearrange("p (a b c) -> p a b c", a=4, b=2)
                        sv = s[:, :, :].rearrange("p (a b) c -> p a b c", b=2)
                        A("dve", lambda e, sv=sv, pv=pv: e.tensor_copy(out=sv[:, :, 0, 0:64], in_=pv[:, :, 0, :]), [ps], [s])
                        A("dve", lambda e, sv=sv, pv=pv: e.tensor_copy(out=sv[:, :, 1, 64:128], in_=pv[:, :, 1, :]), [ps], [s])
                        p.dma("sp", fva_d[i // 8][(i % 8) * 128:(i % 8 + 1) * 128, :], s[:, :, :].rearrange("p a b -> p (a b)"), [s], [exB["fva"]])
                elif name == "small":
                    def cons(n, ps):
                        evac(gdT[0:16, n * 512:(n + 1) * 512], ps[0:16, :], [ps], [gdT])
                    proj_F(wt, 0, 16, hT, TOK, cons)
                    for i in range(NT):
                        ps = nextps()
                        for kt in range(KT):
                            mm(ps[:, 0:8], hT[:, kt, i * 128:(i + 1) * 128], wt[:, kt, 16:24], kt == 0, kt == KT - 1, [wt, hT], [ps])
                        A("dve", lambda e, ps=ps: e.tensor_tensor(out=lft[:, 0:8], in0=ps[:, 0:8], in1=bf_bc[:], op=ALU.add), [ps, bf_bc], [lft])
                        A("act", lambda e: e.activation(out=lft[:, 8:16], in_=lft[:, 0:8], func=AF.Exp, scale=-1.0), [lft], [lft])
                        A("act", lambda e, i=i: e.activation(out=lfp[:, i, :], in_=lft[:, 8:16], func=AF.Ln, bias=c_one[:], scale=1.0), [lft, c_one], [lfp])
            p.barrier()
            if stop_after == "A2":
                break

            ar.reset()
            kt_t = [ar.alloc(f"ktok{i}", [128, 512], BF16) for i in range(2)]
            vt_t = [ar.alloc(f"vtok{i}", [128, 1024], BF16) for i in range(2)]
            t1 = [ar.alloc(f"t1{i}", [128, 512], F32) for i in range(2)]
            gp_t = [ar.alloc(f"gp{i}", [128, 512], F32) for i in range(2)]
            eR = [ar.alloc(f"eR{i}", [128, 512], F32) for i in range(2)]
            kd_t = [ar.alloc(f"kd{i}", [128, 512], BF16) for i in range(2)]
            dec = [ar.alloc(f"dec{i}", [128, 4], F32) for i in range(2)]
            A("pool", lambda e: e.memset(Sst[:], 0.0), (), [Sst])
            for i in range(NT):
                k_, v_, t_, g_, r_, d_, dc = kt_t[i % 2], vt_t[i % 2], t1[i % 2], gp_t[i % 2], eR[i % 2], kd_t[i % 2], dec[i % 2]
                p.dma("sp", k_[:], ktok_d[i * 128:(i + 1) * 128, :], (), [k_])
                p.dma("sp", v_[:], vtok_d[i * 128:(i + 1) * 128, :], (), [v_])
                ps = nextps()
                mm(ps[:, :], gdT[0:17, i * 128:(i + 1) * 128], wupb[0:17, :], True, True, [gdT, wupb], [ps])
                A("act", lambda e, t_=t_, ps=ps: e.activation(out=t_[:], in_=ps[:, :], func=AF.Exp, scale=-1.0), [ps], [t_])
                A("act", lambda e, t_=t_, g_=g_: e.activation(out=g_[:], in_=t_[:], func=AF.Ln, bias=c_one[:], scale=1.0), [t_, c_one], [g_])
                p.dma("sp", gp_d[i * 128:(i + 1) * 128, :], g_[:], [g_], ())
                ps2 = nextps()
                mm(ps2[:, :], Lrev[:], g_[:], True, True, [Lrev, g_], [ps2])
                A("act", lambda e, r_=r_, ps2=ps2: e.activation(out=r_[:], in_=ps2[:, :], func=AF.Exp), [ps2], [r_])
                A("dve", lambda e, d_=d_, k_=k_, r_=r_: e.tensor_tensor(out=d_[:], in0=k_[:], in1=r_[:], op=ALU.mult), [k_, r_], [d_])
                p.dma("sp", kdec_d[i * 128:(i + 1) * 128, :], d_[:], [d_], ())
                conv_some(1)
                ps3 = nextps()
                for h in range(4):
                    mm(ps3[:, h:h + 1], g_[:, h * 128:(h + 1) * 128], neg16[:], True, True, [g_, neg16], [ps3])
                A("act", lambda e, dc=dc, ps3=ps3: e.activation(out=dc[:], in_=ps3[:, 0:4], func=AF.Exp), [ps3], [dc])
                for hp in range(2):
                    ps4 = nextps()
                    for hh in range(2):
                        h = hp * 2 + hh
                        mm(ps4[:, hh * 256:(hh + 1) * 256], d_[:, h * 128:(h + 1) * 128], v_[:, h * 256:(h + 1) * 256], True, True, [d_, v_], [ps4])
                    for hh in range(2):
                        h = hp * 2 + hh
                        A("dve", lambda e, h=h, hh=hh, dc=dc, ps4=ps4: e.scalar_tensor_tensor(
                            out=Sst[:, h, :], in0=Sst[:, h, :], scalar=dc[:, h:h + 1], in1=ps4[:, hh * 256:(hh + 1) * 256],
                            op0=ALU.mult, op1=ALU.add), [Sst, dc, ps4], [Sst])
            p.dma("sp", cinU_d[:, :], Sst[:, :, :].rearrange("p a b -> p (a b)"), [Sst], [exB["cinU"]])

            rs_sb = ar.alloc("rs_sb", [128, NT, 8], F32)
            tot_sb = ar.alloc("tot_sb", [128, NT, 8], F32)
            acc = ar.alloc("acc", [128, 8], F32)
            psr, pst = nextps(), nextps()
            for j in range(NT):
                mm(psr[:, j * 8:(j + 1) * 8], SU[:], lfp[:, j, :], True, True, [SU, lfp], [psr])
                mm(pst[:, j * 8:(j + 1) * 8], onesf[:], lfp[:, j, :], True, True, [onesf, lfp], [pst])
            evac(rs_sb[:, :, :].rearrange("p a b -> p (a b)"), psr[:, 0:128], [psr], [rs_sb], eng="dve")
            evac(tot_sb[:, :, :].rearrange("p a b -> p (a b)"), pst[:, 0:128], [pst], [tot_sb], eng="dve")
            A("dve", lambda e: e.memset(acc[:], 0.0), (), [acc])
            A("dve", lambda e: e.memset(Tn[:, 3, :], 0.0), (), [Tn])
            for j in range(NT - 1, -1, -1):
                A("dve", lambda e, j=j: e.tensor_tensor(out=SUF[:, j, :], in0=rs_sb[:, j, :], in1=acc[:], op=ALU.add), [rs_sb, acc], [SUF])
                A("dve", lambda e, j=j: e.tensor_tensor(out=acc[:], in0=acc[:], in1=tot_sb[:, j, :], op=ALU.add), [acc, tot_sb], [acc])
                if j % 4 == 0 and j > 0:
                    n = j // 4 - 1
                    A("dve", lambda e, n=n: e.tensor_copy(out=Tn[:, n, :], in_=acc[:]), [acc], [Tn])
            for n in range(NG):
                A("dve", lambda e, n=n: e.tensor_tensor(out=PREn[:, n, :], in0=acc[:], in1=Tn[:, n, :], op=ALU.subtract), [acc, Tn], [PREn])
            p.dma("sp", cinS_d[:, :], SUF[:, :, :].rearrange("p a b -> p (a b)"), [SUF], [exB["cinS"]])
            p.barrier()

            rg = [[2 * i, 2 * i + 1] for i in range(ncores // 2)]
            for src, dst, sn, dn in ((fkT_d, fkT_all, "fkT", "fkT_all"), (fva_d[0], fva_all[0], "fva", "fva_all"), (fva_d[1], fva_all[1], "fva", "fva_all"),
                                     (cinU_d, coutU_d, "cinU", "coutU"), (cinS_d, coutS_d, "cinS", "coutS")):
                if nocc:
                    continue
                p.cc(lambda e, src=src, dst=dst: e.collective_compute("AllGather", ALU.bypass, replica_groups=rg,
                                                                      ins=[src[:, :]], outs=[dst[:, :]]), [exB[sn]], [exB[dn]])
            ar.reset()
            wb2 = [ar.alloc(f"wbg{i}", [128, KT, 512], BF16) for i in range(2)]
            ggroups = [(O_GG, 0), (O_GG + 512, 4), (O_FG, 8), (O_MG, 12)]
            load_w(win_l, ggroups[0][0], 512, wb2[0])
            for gi, (c0, ch0) in enumerate(ggroups):
                if gi + 1 < len(ggroups):
                    load_w(win_l, ggroups[gi + 1][0], 512, wb2[(gi + 1) % 2])
                for c in range(4):
                    def cons(n, ps, ch=ch0 + c):
                        evac(omix[:, ch, n * 512:(n + 1) * 512], ps[:, :], [ps], [omix], func=AF.Silu)
                    proj_F(wb2[gi % 2], c * 128, 128, hT, TOK, cons)
                    conv_some(1)
            p.barrier()
            if stop_after == "A3":
                break

            ar.reset()
            qT_t = [ar.alloc(f"qT{i}", [128, 4, 128], BF16) for i in range(2)]
            kT_t = [ar.alloc(f"kT{i}", [128, 4, 128], BF16) for i in range(2)]
            kd_t = [ar.alloc(f"kd{i}", [128, 512], BF16) for i in range(2)]
            vt_t = [ar.alloc(f"vtok{i}", [128, 1024], BF16) for i in range(2)]
            gp_t = [ar.alloc(f"gp{i}", [128, 512], F32) for i in range(2)]
            NS = 3
            E_t = [[ar.alloc(f"E{k}_{i}", [128, 128], F32) for i in range(4)] for k in range(NS)]
            Ei_t = [[ar.alloc(f"Ei{k}_{i}", [128, 128], F32) for i in range(4)] for k in range(NS)]
            qd_t = [[ar.alloc(f"qd{k}_{i}", [128, 128], BF16) for i in range(4)] for k in range(NS)]
            kdd_t = [[ar.alloc(f"kdd{k}_{i}", [128, 128], BF16) for i in range(4)] for k in range(NS)]
            at_t = [[ar.alloc(f"at{k}_{i}", [128, 128], BF16) for i in range(4)] for k in range(NS)]
            on_t = [ar.alloc(f"on{i}", [128, 1024], BF16) for i in range(2)]
            nst = [ar.alloc(f"gst{i}", [128, 16], F32) for i in range(2)]
            junkg = ar.alloc("junkg", [128, 256], BF16)
            Uin = ar.alloc("Uin", [128, 1024], F32)
            p.dma("sp", Uin[:], coutU_d[0:128, :], [exB["coutU"]], [Uin])
            A("dve", lambda e: e.tensor_scalar(out=Sst[:, :, :].rearrange("p a b -> p (a b)"), in0=Uin[:], scalar1=flags[:, 1:2], scalar2=None, op0=ALU.mult),
              [Uin, flags], [Sst])
            A("act", lambda e: e.activation(out=Sbf[:, :, :].rearrange("p a b -> p (a b)"), in_=Sst[:, :, :].rearrange("p a b -> p (a b)"), func=AF.Copy), [Sst], [Sbf])
            g_rr = [0]

            def gps():
                g_rr[0] = (g_rr[0] + 1) % 5
                return psb[3 + g_rr[0]]

            def gla_A1(i):
                q_, k_, g_ = qT_t[i % 2], kT_t[i % 2], gp_t[i % 2]
                tsl = slice(i * 128, (i + 1) * 128)
                p.dma("sp", q_[:], gqT_d[:, :, tsl].rearrange("m p t -> p m t"), (), [q_])
                p.dma("sp", k_[:], gkT_d[:, :, tsl].rearrange("m p t -> p m t"), (), [k_])
                p.dma("sp", g_[:], gp_d[tsl, :], (), [g_])
                pbs = []
                for h in range(4):
                    psb_ = gps()
                    mm(psb_[:, 0:128], g_[:, h * 128:(h + 1) * 128], Ucum[:], True, True, [g_, Ucum], [psb_])
                    E, Ei = E_t[i % NS][h], Ei_t[i % NS][h]
                    A("act", lambda e, E=E, psb_=psb_: e.activation(out=E[:], in_=psb_[:, 0:128], func=AF.Exp), [psb_], [E])
                    A("act", lambda e, Ei=Ei, psb_=psb_: e.activation(out=Ei[:], in_=psb_[:, 0:128], func=AF.Exp, scale=-1.0), [psb_], [Ei])
                for h in range(4):
                    E, Ei, qd, kdd = E_t[i % NS][h], Ei_t[i % NS][h], qd_t[i % NS][h], kdd_t[i % NS][h]
                    A("dve", lambda e, qd=qd, q_=q_, h=h, E=E: e.tensor_tensor(out=qd[:], in0=q_[:, h, :], in1=E[:], op=ALU.mult), [q_, E], [qd])
                    A("dve", lambda e, kdd=kdd, k_=k_, h=h, Ei=Ei: e.tensor_tensor(out=kdd[:], in0=k_[:, h, :], in1=Ei[:], op=ALU.mult), [k_, Ei], [kdd])

            def gla_A2(i):
                d_, v_ = kd_t[i % 2], vt_t[i % 2]
                tsl = slice(i * 128, (i + 1) * 128)
                p.dma("sp", d_[:], kdec_d[tsl, :], (), [d_])
                p.dma("sp", v_[:], vtok_d[tsl, :], (), [v_])
                for h in range(4):
                    qd, kdd, at = qd_t[i % NS][h], kdd_t[i % NS][h], at_t[i % NS][h]
                    psa = gps()
                    mm(psa[:, 0:128], kdd[:], qd[:], True, True, [kdd, qd], [psa])
                    A("dve", lambda e, at=at, psa=psa: e.tensor_tensor(out=at[:], in0=psa[:, 0:128], in1=maskGf[:], op=ALU.mult), [psa, maskGf], [at])

            def gla_B(i):
                d_, v_ = kd_t[i % 2], vt_t[i % 2]
                on, s = on_t[i % 2], nst[i % 2]
                tsl = slice(i * 128, (i + 1) * 128)
                pso = [psb[0], psb[1]]
                for h in range(4):
                    E, qd, at = E_t[i % NS][h], qd_t[i % NS][h], at_t[i % NS][h]
                    hs = slice(h * 128, (h + 1) * 128)
                    vs = slice(h * 256, (h + 1) * 256)
                    po = pso[h // 2]
                    pos = slice((h % 2) * 256, (h % 2 + 1) * 256)
                    mm(po[:, pos], qd[:], Sbf[:, h, :], True, False, [qd, Sbf], [po])
                    mm(po[:, pos], at[:], v_[:, vs], False, True, [at, v_], [po])
                    pss = gps()
                    mm(pss[:, 0:256], d_[:, hs], v_[:, vs], True, True, [d_, v_], [pss])
                    A("dve", lambda e, h=h, E=E, pss=pss: e.scalar_tensor_tensor(out=Sst[:, h, :], in0=Sst[:, h, :], scalar=E[:, 127:128], in1=pss[:, 0:256],
                                                                              op0=ALU.mult, op1=ALU.add), [Sst, E, pss], [Sst])
                    A("act", lambda e, h=h: e.activation(out=Sbf[:, h, :], in_=Sst[:, h, :], func=AF.Copy), [Sst], [Sbf])
                for h in range(4):
                    po = pso[h // 2]
                    pos = slice((h % 2) * 256, (h % 2 + 1) * 256)
                    A("act", lambda e, po=po, pos=pos, s=s, h=h: e.activation(out=junkg[:], in_=po[:, pos], func=AF.Square, accum_out=s[:, h:h + 1]), [po], [junkg, s])
                A("act", lambda e, s=s: e.activation(out=s[:, 4:8], in_=s[:, 0:4], func=AF.Sqrt, scale=1.0 / 256.0, bias=c_eps[:]), [s, c_eps], [s])
                A("dve", lambda e, s=s: e.reciprocal(out=s[:, 8:12], in_=s[:, 4:8]), [s], [s])
                for h in range(4):
                    po = pso[h // 2]
                    pos = slice((h % 2) * 256, (h % 2 + 1) * 256)
                    A("dve", lambda e, on=on, po=po, pos=pos, s=s, h=h: e.scalar_tensor_tensor(
                        out=on[:, h * 256:(h + 1) * 256], in0=po[:, pos], scalar=s[:, 8 + h:9 + h], in1=gng_bc[:], op0=ALU.mult, op1=ALU.mult),
                      [po, s, gng_bc], [on])
                pt = psb[2]
                ptv = psbf(pt)
                for c in range(8):
                    A("pe", lambda e, ptv=ptv, c=c, on=on: e.transpose(ptv[:, c * 128:(c + 1) * 128], on[:, c * 128:(c + 1) * 128], ident[:]), [on, ident], [pt])
                A("dve", lambda e, ptv=ptv, tsl=tsl: e.tensor_tensor(out=omix[:, 0:8, tsl], in0=ptv[:, :].rearrange("p (a b) -> p a b", a=8),
                                                                      in1=omix[:, 0:8, tsl], op=ALU.mult), [pt, omix], [omix])

            gla_A1(0)
            gla_A1(1)
            gla_A2(0)
            for i in range(NT):
                if i + 2 < NT:
                    gla_A1(i + 2)
                if i + 1 < NT:
                    gla_A2(i + 1)
                gla_B(i)
                conv_some(3)
            p.barrier()
            if stop_after == "B1a":
                break

            ar.reset()
            p.dma("sp", SUFo[:, :, :].rearrange("p a b -> p (a b)"), coutS_d[0:128, :], [exB["coutS"]], [SUFo])
            for n in range(NG):
                A("dve", lambda e, n=n: e.tensor_scalar(out=small[:, 0:8], in0=PREn[:, n, :], scalar1=-1.0, scalar2=flags[:, 0:1], op0=ALU.mult, op1=ALU.add),
                  [PREn, flags], [small])
                for j in range(NT):
                    A("dve", lambda e, n=n, j=j: e.tensor_tensor(out=bias_all[:, n, j, :], in0=small[:, 0:8], in1=SUFo[:, j, :], op=ALU.subtract),
                      [small, SUFo], [bias_all])
                    A("dve", lambda e, n=n, j=j: e.tensor_tensor(out=bias_all[:, n, 16 + j, :], in0=Tn[:, n, :], in1=SUF[:, j, :], op=ALU.subtract),
                      [Tn, SUF], [bias_all])
            QT = [ar.alloc(f"QT{i}", [128, 8, 512], BF16) for i in range(2)]
            for t in QT:
                A("pool", lambda e, t=t: e.memset(t[:], 0.0), (), [t])
            NKV = 5
            KTt = [ar.alloc(f"KTt{i}", [128, 4, 128], BF16) for i in range(NKV)]
            Vt = [ar.alloc(f"Vt{i}", [128, 8, 128], BF16) for i in range(NKV)]
            PT = [ar.alloc(f"PT{i}", [128, 512], BF16) for i in range(4)]
            rc = [ar.alloc(f"rc{i}", [128, 512], F32) for i in range(2)]
            tmpo = [ar.alloc(f"tmpo{i}", [128, 512], F32) for i in range(2)]
            acs = [ar.alloc(f"acs{i}", [128, 512], F32) for i in range(2)]

            kvc = [0]
            ptc = [0]
            LA = 2
            PF = 3
            for n in range(NG):
                qt = QT[n % 2]
                for par in range(2):
                    rsl = slice(par * 64, par * 64 + 64)
                    p.dma("sp", qt.ap[rsl, par::2, :], fqT_d[:, rsl, n * 512:(n + 1) * 512].rearrange("m p t -> p m t"), (), [qt])
                keys = [("o", j) for j in range(NT)] + [("s", j) for j in range(4 * n + 4)]
                nk = len(keys)
                for hb in range(2):
                    accs = [psb[k] for k in range(4)]
                    kv = {}

                    def load_kv(ki):
                        kind, j = keys[ki]
                        kt_ = KTt[kvc[0] % NKV]
                        vt_ = Vt[kvc[0] % NKV]
                        kvc[0] += 1
                        if kind == "o":
                            p.dma("sp", kt_[:], fkT_all[0:512, j * 128:(j + 1) * 128].rearrange("(m p) t -> p m t", p=128), [exB["fkT_all"]], [kt_])
                            p.dma("sp", vt_[:, :, :].rearrange("p a b -> p (a b)"), fva_all[j // 8][(j % 8) * 128:(j % 8 + 1) * 128, :], [exB["fva_all"]], [vt_])
                            kv[ki] = (kt_, vt_, j, 0)
                        else:
                            p.dma("sp", kt_[:], fkT_d[:, j * 128:(j + 1) * 128].rearrange("(m p) t -> p m t", p=128), [exB["fkT"]], [kt_])
                            p.dma("sp", vt_[:, :, :].rearrange("p a b -> p (a b)"), fva_d[j // 8][(j % 8) * 128:(j % 8 + 1) * 128, :], [exB["fva"]], [vt_])
                            kv[ki] = (kt_, vt_, 16 + j, max(0, j - 4 * n) * 128)

                    units = [(ki, hh) for ki in range(nk) for hh in range(4)]
                    ust = {}

                    def front(u):
                        ki, hh = units[u]
                        kind, j = keys[ki]
                        kt_, vt_, bj, q0 = kv[ki]
                        h = hb * 4 + hh
                        rows = slice((h % 2) * 64, (h % 2) * 64 + 64)
                        pr = h // 2
                        pss = psb[4 + (ptc[0] % 4)]
                        pt_ = PT[ptc[0] % 4]
                        ptc[0] += 1
                        mm(pss[:, q0:512], kt_[:, pr, :], qt[:, h, q0:512], True, True, [kt_, qt], [pss])
                        A("act", lambda e, pt_=pt_, pss=pss, q0=q0, n=n, bj=bj, h=h: e.activation(
                            out=pt_[:, q0:512], in_=pss[:, q0:512], func=AF.Exp, scale=0.125, bias=bias_all[:, n, bj, h:h + 1]),
                          [pss, bias_all], [pt_])
                        if kind == "s" and j >= 4 * n:
                            A("dve", lambda e, pt_=pt_, q0=q0: e.tensor_tensor(out=pt_[:, q0:q0 + 128], in0=pt_[:, q0:q0 + 128], in1=maskG[:], op=ALU.mult),
                              [pt_, maskG], [pt_])
                        ust[u] = pt_

                    def back(u):
                        ki, hh = units[u]
                        kt_, vt_, bj, q0 = kv[ki]
                        h = hb * 4 + hh
                        pt_ = ust.pop(u)
                        mm(accs[hh][:, q0:512], vt_[:, h, :], pt_[:, q0:512], ki == 0, ki == nk - 1, [vt_, pt_], [accs[hh]])

                    for ki in range(min(PF, nk)):
                        load_kv(ki)
                    for idx in range(len(units) + LA):
                        if idx < len(units):
                            ki, hh = units[idx]
                            if hh == 0 and ki + PF < nk:
                                load_kv(ki + PF)
                            front(idx)
                        if idx >= LA:
                            back(idx - LA)
                    for hh in range(4):
                        ac = acs[hh % 2]
                        A("dve", lambda e, ac=ac, acc_=accs[hh]: e.tensor_copy(out=ac[:], in_=acc_[:, :]), [accs[hh]], [ac])
                        h = hb * 4 + hh
                        orow = slice((h % 2) * 64, (h % 2) * 64 + 64)
                        srow = slice((1 - h % 2) * 64, (1 - h % 2) * 64 + 64)
                        rc_, tm_ = rc[hh % 2], tmpo[hh % 2]
                        A("dve", lambda e, rc_=rc_, ac=ac, orow=orow, srow=srow: e.reciprocal(out=rc_[orow, :], in_=ac[srow, :]), [ac], [rc_])
                        A("dve", lambda e, tm_=tm_, rc_=rc_, ac=ac, orow=orow: e.tensor_tensor(out=tm_[orow, :], in0=ac[orow, :], in1=rc_[orow, :], op=ALU.mult),
                          [ac, rc_], [tm_])
                        A("dve", lambda e, tm_=tm_, orow=orow, h=h, n=n: e.tensor_tensor(out=omix[orow, 8 + h // 2, n * 512:(n + 1) * 512], in0=tm_[orow, :],
                                                                                      in1=omix[orow, 8 + h // 2, n * 512:(n + 1) * 512], op=ALU.mult),
                          [tm_, omix], [omix])
            p.barrier()
            if stop_after == "B1b":
                break

            ar.reset()
            mqT = ar.alloc("mqT", [128, 4, TOK], BF16)
            wb2 = [ar.alloc(f"wb{i}", [128, KT, 512], BF16) for i in range(2)]
            load_w(win_l, O_MQ, 512, wb2[1])
            for c in range(4):
                def cons2(n, ps, c=c):
                    evac(mqT[:, c, n * 512:(n + 1) * 512], ps[:, :], [ps], [mqT])
                proj_F(wb2[1], c * 128, 128, hT, TOK, cons2)
            p.barrier()
            ar.off -= 2 * (KT * 512 // 2)
            PTm = [ar.alloc(f"PTm{i}", [128, 512], BF16) for i in range(4)]
            rc = [ar.alloc(f"rc{i}", [128, 512], F32) for i in range(2)]
            tmpo = [ar.alloc(f"tmpo{i}", [128, 512], F32) for i in range(2)]
            pc = [0]
            for n in range(NG):
                for h in range(4):
                    conv_some(2)
                    pts = []
                    for mt in range(2):
                        pss = nextps()
                        pt_ = PTm[pc[0] % 4]
                        pc[0] += 1
                        mm(pss[:, :], memKT[:, h, mt * 128:(mt + 1) * 128], mqT[:, h, n * 512:(n + 1) * 512], True, True, [memKT, mqT], [pss])
                        A("act", lambda e, pt_=pt_, pss=pss: e.activation(out=pt_[:], in_=pss[:, :], func=AF.Exp, scale=128.0 ** -0.5), [pss], [pt_])
                        pts.append(pt_)
                    pso_, psm = nextps(), nextps()
                    for mt in range(2):
                        mm(pso_[:, :], memV[:, mt, h * 128:(h + 1) * 128], pts[mt][:], mt == 0, mt == 1, [memV, pts[mt]], [pso_])
                    for mt in range(2):
                        mm(psm[:, :], onesb[:], pts[mt][:], mt == 0, mt == 1, [onesb, pts[mt]], [psm])
                    rc_, tm_ = rc[h % 2], tmpo[h % 2]
                    A("act", lambda e, rc_=rc_, psm=psm: e.activation(out=rc_[:], in_=psm[:, :], func=AF.Ln), [psm], [rc_])
                    A("act", lambda e, rc_=rc_: e.activation(out=rc_[:], in_=rc_[:], func=AF.Exp, scale=-1.0), [rc_], [rc_])
                    A("dve", lambda e, tm_=tm_, rc_=rc_, pso_=pso_: e.tensor_tensor(out=tm_[:], in0=pso_[:, :], in1=rc_[:], op=ALU.mult), [pso_, rc_], [tm_])
                    A("dve", lambda e, tm_=tm_, h=h, n=n: e.tensor_tensor(out=omix[:, 12 + h, n * 512:(n + 1) * 512], in0=tm_[:],
                                                                           in1=omix[:, 12 + h, n * 512:(n + 1) * 512], op=ALU.mult), [tm_, omix], [omix])
            p.barrier()
            if dbg_omix is not None and l == 0:
                for c in range(16):
                    p.dma("sp", dbg_omix[c], omix[:, c, :], [omix], ())
                p.barrier()
            if stop_after == "B1c":
                break

            conv_some(1000)
            p.barrier()
            ar.reset()
            wall_t = [ar.alloc(f"wall{i}", [128, KT, 512], BF16) for i in range(2)]
            G_t = [ar.alloc(f"G{i}", [128, 512], F32) for i in range(4)]
            y_t = [ar.alloc(f"y{i}", [128, 512], F32) for i in range(2)]
            t_t = [ar.alloc(f"t{i}", [128, 512], F32) for i in range(2)]
            yb_t = [ar.alloc(f"yb{i}", [128, 512], BF16) for i in range(2)]
            branches = [(0, 8), (8, 12), (12, 16)]

            def load_f(f):
                wa = wall_t[f % 2]
                srcv = wconv_d[f].rearrange("p (kt c) -> p kt c", c=512)
                for hf in range(2):
                    p.dma("sp", wa[:, hf * 8:(hf + 1) * 8, :], srcv[:, hf * 8:(hf + 1) * 8, :], (), [wa])
            load_f(0)
            gc = [0]
            for f in range(16):
                if f + 1 < 16:
                    load_f(f + 1)
                wa = wall_t[f % 2]
                for n in range(NG):
                    ns = slice(n * 512, (n + 1) * 512)
                    Gs = []
                    for br in range(3):
                        ps = nextps()
                        for kt in range(KT):
                            mm(ps[:, :], wa[:, kt, 128 + br * 128:128 + (br + 1) * 128], hT[:, kt, ns], kt == 0, kt == KT - 1, [wa, hT], [ps])
                        G = G_t[gc[0] % 4]
                        gc[0] += 1
                        A("act", lambda e, G=G, ps=ps: e.activation(out=G[:], in_=ps[:, :], func=AF.Sigmoid), [ps], [G])
                        Gs.append(G)
                    y, t, yb = y_t[n % 2], t_t[n % 2], yb_t[n % 2]
                    for br, (k0, k1) in enumerate(branches):
                        ps = nextps()
                        for kc in range(k0, k1):
                            mm(ps[:, :], wa[:, kc, 0:128], omix[:, kc, ns], kc == k0, kc == k1 - 1, [wa, omix], [ps])
                        if br == 0:
                            A("dve", lambda e, y=y, ps=ps, G=Gs[0]: e.tensor_tensor(out=y[:], in0=ps[:, :], in1=G[:], op=ALU.mult), [ps, Gs[0]], [y])
                        else:
                            A("dve", lambda e, t=t, ps=ps, G=Gs[br]: e.tensor_tensor(out=t[:], in0=ps[:, :], in1=G[:], op=ALU.mult), [ps, Gs[br]], [t])
                            if br == 1:
                                A("dve", lambda e, y=y, t=t: e.tensor_tensor(out=y[:], in0=y[:], in1=t[:], op=ALU.add), [y, t], [y])
                            else:
                                A("dve", lambda e, y=y, t=t, yb=yb: e.tensor_tensor(out=yb[:], in0=y[:], in1=t[:], op=ALU.add), [y, t], [yb])
                    p.dma("sp", yT_d[f, :, ns], yb[:], [yb], ())
            p.barrier()

            ar.reset()
            wb2 = [ar.alloc(f"wb{i}", [128, KT, 512], BF16) for i in range(2)]
            xt_t = [ar.alloc(f"xt{i}", [128, 512], F32) for i in range(3)]
            xo_t = [ar.alloc(f"xo{i}", [128, 512], F32) for i in range(3)]
            yT = omix
            for f in range(16):
                p.dma("sp", yT[:, f, :], yT_d[f], (), [yT])
            load_w(wout_l, 0, 512, wb2[0])
            xc = [0]
            for cg in range(4):
                if cg + 1 < 4:
                    load_w(wout_l, (cg + 1) * 512, 512, wb2[(cg + 1) % 2])
                wt = wb2[cg % 2]
                cs = slice(cg * 512, (cg + 1) * 512)
                for i in range(NT):
                    xt, xo = xt_t[xc[0] % 3], xo_t[xc[0] % 3]
                    xc[0] += 1
                    p.dma("sp", xt[:], x_src[i * 128:(i + 1) * 128, cs], (), [xt])
                    ps = nextps()
                    for f in range(16):
                        mm(ps[:, :], yT[:, f, i * 128:(i + 1) * 128], wt[:, f, :], f == 0, f == 15, [yT, wt], [ps])
                    A("dve", lambda e, xo=xo, ps=ps, xt=xt: e.tensor_tensor(out=xo[:], in0=ps[:, :], in1=xt[:], op=ALU.add), [ps, xt], [xo])
                    p.dma("sp", x_dst[i * 128:(i + 1) * 128, cs], xo[:], [xo], ())
            p.barrier()

        if stop_after is None:
            ar.reset()
            xinF = [ar.alloc(f"xinF{i}", [128, D], F32) for i in range(2)]
            xoF = [ar.alloc(f"xoF{i}", [128, D], F32) for i in range(2)]
            gbcF = ar.alloc("gbcF", [128, D], F32)
            junkF = ar.alloc("junkF", [128, D], BF16)
            stF = [ar.alloc(f"nst{i}", [128, 4], F32) for i in range(2)]
            xf = xres[(nlayers - 1) % 2]
            p.dma("sp", gbcF[:], fg_d[0:1, :].partition_broadcast(128), (), [gbcF])
            for i in range(NT):
                xt, o_, s = xinF[i % 2], xoF[i % 2], stF[i % 2]
                p.dma("sp", xt[:], xf[i * 128:(i + 1) * 128, :], (), [xt])
                A("act", lambda e, xt=xt, s=s: e.activation(out=junkF[:], in_=xt[:], func=AF.Square, accum_out=s[:, 0:1]), [xt], [junkF, s])
                A("act", lambda e, s=s: e.activation(out=s[:, 1:2], in_=s[:, 0:1], func=AF.Sqrt, scale=1.0 / D, bias=c_eps[:]), [s, c_eps], [s])
                A("dve", lambda e, s=s: e.reciprocal(out=s[:, 2:3], in_=s[:, 1:2]), [s], [s])
                A("dve", lambda e, xt=xt, o_=o_, s=s: e.scalar_tensor_tensor(out=o_[:], in0=xt[:], scalar=s[:, 2:3], in1=gbcF[:],
                                                                              op0=ALU.mult, op1=ALU.mult), [xt, s, gbcF], [o_])
                p.dma("sp", out_d[i * 128:(i + 1) * 128, :], o_[:], [o_], ())
        p.barrier()
        counts = p.emit()
        _NC_CACHE['prog'] = p
    return nc, counts


_NC_CACHE = {}


def make_in_maps(inputs):
    f32 = np.float32
    x = np.asarray(inputs["x"], dtype=f32)
    mem = np.asarray(inputs["mem"], dtype=f32)
    shared = {k: np.ascontiguousarray(np.asarray(inputs[k], dtype=f32)) for k in
              ("norm_gain", "w_in", "w_gk_up", "b_gk", "gla_norm_gain", "b_f", "mem_norm_gain", "w_mem_kv", "w_branch", "w_out")}
    shared["final_gain"] = np.ascontiguousarray(np.asarray(inputs["final_gain"], dtype=f32).reshape(1, D))
    in_maps = []
    for c in range(8):
        b, half = c // 2, c % 2
        flags = np.zeros((128, 2), f32)
        flags[:, 0] = 0.0 if half == 1 else -30000.0
        flags[:, 1] = 1.0 if half == 1 else 0.0
        m = dict(shared)
        m["x"] = np.ascontiguousarray(x[b, half * TOK:(half + 1) * TOK])
        m["mem"] = np.ascontiguousarray(mem[b])
        m["flags"] = flags
        in_maps.append(m)
    return in_maps


def kernel(**inputs):
    if "nc" not in _NC_CACHE:
        _NC_CACHE["nc"] = build()[0]
    nc = _NC_CACHE["nc"]
    in_maps = make_in_maps(inputs)
    res = run_bass_kernel_spmd(nc, in_maps, core_ids=list(range(8)))
    out = np.empty((4, 4096, D), np.float32)
    for c in range(8):
        b, half = c // 2, c % 2
        out[b, half * TOK:(half + 1) * TOK] = np.asarray(res.results[c]["out"], dtype=np.float32)
    return out
```

```python
import contextlib
import numpy as np
import concourse.bass as bass
import concourse.mybir as mybir
from concourse.bass_utils import run_bass_kernel_spmd

F32 = mybir.dt.float32
BF16 = mybir.dt.bfloat16
AF = mybir.ActivationFunctionType
ALU = mybir.AluOpType

D = 2048
TOK = 2048
NT = 16
KT = 16
NG = 4
DEPTH = 2
INC = 12312
MEM = 256
O_GQ, O_GK, O_GV, O_GG, O_GD, O_FQ, O_FK, O_FV, O_FL, O_FG, O_MQ, O_MG, O_MERGE = (
    0, 512, 1024, 2048, 3072, 3088, 3600, 4112, 4624, 4632, 5144, 5656, 6168)
EPS = 1e-6
SAME_ENGINE_SYNC = True


class Buf:
    __slots__ = ("name", "w", "r")

    def __init__(self, name):
        self.name = name
        self.w = None
        self.r = {}


class V:
    def __init__(self, ap, name, buf=None):
        self.ap = ap
        self.buf = buf or Buf(name)

    def __getitem__(self, k):
        return self.ap[k]


def _b(x):
    return getattr(x, "buf", x)


class Prog:
    ENGS = ("pe", "act", "dve", "pool", "sp")
    CENGS = ("pe", "act", "dve", "pool")

    def __init__(self, nc, es):
        self.nc = nc
        self.stream = {k: [] for k in self.ENGS}
        self.ecount = {k: 0 for k in self.ENGS}
        self.known = {k: {} for k in self.ENGS}
        self.sems = {}
        for k in self.CENGS:
            self.sems[("e", k)] = es.enter_context(nc.semaphore("es_" + k))
        self.dkeys = {}
        self.dcount = {}
        self.dnext = {}
        for q, n in {"sp": 12, "pool": 8}.items():
            self.dkeys[q] = []
            for i in range(n):
                key = ("d", q, i)
                self.sems[key] = es.enter_context(nc.semaphore(f"ds_{q}{i}"))
                self.dkeys[q].append(key)
                self.dcount[key] = 0
            self.dnext[q] = 0
        self.cckey = ("c", "cc")
        self.sems[self.cckey] = es.enter_context(nc.semaphore("cc_sem"))
        self.dcount[self.cckey] = 0
        self.signaled = {k: set() for k in self.CENGS}

    def _dep(self, eng, tok):
        if tok is None:
            return
        key, idx = tok
        if key[0] == "e" and key[1] == eng:
            if eng == "pe" or not SAME_ENGINE_SYNC:
                return
        if self.known[eng].get(key, 0) >= idx:
            return
        self.known[eng][key] = idx
        if key[0] == "e":
            self.signaled[key[1]].add(idx)
        self.stream[eng].append(("w", key, idx))

    def _deps(self, eng, reads, writes):
        for b in reads:
            self._dep(eng, _b(b).w)
        for b in writes:
            b = _b(b)
            self._dep(eng, b.w)
            for k, v in b.r.items():
                self._dep(eng, (k, v))

    def _mark(self, tok, reads, writes):
        for b in writes:
            b = _b(b)
            b.w = tok
            b.r = {}
        for b in reads:
            b = _b(b)
            if b.r.get(tok[0], 0) < tok[1]:
                b.r[tok[0]] = tok[1]

    def op(self, eng, fn, reads=(), writes=()):
        self._deps(eng, reads, writes)
        self.ecount[eng] += 1
        tok = (("e", eng), self.ecount[eng])
        self.stream[eng].append(("i", fn, tok))
        self._mark(tok, reads, writes)
        return tok

    def dma(self, q, out, in_, reads=(), writes=()):
        i = self.dnext[q]
        self.dnext[q] = (i + 1) % len(self.dkeys[q])
        key = self.dkeys[q][i]
        if self.dcount[key] > 0:
            self._dep(q, (key, self.dcount[key]))
        self._deps(q, reads, writes)
        self.dcount[key] += 16
        tok = (key, self.dcount[key])
        self.stream[q].append(("d", (out, in_), tok))
        self._mark(tok, reads, writes)
        return tok

    def cc(self, fn, reads=(), writes=()):
        q = "pool"
        key = self.cckey
        self._deps(q, reads, writes)
        self.dcount[key] += 1
        tok = (key, self.dcount[key])
        self.stream[q].append(("c", fn, tok))
        self._mark(tok, reads, writes)
        return tok

    def barrier(self):
        for e in self.ENGS:
            for k in self.CENGS:
                if k != e and self.ecount[k] > 0:
                    self._dep(e, (("e", k), self.ecount[k]))
            for key, cnt in self.dcount.items():
                if cnt > 0:
                    self._dep(e, (key, cnt))

    def emit(self):
        nc = self.nc
        rank = {}
        for k, s in self.signaled.items():
            rank[k] = {idx: r + 1 for r, idx in enumerate(sorted(s))}
        prog = self

        def run(engname, e):
            for ent in prog.stream[engname]:
                if ent[0] == "w":
                    _, key, idx = ent
                    val = rank[key[1]][idx] if key[0] == "e" else idx
                    e.wait_ge(prog.sems[key], val)
                elif ent[0] == "i":
                    _, fn, tok = ent
                    ins = fn(e)
                    if tok[1] in prog.signaled[engname]:
                        ins.then_inc(prog.sems[tok[0]], 1)
                elif ent[0] == "d":
                    _, (out, in_), tok = ent
                    e.dma_start(out=out, in_=in_).then_inc(prog.sems[tok[0]], 16)
                elif ent[0] == "c":
                    _, fn, tok = ent
                    fn(e).then_inc(prog.sems[tok[0]])

        with nc.Block() as block:
            @block.sync
            def _(e):
                run("sp", e)

            @block.tensor
            def _(e):
                run("pe", e)

            @block.scalar
            def _(e):
                run("act", e)

            @block.vector
            def _(e):
                run("dve", e)

            @block.gpsimd
            def _(e):
                run("pool", e)
        return {k: len(v) for k, v in self.stream.items()}


def build(dbg=None, nlayers=DEPTH, stop_after=None, ncores=8, nocc=False):
    nc = bass.Bass("TRN2", target_bir_lowering=False)
    es = contextlib.ExitStack()
    dbg = dbg or []
    with es:
        p = Prog(nc, es)

        def din(name, shape, dt=F32):
            return nc.dram_tensor(name, list(shape), dt, kind="ExternalInput").ap()

        def dscr(name, shape, dt, force_internal=False):
            kind = "ExternalOutput" if (name in dbg and not force_internal) else "Internal"
            return nc.dram_tensor(name, list(shape), dt, kind=kind).ap()

        x_d = din("x", [TOK, D])
        mem_d = din("mem", [MEM, D])
        ng_d = din("norm_gain", [DEPTH, D])
        win_d = din("w_in", [DEPTH, D, INC])
        wup_d = din("w_gk_up", [DEPTH, 16, 512])
        bgk_d = din("b_gk", [DEPTH, 512])
        gng_d = din("gla_norm_gain", [DEPTH, 256])
        bf_d = din("b_f", [DEPTH, 8])
        mng_d = din("mem_norm_gain", [DEPTH, D])
        wkv_d = din("w_mem_kv", [DEPTH, D, 1024])
        wbr_d = din("w_branch", [DEPTH, D, D])
        wout_d = din("w_out", [DEPTH, D, D])
        fg_d = din("final_gain", [1, D])
        flags_d = din("flags", [128, 2])
        out_d = nc.dram_tensor("out", [TOK, D], F32, kind="ExternalOutput").ap()

        xres = [dscr("xresA", [TOK, D], F32), dscr("xresB", [TOK, D], F32)]
        gqT_d = dscr("gqT", [4, 128, TOK], BF16)
        gkT_d = dscr("gkT", [4, 128, TOK], BF16)
        ktok_d = dscr("ktok", [TOK, 512], BF16)
        vtok_d = dscr("vtok", [TOK, 1024], BF16)
        gp_d = dscr("gp", [TOK, 512], F32)
        kdec_d = dscr("kdec", [TOK, 512], BF16)
        fqT_d = dscr("fqT", [4, 128, TOK], BF16)
        fkT_d = dscr("fkT", [512, TOK], BF16, True)
        fva_d = [dscr(f"fva{i}", [TOK // 2, 1024], BF16, True) for i in range(2)]
        cinU_d = dscr("cinU", [128, 1024], F32, True)
        cinS_d = dscr("cinS", [128, 128], F32, True)
        fkT_all = dscr("fkT_all", [1024, TOK], BF16, True)
        fva_all = [dscr(f"fva_all{i}", [TOK, 1024], BF16, True) for i in range(2)]
        coutU_d = dscr("coutU", [256, 1024], F32, True)
        coutS_d = dscr("coutS", [256, 128], F32, True)
        yT_d = dscr("yT", [16, 128, TOK], BF16)
        wconv_d = dscr("wconv", [16, 128, KT * 512], BF16)
        dbg_omix = dscr("dbg_omix", [16, 128, TOK], BF16) if "dbg_omix" in dbg else None
        dbg_hT = dscr("dbg_hT", [16, 128, TOK], BF16) if "dbg_hT" in dbg else None
        exB = {n: Buf(n) for n in ("fkT", "fva", "cinU", "cinS", "fkT_all", "fva_all", "coutU", "coutS")}

        def sb(name, shape, dt):
            h = es.enter_context(nc.sbuf_tensor(name, list(shape), dt))
            return V(h, name)

        hT = sb("hT", [128, KT, TOK], BF16)
        omix = sb("omix", [128, 16, TOK], BF16)
        ident = sb("ident", [128, 128], BF16)
        maskG = sb("maskG", [128, 128], BF16)
        maskGf = sb("maskGf", [128, 128], F32)
        Lrev = sb("Lrev", [128, 128], F32)
        Ucum = sb("Ucum", [128, 128], F32)
        SU = sb("SU", [128, 128], F32)
        onesf = sb("onesf", [128, 128], F32)
        onesb = sb("onesb", [128, 128], BF16)
        neg16 = sb("neg16", [128, 1], F32)
        c_eps = sb("c_eps", [128, 1], F32)
        c_one = sb("c_one", [128, 1], F32)
        flags = sb("flags_sb", [128, 2], F32)
        gdT = sb("gdT", [32, TOK], BF16)
        memKT = sb("memKT", [128, 4, MEM], BF16)
        memV = sb("memV", [128, 2, 512], BF16)
        lfp = sb("lfp", [128, NT, 8], F32)
        SUF = sb("SUF", [128, NT, 8], F32)
        SUFo = sb("SUFo", [128, NT, 8], F32)
        Tn = sb("Tn", [128, NG, 8], F32)
        PREn = sb("PREn", [128, NG, 8], F32)
        bias_all = sb("bias_all", [128, NG, 32, 8], F32)
        Sst = sb("Sst", [128, 4, 256], F32)
        Sbf = sb("Sbf", [128, 4, 256], BF16)
        wupb = sb("wupb", [32, 512], BF16)
        gng_bc = sb("gng_bc", [128, 256], F32)
        bf_bc = sb("bf_bc", [128, 8], F32)
        small = sb("small", [128, 64], F32)

        ARENA_COLS = 12800
        arena_h = es.enter_context(nc.sbuf_tensor("arena", [128, ARENA_COLS], F32))

        class Arena:
            def __init__(self):
                self.off = 0
                self.gen = 0

            def reset(self):
                self.off = 0
                self.gen += 1

            def alloc(self, name, shape, dt, parts=128):
                n = int(np.prod(shape[1:]))
                ncol = (n * (2 if dt == BF16 else 4) + 3) // 4
                ncol = (ncol + 7) // 8 * 8
                assert self.off + ncol <= ARENA_COLS, (name, self.off, ncol)
                ap = arena_h[0:shape[0], self.off:self.off + ncol]
                self.off += ncol
                if dt == BF16:
                    ap = ap.bitcast(BF16)
                ap = ap[:, 0:n]
                if len(shape) == 3:
                    ap = ap.rearrange("p (a b) -> p a b", a=shape[1])
                elif len(shape) == 4:
                    ap = ap.rearrange("p (a b c) -> p a b c", a=shape[1], b=shape[2])
                return V(ap, f"{name}_{self.gen}")

        ar = Arena()

        psb = []
        for i in range(8):
            h = es.enter_context(nc.psum_tensor(f"psb{i}", [128, 512], F32))
            psb.append(V(h, f"psb{i}"))
        ps_rr = [0]

        def nextps():
            i = ps_rr[0]
            ps_rr[0] = (i + 1) % 8
            return psb[i]

        def psbf(ps):
            return ps.ap[:, :].bitcast(BF16)

        def A(eng, f, r=(), w=()):
            return p.op(eng, f, reads=r, writes=w)

        def mm(out, lhsT, rhs, start, stop, r, w):
            A("pe", lambda e, out=out, lhsT=lhsT, rhs=rhs, start=start, stop=stop:
              e.matmul(out, lhsT=lhsT, rhs=rhs, start=start, stop=stop), r, w)

        evac_rr = [0]

        def evac(out, in_, r, w, func=None, scale=1.0, eng=None):
            if func is None and eng is None:
                evac_rr[0] ^= 1
                eng = "act" if evac_rr[0] else "dve"
            if func is not None or eng == "act":
                f = func if func is not None else AF.Copy
                A("act", lambda e, out=out, in_=in_, f=f, scale=scale: e.activation(out=out, in_=in_, func=f, scale=scale), r, w)
            else:
                if scale == 1.0:
                    A("dve", lambda e, out=out, in_=in_: e.tensor_copy(out=out, in_=in_), r, w)
                else:
                    A("dve", lambda e, out=out, in_=in_, scale=scale: e.tensor_scalar(out=out, in0=in_, scalar1=scale, scalar2=None, op0=ALU.mult), r, w)

        def fill_tri(t, val, kind):
            A("pool", lambda e: e.memset(t[:], val), (), [t])
            if kind == "gt":
                kw = dict(pattern=[[-1, 128]], compare_op=ALU.is_gt, base=0, channel_multiplier=1)
            elif kind == "le":
                kw = dict(pattern=[[1, 128]], compare_op=ALU.is_gt, base=1, channel_multiplier=-1)
            else:
                kw = dict(pattern=[[-1, 128]], compare_op=ALU.is_equal, base=0, channel_multiplier=1)
            A("pool", lambda e: e.affine_select(out=t[:], in_=t[:], fill=0.0, **kw), [t], [t])

        fill_tri(maskGf, 1.0, "le")
        fill_tri(Lrev, -1.0 / 16.0, "gt")
        fill_tri(Ucum, -1.0 / 16.0, "le")
        fill_tri(SU, 1.0, "gt")
        fill_tri(onesf, 1.0, "eq")
        A("dve", lambda e: e.tensor_copy(out=ident[:], in_=onesf[:]), [onesf], [ident])
        A("dve", lambda e: e.tensor_copy(out=maskG[:], in_=maskGf[:]), [maskGf], [maskG])
        A("pool", lambda e: e.memset(onesf[:], 1.0), (), [onesf])
        A("pool", lambda e: e.memset(onesb[:], 1.0), (), [onesb])
        A("pool", lambda e: e.memset(neg16[:], -1.0 / 16.0), (), [neg16])
        A("pool", lambda e: e.memset(c_eps[:], EPS), (), [c_eps])
        A("pool", lambda e: e.memset(c_one[:], 1.0), (), [c_one])
        A("pool", lambda e: e.memset(gdT[:], 1.0), (), [gdT])
        p.dma("sp", flags[:], flags_d[:, :], (), [flags])
        p.barrier()

        def norm_transpose(src_fn, ntiles, gain_ap, dst, dst_is_hT=True, extra=None):
            ar.reset()
            xin = [ar.alloc(f"xin{i}", [128, D], F32) for i in range(2)]
            xs = [ar.alloc(f"xs{i}", [128, D], BF16) for i in range(2)]
            gbc = ar.alloc("gbc", [128, D], F32)
            junk = ar.alloc("junk", [128, D], BF16)
            st = [ar.alloc(f"nst{i}", [128, 4], F32) for i in range(2)]
            p.dma("sp", gbc[:], gain_ap.partition_broadcast(128), (), [gbc])
            def stats(i):
                xt, xb, s = xin[i % 2], xs[i % 2], st[i % 2]
                p.dma("sp", xt[:], src_fn(i), (), [xt])
                A("act", lambda e, xt=xt, s=s: e.activation(out=junk[:], in_=xt[:], func=AF.Square, accum_out=s[:, 0:1]), [xt], [junk, s])
                A("act", lambda e, s=s: e.activation(out=s[:, 1:2], in_=s[:, 0:1], func=AF.Sqrt, scale=1.0 / D, bias=c_eps[:]), [s, c_eps], [s])
                A("dve", lambda e, s=s: e.reciprocal(out=s[:, 2:3], in_=s[:, 1:2]), [s], [s])
                A("dve", lambda e, xt=xt, xb=xb, s=s: e.scalar_tensor_tensor(out=xb[:], in0=xt[:], scalar=s[:, 2:3], in1=gbc[:],
                                                                              op0=ALU.mult, op1=ALU.mult), [xt, s, gbc], [xb])

            def trans(i):
                xb = xs[i % 2]
                for g in range(4):
                    ps = nextps()
                    pv = psbf(ps)
                    for j in range(4):
                        kt = g * 4 + j
                        A("pe", lambda e, pv=pv, j=j, xb=xb, kt=kt: e.transpose(pv[:, j * 128:(j + 1) * 128], xb[:, kt * 128:(kt + 1) * 128], ident[:]),
                          [xb, ident], [ps])
                    evac(dst[:, g * 4:(g + 1) * 4, i * 128:(i + 1) * 128], pv[:, 0:512].rearrange("p (a b) -> p a b", a=4), [ps], [dst])

            stats(0)
            for i in range(ntiles):
                if i + 1 < ntiles:
                    stats(i + 1)
                trans(i)
                if extra is not None:
                    extra(i)
            p.barrier()

        def load_w(wsrc, c0, ncols, wb_t, c_dst=0):
            src = wsrc[:, c0:c0 + ncols].rearrange("(kt p) c -> p kt c", p=128)
            for hf in range(2):
                p.dma("pool", wb_t[:, hf * 8:(hf + 1) * 8, c_dst:c_dst + ncols], src[:, hf * 8:(hf + 1) * 8, :], (), [wb_t])

        def proj_F(wb_t, m0, msz, act_T, ntok, consume):
            for n in range(ntok // 512 if ntok >= 512 else 1):
                nn = min(512, ntok)
                ps = nextps()
                for kt in range(KT):
                    mm(ps[0:msz, 0:nn], wb_t[:, kt, m0:m0 + msz], act_T[:, kt, n * 512:n * 512 + nn], kt == 0, kt == KT - 1, [wb_t, act_T], [ps])
                consume(n, ps)

        def proj_T(wb_t, c0, ncols, act_T, ntiles, consume):
            for i in range(ntiles):
                ps = nextps()
                for kt in range(KT):
                    mm(ps[:, 0:ncols], act_T[:, kt, i * 128:(i + 1) * 128], wb_t[:, kt, c0:c0 + ncols], kt == 0, kt == KT - 1, [wb_t, act_T], [ps])
                consume(i, ps)

        for l in range(nlayers):
            x_src = x_d if l == 0 else xres[(l - 1) % 2]
            x_dst = xres[l % 2]
            win_l, wbr_l, wout_l, wkv_l = win_d[l], wbr_d[l], wout_d[l], wkv_d[l]

            conv_list = [(f, t, hf) for f in range(16) for t in range(4) for hf in range(2)]

            def conv_some(k, win_l=win_l, wbr_l=wbr_l, conv_list=conv_list):
                return
                for _ in range(k):
                    if not conv_list:
                        return
                    f, t, hf = conv_list.pop(0)
                    c0 = f * 128
                    srcw = wbr_l[:, c0:c0 + 128] if t == 0 else win_l[:, O_MERGE + (t - 1) * D + c0:O_MERGE + (t - 1) * D + c0 + 128]
                    src = srcw.rearrange("(kt p) c -> p kt c", p=128)[:, hf * 8:(hf + 1) * 8, :]
                    dst = wconv_d[f].rearrange("p (kt c) -> p kt c", c=512)[:, hf * 8:(hf + 1) * 8, t * 128:(t + 1) * 128]
                    p.dma("pool", dst, src, (), ())

            p.dma("pool", wupb[0:16, :], wup_d[l], (), [wupb])
            p.dma("pool", wupb[16:17, :], bgk_d[l:l + 1, :], (), [wupb])
            p.dma("sp", gng_bc[:], gng_d[l:l + 1, :].partition_broadcast(128), (), [gng_bc])
            p.dma("sp", bf_bc[:], bf_d[l:l + 1, :].partition_broadcast(128), (), [bf_bc])

            norm_transpose(lambda i: mem_d[i * 128:(i + 1) * 128, :], 2, mng_d[l:l + 1, :], hT)
            ar.reset()
            wb2 = [ar.alloc(f"wb{i}", [128, KT, 512], BF16) for i in range(2)]
            load_w(wkv_l, 0, 512, wb2[0])
            load_w(wkv_l, 512, 512, wb2[1])
            for h in range(4):
                def cons(n, ps, h=h):
                    evac(memKT[:, h, :], ps[:, 0:MEM], [ps], [memKT])
                proj_F(wb2[0], h * 128, 128, hT, MEM, cons)

            def cons(i, ps):
                evac(memV[:, i, :], ps[:, :], [ps], [memV])
            proj_T(wb2[1], 0, 512, hT, 2, cons)
            p.barrier()

            norm_transpose(lambda i: x_src[i * 128:(i + 1) * 128, :], NT, ng_d[l:l + 1, :], hT, extra=lambda i: conv_some(1))
            if dbg_hT is not None and l == 0:
                for kt in range(KT):
                    p.dma("sp", dbg_hT[kt], hT[:, kt, :], [hT], ())
                p.barrier()

            ar.reset()
            wb2 = [ar.alloc(f"wb{i}", [128, KT, 512], BF16) for i in range(2)]
            stg = [ar.alloc(f"stg{i}", [128, 4, 512], BF16) for i in range(3)]
            stv = [ar.alloc(f"stv{i}", [128, 8, 128], BF16) for i in range(2)]
            lft = ar.alloc("lft", [128, 16], F32)
            for t in stv:
                A("pool", lambda e, t=t: e.memset(t[:], 1.0), (), [t])
            groups = [("gq", O_GQ), ("gk", O_GK), ("gv0", O_GV), ("gv1", O_GV + 512), ("small", None),
                      ("fq", O_FQ), ("fk", O_FK), ("fv", O_FV)]

            def issue_load(gi):
                name, c0 = groups[gi]
                wt = wb2[gi % 2]
                if name == "small":
                    load_w(win_l, O_GD, 16, wt, 0)
                    load_w(win_l, O_FL, 8, wt, 16)
                else:
                    load_w(win_l, c0, 512, wt)
            issue_load(0)
            sg = [0]

            def F_to_dram(wt, dst3, scale=1.0):
                for n in range(NG):
                    s = stg[sg[0] % 3]
                    sg[0] += 1
                    for m in range(4):
                        ps = nextps()
                        for kt in range(KT):
                            mm(ps[:, :], wt[:, kt, m * 128:(m + 1) * 128], hT[:, kt, n * 512:(n + 1) * 512], kt == 0, kt == KT - 1, [wt, hT], [ps])
                        evac(s[:, m, :], ps[:, :], [ps], [s], scale=scale)
                    p.dma("sp", dst3[:, :, n * 512:(n + 1) * 512].rearrange("m p t -> p m t"), s[:], [s], ())

            def T_to_dram(wt, dst2, c_dst):
                for i4 in range(4):
                    s = stg[sg[0] % 3]
                    sg[0] += 1
                    for j in range(4):
                        i = i4 * 4 + j
                        ps = nextps()
                        for kt in range(KT):
                            mm(ps[:, :], hT[:, kt, i * 128:(i + 1) * 128], wt[:, kt, 0:512], kt == 0, kt == KT - 1, [wt, hT], [ps])
                        evac(s[:, j, :], ps[:, :], [ps], [s])
                    p.dma("sp", dst2[i4 * 512:(i4 + 1) * 512, c_dst:c_dst + 512].rearrange("(j p) c -> p j c", p=128), s[:], [s], ())

            for gi, (name, c0) in enumerate(groups):
                if gi + 1 < len(groups):
                    issue_load(gi + 1)
                wt = wb2[gi % 2]
                if name == "gq":
                    F_to_dram(wt, gqT_d, scale=128.0 ** -0.5)
                elif name == "gk":
                    F_to_dram(wt, gkT_d)
                    T_to_dram(wt, ktok_d, 0)
                elif name == "gv0":
                    T_to_dram(wt, vtok_d, 0)
                elif name == "gv1":
                    T_to_dram(wt, vtok_d, 512)
                elif name == "fq":
                    F_to_dram(wt, fqT_d)
                elif name == "fk":
                    F_to_dram(wt, fkT_d.rearrange("(m p) t -> m p t", p=128))
                elif name == "fv":
                    for i in range(NT):
                        s = stv[i % 2]
                        ps = nextps()
                        for kt in range(KT):
                            mm(ps[:, :], hT[:, kt, i * 128:(i + 1) * 128], wt[:, kt, 0:512], kt == 0, kt == KT - 1, [wt, hT], [ps])
                        pv = ps[:, :].rearrange("p (a b c) -> p a b c", a=4, b=2)
                        sv = s[:, :, :].rearrange("p (a b) c -> p a b c", b=2)
                        A("dve", lambda e, sv=sv, pv=pv: e.tensor_copy(out=sv[:, :, 0, 0:64], in_=pv[:, :, 0, :]), [ps], [s])
                        A("dve", lambda e, sv=sv, pv=pv: e.tensor_copy(out=sv[:, :, 1, 64:128], in_=pv[:, :, 1, :]), [ps], [s])
                        p.dma("sp", fva_d[i // 8][(i % 8) * 128:(i % 8 + 1) * 128, :], s[:, :, :].rearrange("p a b -> p (a b)"), [s], [exB["fva"]])
                elif name == "small":
                    def cons(n, ps):
                        evac(gdT[0:16, n * 512:(n + 1) * 512], ps[0:16, :], [ps], [gdT])
                    proj_F(wt, 0, 16, hT, TOK, cons)
                    for i in range(NT):
                        ps = nextps()
                        for kt in range(KT):
                            mm(ps[:, 0:8], hT[:, kt, i * 128:(i + 1) * 128], wt[:, kt, 16:24], kt == 0, kt == KT - 1, [wt, hT], [ps])
                        A("dve", lambda e, ps=ps: e.tensor_tensor(out=lft[:, 0:8], in0=ps[:, 0:8], in1=bf_bc[:], op=ALU.add), [ps, bf_bc], [lft])
                        A("act", lambda e: e.activation(out=lft[:, 8:16], in_=lft[:, 0:8], func=AF.Exp, scale=-1.0), [lft], [lft])
                        A("act", lambda e, i=i: e.activation(out=lfp[:, i, :], in_=lft[:, 8:16], func=AF.Ln, bias=c_one[:], scale=1.0), [lft, c_one], [lfp])
            p.barrier()
            if stop_after == "A2":
                break

            ar.reset()
            kt_t = [ar.alloc(f"ktok{i}", [128, 512], BF16) for i in range(2)]
            vt_t = [ar.alloc(f"vtok{i}", [128, 1024], BF16) for i in range(2)]
            t1 = [ar.alloc(f"t1{i}", [128, 512], F32) for i in range(2)]
            gp_t = [ar.alloc(f"gp{i}", [128, 512], F32) for i in range(2)]
            eR = [ar.alloc(f"eR{i}", [128, 512], F32) for i in range(2)]
            kd_t = [ar.alloc(f"kd{i}", [128, 512], BF16) for i in range(2)]
            dec = [ar.alloc(f"dec{i}", [128, 4], F32) for i in range(2)]
            A("pool", lambda e: e.memset(Sst[:], 0.0), (), [Sst])
            for i in range(NT):
                k_, v_, t_, g_, r_, d_, dc = kt_t[i % 2], vt_t[i % 2], t1[i % 2], gp_t[i % 2], eR[i % 2], kd_t[i % 2], dec[i % 2]
                p.dma("sp", k_[:], ktok_d[i * 128:(i + 1) * 128, :], (), [k_])
                p.dma("sp", v_[:], vtok_d[i * 128:(i + 1) * 128, :], (), [v_])
                ps = nextps()
                mm(ps[:, :], gdT[0:17, i * 128:(i + 1) * 128], wupb[0:17, :], True, True, [gdT, wupb], [ps])
                A("act", lambda e, t_=t_, ps=ps: e.activation(out=t_[:], in_=ps[:, :], func=AF.Exp, scale=-1.0), [ps], [t_])
                A("act", lambda e, t_=t_, g_=g_: e.activation(out=g_[:], in_=t_[:], func=AF.Ln, bias=c_one[:], scale=1.0), [t_, c_one], [g_])
                p.dma("pool", gp_d[i * 128:(i + 1) * 128, :], g_[:], [g_], ())
                ps2 = nextps()
                mm(ps2[:, :], Lrev[:], g_[:], True, True, [Lrev, g_], [ps2])
                A("act", lambda e, r_=r_, ps2=ps2: e.activation(out=r_[:], in_=ps2[:, :], func=AF.Exp), [ps2], [r_])
                A("dve", lambda e, d_=d_, k_=k_, r_=r_: e.tensor_tensor(out=d_[:], in0=k_[:], in1=r_[:], op=ALU.mult), [k_, r_], [d_])
                p.dma("pool", kdec_d[i * 128:(i + 1) * 128, :], d_[:], [d_], ())
                conv_some(1)
                ps3 = nextps()
                for h in range(4):
                    mm(ps3[:, h:h + 1], g_[:, h * 128:(h + 1) * 128], neg16[:], True, True, [g_, neg16], [ps3])
                A("act", lambda e, dc=dc, ps3=ps3: e.activation(out=dc[:], in_=ps3[:, 0:4], func=AF.Exp), [ps3], [dc])
                for hp in range(2):
                    ps4 = nextps()
                    for hh in range(2):
                        h = hp * 2 + hh
                        mm(ps4[:, hh * 256:(hh + 1) * 256], d_[:, h * 128:(h + 1) * 128], v_[:, h * 256:(h + 1) * 256], True, True, [d_, v_], [ps4])
                    for hh in range(2):
                        h = hp * 2 + hh
                        A("dve", lambda e, h=h, hh=hh, dc=dc, ps4=ps4: e.scalar_tensor_tensor(
                            out=Sst[:, h, :], in0=Sst[:, h, :], scalar=dc[:, h:h + 1], in1=ps4[:, hh * 256:(hh + 1) * 256],
                            op0=ALU.mult, op1=ALU.add), [Sst, dc, ps4], [Sst])
            p.dma("sp", cinU_d[:, :], Sst[:, :, :].rearrange("p a b -> p (a b)"), [Sst], [exB["cinU"]])

            rs_sb = ar.alloc("rs_sb", [128, NT, 8], F32)
            tot_sb = ar.alloc("tot_sb", [128, NT, 8], F32)
            acc = ar.alloc("acc", [128, 8], F32)
            psr, pst = nextps(), nextps()
            for j in range(NT):
                mm(psr[:, j * 8:(j + 1) * 8], SU[:], lfp[:, j, :], True, True, [SU, lfp], [psr])
                mm(pst[:, j * 8:(j + 1) * 8], onesf[:], lfp[:, j, :], True, True, [onesf, lfp], [pst])
            evac(rs_sb[:, :, :].rearrange("p a b -> p (a b)"), psr[:, 0:128], [psr], [rs_sb], eng="dve")
            evac(tot_sb[:, :, :].rearrange("p a b -> p (a b)"), pst[:, 0:128], [pst], [tot_sb], eng="dve")
            A("dve", lambda e: e.memset(acc[:], 0.0), (), [acc])
            A("dve", lambda e: e.memset(Tn[:, 3, :], 0.0), (), [Tn])
            for j in range(NT - 1, -1, -1):
                A("dve", lambda e, j=j: e.tensor_tensor(out=SUF[:, j, :], in0=rs_sb[:, j, :], in1=acc[:], op=ALU.add), [rs_sb, acc], [SUF])
                A("dve", lambda e, j=j: e.tensor_tensor(out=acc[:], in0=acc[:], in1=tot_sb[:, j, :], op=ALU.add), [acc, tot_sb], [acc])
                if j % 4 == 0 and j > 0:
                    n = j // 4 - 1
                    A("dve", lambda e, n=n: e.tensor_copy(out=Tn[:, n, :], in_=acc[:]), [acc], [Tn])
            for n in range(NG):
                A("dve", lambda e, n=n: e.tensor_tensor(out=PREn[:, n, :], in0=acc[:], in1=Tn[:, n, :], op=ALU.subtract), [acc, Tn], [PREn])
            p.dma("sp", cinS_d[:, :], SUF[:, :, :].rearrange("p a b -> p (a b)"), [SUF], [exB["cinS"]])
            p.barrier()

            ar.reset()
            wb2 = [ar.alloc(f"wbg{i}", [128, KT, 512], BF16) for i in range(2)]
            ggroups = [(O_GG, 0), (O_GG + 512, 4), (O_FG, 8), (O_MG, 12)]
            load_w(win_l, ggroups[0][0], 512, wb2[0])
            load_w(win_l, ggroups[1][0], 512, wb2[1])
            rg = [[2 * i, 2 * i + 1] for i in range(ncores // 2)]
            for src, dst, sn, dn in ((fkT_d, fkT_all, "fkT", "fkT_all"), (fva_d[0], fva_all[0], "fva", "fva_all"), (fva_d[1], fva_all[1], "fva", "fva_all"),
                                     (cinU_d, coutU_d, "cinU", "coutU"), (cinS_d, coutS_d, "cinS", "coutS")):
                if nocc:
                    continue
                p.cc(lambda e, src=src, dst=dst: e.collective_compute("AllGather", ALU.bypass, replica_groups=rg,
                                                                      ins=[src[:, :]], outs=[dst[:, :]]), [exB[sn]], [exB[dn]])
            for gi, (c0, ch0) in enumerate(ggroups):
                if gi >= 1 and gi + 1 < len(ggroups):
                    load_w(win_l, ggroups[gi + 1][0], 512, wb2[(gi + 1) % 2])
                for c in range(4):
                    def cons(n, ps, ch=ch0 + c):
                        evac(omix[:, ch, n * 512:(n + 1) * 512], ps[:, :], [ps], [omix], func=AF.Silu)
                    proj_F(wb2[gi % 2], c * 128, 128, hT, TOK, cons)
            p.barrier()
            if stop_after == "A3":
                break

            ar.reset()
            qT_t = [ar.alloc(f"qT{i}", [128, 4, 128], BF16) for i in range(2)]
            kT_t = [ar.alloc(f"kT{i}", [128, 4, 128], BF16) for i in range(2)]
            kd_t = [ar.alloc(f"kd{i}", [128, 512], BF16) for i in range(2)]
            vt_t = [ar.alloc(f"vtok{i}", [128, 1024], BF16) for i in range(2)]
            gp_t = [ar.alloc(f"gp{i}", [128, 512], F32) for i in range(2)]
            NS = 3
            E_t = [[ar.alloc(f"E{k}_{i}", [128, 128], F32) for i in range(4)] for k in range(NS)]
            Ei_t = [[ar.alloc(f"Ei{k}_{i}", [128, 128], F32) for i in range(4)] for k in range(NS)]
            qd_t = [[ar.alloc(f"qd{k}_{i}", [128, 128], BF16) for i in range(4)] for k in range(NS)]
            kdd_t = [[ar.alloc(f"kdd{k}_{i}", [128, 128], BF16) for i in range(4)] for k in range(NS)]
            at_t = [[ar.alloc(f"at{k}_{i}", [128, 128], BF16) for i in range(4)] for k in range(NS)]
            on_t = [ar.alloc(f"on{i}", [128, 1024], BF16) for i in range(2)]
            nst = [ar.alloc(f"gst{i}", [128, 16], F32) for i in range(2)]
            junkg = ar.alloc("junkg", [128, 256], BF16)
            Uin = ar.alloc("Uin", [128, 1024], F32)
            p.dma("sp", Uin[:], coutU_d[0:128, :], [exB["coutU"]], [Uin])
            A("dve", lambda e: e.tensor_scalar(out=Sst[:, :, :].rearrange("p a b -> p (a b)"), in0=Uin[:], scalar1=flags[:, 1:2], scalar2=None, op0=ALU.mult),
              [Uin, flags], [Sst])
            A("act", lambda e: e.activation(out=Sbf[:, :, :].rearrange("p a b -> p (a b)"), in_=Sst[:, :, :].rearrange("p a b -> p (a b)"), func=AF.Copy), [Sst], [Sbf])
            g_rr = [0]

            def gps():
                g_rr[0] = (g_rr[0] + 1) % 5
                return psb[3 + g_rr[0]]

            def gla_A1(i):
                q_, k_, g_ = qT_t[i % 2], kT_t[i % 2], gp_t[i % 2]
                tsl = slice(i * 128, (i + 1) * 128)
                p.dma("sp", q_[:], gqT_d[:, :, tsl].rearrange("m p t -> p m t"), (), [q_])
                p.dma("sp", k_[:], gkT_d[:, :, tsl].rearrange("m p t -> p m t"), (), [k_])
                p.dma("sp", g_[:], gp_d[tsl, :], (), [g_])
                pbs = []
                for h in range(4):
                    psb_ = gps()
                    mm(psb_[:, 0:128], g_[:, h * 128:(h + 1) * 128], Ucum[:], True, True, [g_, Ucum], [psb_])
                    E, Ei = E_t[i % NS][h], Ei_t[i % NS][h]
                    A("act", lambda e, E=E, psb_=psb_: e.activation(out=E[:], in_=psb_[:, 0:128], func=AF.Exp), [psb_], [E])
                    A("act", lambda e, Ei=Ei, psb_=psb_: e.activation(out=Ei[:], in_=psb_[:, 0:128], func=AF.Exp, scale=-1.0), [psb_], [Ei])
                for h in range(4):
                    E, Ei, qd, kdd = E_t[i % NS][h], Ei_t[i % NS][h], qd_t[i % NS][h], kdd_t[i % NS][h]
                    A("dve", lambda e, qd=qd, q_=q_, h=h, E=E: e.tensor_tensor(out=qd[:], in0=q_[:, h, :], in1=E[:], op=ALU.mult), [q_, E], [qd])
                    A("dve", lambda e, kdd=kdd, k_=k_, h=h, Ei=Ei: e.tensor_tensor(out=kdd[:], in0=k_[:, h, :], in1=Ei[:], op=ALU.mult), [k_, Ei], [kdd])

            def gla_A2(i):
                d_, v_ = kd_t[i % 2], vt_t[i % 2]
                tsl = slice(i * 128, (i + 1) * 128)
                p.dma("sp", d_[:], kdec_d[tsl, :], (), [d_])
                p.dma("sp", v_[:], vtok_d[tsl, :], (), [v_])
                for h in range(4):
                    qd, kdd, at = qd_t[i % NS][h], kdd_t[i % NS][h], at_t[i % NS][h]
                    psa = gps()
                    mm(psa[:, 0:128], kdd[:], qd[:], True, True, [kdd, qd], [psa])
                    A("dve", lambda e, at=at, psa=psa: e.tensor_tensor(out=at[:], in0=psa[:, 0:128], in1=maskGf[:], op=ALU.mult), [psa, maskGf], [at])

            def gla_B(i):
                d_, v_ = kd_t[i % 2], vt_t[i % 2]
                on, s = on_t[i % 2], nst[i % 2]
                tsl = slice(i * 128, (i + 1) * 128)
                pso = [psb[0], psb[1]]
                for h in range(4):
                    E, qd, at = E_t[i % NS][h], qd_t[i % NS][h], at_t[i % NS][h]
                    hs = slice(h * 128, (h + 1) * 128)
                    vs = slice(h * 256, (h + 1) * 256)
                    po = pso[h // 2]
                    pos = slice((h % 2) * 256, (h % 2 + 1) * 256)
                    mm(po[:, pos], qd[:], Sbf[:, h, :], True, False, [qd, Sbf], [po])
                    mm(po[:, pos], at[:], v_[:, vs], False, True, [at, v_], [po])
                    pss = gps()
                    mm(pss[:, 0:256], d_[:, hs], v_[:, vs], True, True, [d_, v_], [pss])
                    A("dve", lambda e, h=h, E=E, pss=pss: e.scalar_tensor_tensor(out=Sst[:, h, :], in0=Sst[:, h, :], scalar=E[:, 127:128], in1=pss[:, 0:256],
                                                                              op0=ALU.mult, op1=ALU.add), [Sst, E, pss], [Sst])
                    A("act", lambda e, h=h: e.activation(out=Sbf[:, h, :], in_=Sst[:, h, :], func=AF.Copy), [Sst], [Sbf])
                for h in range(4):
                    po = pso[h // 2]
                    pos = slice((h % 2) * 256, (h % 2 + 1) * 256)
                    A("act", lambda e, po=po, pos=pos, s=s, h=h: e.activation(out=junkg[:], in_=po[:, pos], func=AF.Square, accum_out=s[:, h:h + 1]), [po], [junkg, s])
                A("act", lambda e, s=s: e.activation(out=s[:, 4:8], in_=s[:, 0:4], func=AF.Sqrt, scale=1.0 / 256.0, bias=c_eps[:]), [s, c_eps], [s])
                A("dve", lambda e, s=s: e.reciprocal(out=s[:, 8:12], in_=s[:, 4:8]), [s], [s])
                for h in range(4):
                    po = pso[h // 2]
                    pos = slice((h % 2) * 256, (h % 2 + 1) * 256)
                    A("dve", lambda e, on=on, po=po, pos=pos, s=s, h=h: e.scalar_tensor_tensor(
                        out=on[:, h * 256:(h + 1) * 256], in0=po[:, pos], scalar=s[:, 8 + h:9 + h], in1=gng_bc[:], op0=ALU.mult, op1=ALU.mult),
                      [po, s, gng_bc], [on])
                pt = psb[2]
                ptv = psbf(pt)
                for c in range(8):
                    A("pe", lambda e, ptv=ptv, c=c, on=on: e.transpose(ptv[:, c * 128:(c + 1) * 128], on[:, c * 128:(c + 1) * 128], ident[:]), [on, ident], [pt])
                A("dve", lambda e, ptv=ptv, tsl=tsl: e.tensor_tensor(out=omix[:, 0:8, tsl], in0=ptv[:, :].rearrange("p (a b) -> p a b", a=8),
                                                                      in1=omix[:, 0:8, tsl], op=ALU.mult), [pt, omix], [omix])

            gla_A1(0)
            gla_A1(1)
            gla_A2(0)
            for i in range(NT):
                if i + 2 < NT:
                    gla_A1(i + 2)
                if i + 1 < NT:
                    gla_A2(i + 1)
                gla_B(i)
                conv_some(3)
            p.barrier()
            if stop_after == "B1a":
                break

            ar.reset()
            p.dma("sp", SUFo[:, :, :].rearrange("p a b -> p (a b)"), coutS_d[0:128, :], [exB["coutS"]], [SUFo])
            for n in range(NG):
                A("dve", lambda e, n=n: e.tensor_scalar(out=small[:, 0:8], in0=PREn[:, n, :], scalar1=-1.0, scalar2=flags[:, 0:1], op0=ALU.mult, op1=ALU.add),
                  [PREn, flags], [small])
                for j in range(NT):
                    A("dve", lambda e, n=n, j=j: e.tensor_tensor(out=bias_all[:, n, j, :], in0=small[:, 0:8], in1=SUFo[:, j, :], op=ALU.subtract),
                      [small, SUFo], [bias_all])
                    A("dve", lambda e, n=n, j=j: e.tensor_tensor(out=bias_all[:, n, 16 + j, :], in0=Tn[:, n, :], in1=SUF[:, j, :], op=ALU.subtract),
                      [Tn, SUF], [bias_all])
            QT = [ar.alloc(f"QT{i}", [128, 8, 512], BF16) for i in range(2)]
            for t in QT:
                A("pool", lambda e, t=t: e.memset(t[:], 0.0), (), [t])
            NKV = 5
            KTt = [ar.alloc(f"KTt{i}", [128, 4, 128], BF16) for i in range(NKV)]
            Vt = [ar.alloc(f"Vt{i}", [128, 8, 128], BF16) for i in range(NKV)]
            PT = [ar.alloc(f"PT{i}", [128, 512], BF16) for i in range(4)]
            rc = [ar.alloc(f"rc{i}", [128, 512], F32) for i in range(2)]
            tmpo = [ar.alloc(f"tmpo{i}", [128, 512], F32) for i in range(2)]
            acs = [ar.alloc(f"acs{i}", [128, 512], F32) for i in range(2)]

            kvc = [0]
            ptc = [0]
            LA = 2
            PF = 3
            for n in range(NG):
                qt = QT[n % 2]
                for par in range(2):
                    rsl = slice(par * 64, par * 64 + 64)
                    p.dma("sp", qt.ap[rsl, par::2, :], fqT_d[:, rsl, n * 512:(n + 1) * 512].rearrange("m p t -> p m t"), (), [qt])
                keys = [("o", j) for j in range(NT)] + [("s", j) for j in range(4 * n + 4)]
                nk = len(keys)
                for hb in range(2):
                    accs = [psb[k] for k in range(4)]
                    kv = {}

                    def load_kv(ki):
                        kind, j = keys[ki]
                        kt_ = KTt[kvc[0] % NKV]
                        vt_ = Vt[kvc[0] % NKV]
                        kvc[0] += 1
                        if kind == "o":
                            p.dma("sp", kt_[:], fkT_all[0:512, j * 128:(j + 1) * 128].rearrange("(m p) t -> p m t", p=128), [exB["fkT_all"]], [kt_])
                            p.dma("sp", vt_[:, :, :].rearrange("p a b -> p (a b)"), fva_all[j // 8][(j % 8) * 128:(j % 8 + 1) * 128, :], [exB["fva_all"]], [vt_])
                            kv[ki] = (kt_, vt_, j, 0)
                        else:
                            p.dma("sp", kt_[:], fkT_d[:, j * 128:(j + 1) * 128].rearrange("(m p) t -> p m t", p=128), [exB["fkT"]], [kt_])
                            p.dma("sp", vt_[:, :, :].rearrange("p a b -> p (a b)"), fva_d[j // 8][(j % 8) * 128:(j % 8 + 1) * 128, :], [exB["fva"]], [vt_])
                            kv[ki] = (kt_, vt_, 16 + j, max(0, j - 4 * n) * 128)

                    units = [(ki, hh) for ki in range(nk) for hh in range(4)]
                    ust = {}

                    def front(u):
                        ki, hh = units[u]
                        kind, j = keys[ki]
                        kt_, vt_, bj, q0 = kv[ki]
                        h = hb * 4 + hh
                        rows = slice((h % 2) * 64, (h % 2) * 64 + 64)
                        pr = h // 2
                        pss = psb[4 + (ptc[0] % 4)]
                        pt_ = PT[ptc[0] % 4]
                        ptc[0] += 1
                        mm(pss[:, q0:512], kt_[:, pr, :], qt[:, h, q0:512], True, True, [kt_, qt], [pss])
                        A("act", lambda e, pt_=pt_, pss=pss, q0=q0, n=n, bj=bj, h=h: e.activation(
                            out=pt_[:, q0:512], in_=pss[:, q0:512], func=AF.Exp, scale=0.125, bias=bias_all[:, n, bj, h:h + 1]),
                          [pss, bias_all], [pt_])
                        if kind == "s" and j >= 4 * n:
                            A("dve", lambda e, pt_=pt_, q0=q0: e.tensor_tensor(out=pt_[:, q0:q0 + 128], in0=pt_[:, q0:q0 + 128], in1=maskG[:], op=ALU.mult),
                              [pt_, maskG], [pt_])
                        ust[u] = pt_

                    def back(u):
                        ki, hh = units[u]
                        kt_, vt_, bj, q0 = kv[ki]
                        h = hb * 4 + hh
                        pt_ = ust.pop(u)
                        mm(accs[hh][:, q0:512], vt_[:, h, :], pt_[:, q0:512], ki == 0, ki == nk - 1, [vt_, pt_], [accs[hh]])

                    for ki in range(min(PF, nk)):
                        load_kv(ki)
                    for idx in range(len(units) + LA):
                        if idx < len(units):
                            ki, hh = units[idx]
                            if hh == 0 and ki + PF < nk:
                                load_kv(ki + PF)
                            front(idx)
                        if idx >= LA:
                            back(idx - LA)
                    for hh in range(4):
                        ac = acs[hh % 2]
                        A("dve", lambda e, ac=ac, acc_=accs[hh]: e.tensor_copy(out=ac[:], in_=acc_[:, :]), [accs[hh]], [ac])
                        h = hb * 4 + hh
                        orow = slice((h % 2) * 64, (h % 2) * 64 + 64)
                        srow = slice((1 - h % 2) * 64, (1 - h % 2) * 64 + 64)
                        rc_, tm_ = rc[hh % 2], tmpo[hh % 2]
                        A("dve", lambda e, rc_=rc_, ac=ac, orow=orow, srow=srow: e.reciprocal(out=rc_[orow, :], in_=ac[srow, :]), [ac], [rc_])
                        A("dve", lambda e, tm_=tm_, rc_=rc_, ac=ac, orow=orow: e.tensor_tensor(out=tm_[orow, :], in0=ac[orow, :], in1=rc_[orow, :], op=ALU.mult),
                          [ac, rc_], [tm_])
                        A("dve", lambda e, tm_=tm_, orow=orow, h=h, n=n: e.tensor_tensor(out=omix[orow, 8 + h // 2, n * 512:(n + 1) * 512], in0=tm_[orow, :],
                                                                                      in1=omix[orow, 8 + h // 2, n * 512:(n + 1) * 512], op=ALU.mult),
                          [tm_, omix], [omix])
            p.barrier()
            if stop_after == "B1b":
                break

            ar.reset()
            mqT = ar.alloc("mqT", [128, 4, TOK], BF16)
            wb2 = [ar.alloc(f"wb{i}", [128, KT, 512], BF16) for i in range(2)]
            load_w(win_l, O_MQ, 512, wb2[1])
            for c in range(4):
                def cons2(n, ps, c=c):
                    evac(mqT[:, c, n * 512:(n + 1) * 512], ps[:, :], [ps], [mqT])
                proj_F(wb2[1], c * 128, 128, hT, TOK, cons2)
            p.barrier()
            ar.off -= 2 * (KT * 512 // 2)
            PTm = [ar.alloc(f"PTm{i}", [128, 512], BF16) for i in range(4)]
            rc = [ar.alloc(f"rc{i}", [128, 512], F32) for i in range(2)]
            tmpo = [ar.alloc(f"tmpo{i}", [128, 512], F32) for i in range(2)]
            pc = [0]
            munits = [(n, h) for n in range(NG) for h in range(4)]
            mst = {}

            def mfront(u):
                n, h = munits[u]
                pts = []
                for mt in range(2):
                    pss = nextps()
                    pt_ = PTm[pc[0] % 4]
                    pc[0] += 1
                    mm(pss[:, :], memKT[:, h, mt * 128:(mt + 1) * 128], mqT[:, h, n * 512:(n + 1) * 512], True, True, [memKT, mqT], [pss])
                    A("act", lambda e, pt_=pt_, pss=pss: e.activation(out=pt_[:], in_=pss[:, :], func=AF.Exp, scale=128.0 ** -0.5), [pss], [pt_])
                    pts.append(pt_)
                mst[u] = pts

            def mback(u):
                n, h = munits[u]
                pts = mst.pop(u)
                pso_, psm = nextps(), nextps()
                for mt in range(2):
                    mm(pso_[:, :], memV[:, mt, h * 128:(h + 1) * 128], pts[mt][:], mt == 0, mt == 1, [memV, pts[mt]], [pso_])
                for mt in range(2):
                    mm(psm[:, :], onesb[:], pts[mt][:], mt == 0, mt == 1, [onesb, pts[mt]], [psm])
                rc_, tm_ = rc[h % 2], tmpo[h % 2]
                A("act", lambda e, rc_=rc_, psm=psm: e.activation(out=rc_[:], in_=psm[:, :], func=AF.Ln), [psm], [rc_])
                A("act", lambda e, rc_=rc_: e.activation(out=rc_[:], in_=rc_[:], func=AF.Exp, scale=-1.0), [rc_], [rc_])
                A("dve", lambda e, tm_=tm_, rc_=rc_, pso_=pso_: e.tensor_tensor(out=tm_[:], in0=pso_[:, :], in1=rc_[:], op=ALU.mult), [pso_, rc_], [tm_])
                A("dve", lambda e, tm_=tm_, h=h, n=n: e.tensor_tensor(out=omix[:, 12 + h, n * 512:(n + 1) * 512], in0=tm_[:],
                                                                       in1=omix[:, 12 + h, n * 512:(n + 1) * 512], op=ALU.mult), [tm_, omix], [omix])

            mfront(0)
            for u in range(len(munits)):
                if u + 1 < len(munits):
                    mfront(u + 1)
                mback(u)
            p.barrier()
            if dbg_omix is not None and l == 0:
                for c in range(16):
                    p.dma("sp", dbg_omix[c], omix[:, c, :], [omix], ())
                p.barrier()
            if stop_after == "B1c":
                break

            conv_some(1000)
            p.barrier()
            ar.reset()
            wbr_t = [ar.alloc(f"wbr{i}", [128, KT, 128], BF16) for i in range(2)]
            wmg_t = [ar.alloc(f"wmg{i}", [128, KT, 384], BF16) for i in range(2)]
            G_t = [ar.alloc(f"G{i}", [128, 512], F32) for i in range(4)]
            y_t = [ar.alloc(f"y{i}", [128, 512], F32) for i in range(2)]
            t_t = [ar.alloc(f"t{i}", [128, 512], F32) for i in range(2)]
            yb_t = [ar.alloc(f"yb{i}", [128, 512], BF16) for i in range(2)]
            branches = [(0, 8), (8, 12), (12, 16)]

            def load_f(f):
                load_w(wbr_l, f * 128, 128, wbr_t[f % 2])
                for br in range(3):
                    load_w(win_l, O_MERGE + br * D + f * 128, 128, wmg_t[f % 2], br * 128)
            load_f(0)
            gc = [0]
            for f in range(16):
                if f + 1 < 16:
                    load_f(f + 1)
                wr, wm = wbr_t[f % 2], wmg_t[f % 2]
                for n in range(NG):
                    ns = slice(n * 512, (n + 1) * 512)
                    Gs = []
                    for br in range(3):
                        ps = nextps()
                        for kt in range(KT):
                            mm(ps[:, :], wm[:, kt, br * 128:(br + 1) * 128], hT[:, kt, ns], kt == 0, kt == KT - 1, [wm, hT], [ps])
                        G = G_t[gc[0] % 4]
                        gc[0] += 1
                        A("act", lambda e, G=G, ps=ps: e.activation(out=G[:], in_=ps[:, :], func=AF.Sigmoid), [ps], [G])
                        Gs.append(G)
                    y, t, yb = y_t[n % 2], t_t[n % 2], yb_t[n % 2]
                    for br, (k0, k1) in enumerate(branches):
                        ps = nextps()
                        for kc in range(k0, k1):
                            mm(ps[:, :], wr[:, kc, :], omix[:, kc, ns], kc == k0, kc == k1 - 1, [wr, omix], [ps])
                        if br == 0:
                            A("dve", lambda e, y=y, ps=ps, G=Gs[0]: e.tensor_tensor(out=y[:], in0=ps[:, :], in1=G[:], op=ALU.mult), [ps, Gs[0]], [y])
                        else:
                            A("dve", lambda e, t=t, ps=ps, G=Gs[br]: e.tensor_tensor(out=t[:], in0=ps[:, :], in1=G[:], op=ALU.mult), [ps, Gs[br]], [t])
                            if br == 1:
                                A("dve", lambda e, y=y, t=t: e.tensor_tensor(out=y[:], in0=y[:], in1=t[:], op=ALU.add), [y, t], [y])
                            else:
                                A("dve", lambda e, y=y, t=t, yb=yb: e.tensor_tensor(out=yb[:], in0=y[:], in1=t[:], op=ALU.add), [y, t], [yb])
                    p.dma("sp", yT_d[f, :, ns], yb[:], [yb], ())
            p.barrier()

            ar.reset()
            wb2 = [ar.alloc(f"wb{i}", [128, KT, 512], BF16) for i in range(2)]
            xt_t = [ar.alloc(f"xt{i}", [128, 512], F32) for i in range(3)]
            xo_t = [ar.alloc(f"xo{i}", [128, 512], F32) for i in range(3)]
            yT = omix
            for f in range(16):
                p.dma("sp", yT[:, f, :], yT_d[f], (), [yT])
            load_w(wout_l, 0, 512, wb2[0])
            jobs = [(cg, i) for cg in range(4) for i in range(NT)]

            def xload(k):
                cg, i = jobs[k]
                xt = xt_t[k % 3]
                p.dma("sp", xt[:], x_src[i * 128:(i + 1) * 128, cg * 512:(cg + 1) * 512], (), [xt])
            xload(0)
            xload(1)
            for k, (cg, i) in enumerate(jobs):
                if i == 0 and cg + 1 < 4:
                    load_w(wout_l, (cg + 1) * 512, 512, wb2[(cg + 1) % 2])
                if k + 2 < len(jobs):
                    xload(k + 2)
                wt = wb2[cg % 2]
                cs = slice(cg * 512, (cg + 1) * 512)
                xt, xo = xt_t[k % 3], xo_t[k % 3]
                ps = nextps()
                for f in range(16):
                    mm(ps[:, :], yT[:, f, i * 128:(i + 1) * 128], wt[:, f, :], f == 0, f == 15, [yT, wt], [ps])
                A("dve", lambda e, xo=xo, ps=ps, xt=xt: e.tensor_tensor(out=xo[:], in0=ps[:, :], in1=xt[:], op=ALU.add), [ps, xt], [xo])
                p.dma("pool", x_dst[i * 128:(i + 1) * 128, cs], xo[:], [xo], ())
            p.barrier()

        if stop_after is None:
            ar.reset()
            xinF = [ar.alloc(f"xinF{i}", [128, D], F32) for i in range(2)]
            xoF = [ar.alloc(f"xoF{i}", [128, D], F32) for i in range(2)]
            gbcF = ar.alloc("gbcF", [128, D], F32)
            junkF = ar.alloc("junkF", [128, D], BF16)
            stF = [ar.alloc(f"nst{i}", [128, 4], F32) for i in range(2)]
            xf = xres[(nlayers - 1) % 2]
            p.dma("sp", gbcF[:], fg_d[0:1, :].partition_broadcast(128), (), [gbcF])
            for i in range(NT):
                xt, o_, s = xinF[i % 2], xoF[i % 2], stF[i % 2]
                p.dma("sp", xt[:], xf[i * 128:(i + 1) * 128, :], (), [xt])
                A("act", lambda e, xt=xt, s=s: e.activation(out=junkF[:], in_=xt[:], func=AF.Square, accum_out=s[:, 0:1]), [xt], [junkF, s])
                A("act", lambda e, s=s: e.activation(out=s[:, 1:2], in_=s[:, 0:1], func=AF.Sqrt, scale=1.0 / D, bias=c_eps[:]), [s, c_eps], [s])
                A("dve", lambda e, s=s: e.reciprocal(out=s[:, 2:3], in_=s[:, 1:2]), [s], [s])
                A("dve", lambda e, xt=xt, o_=o_, s=s: e.scalar_tensor_tensor(out=o_[:], in0=xt[:], scalar=s[:, 2:3], in1=gbcF[:],
                                                                              op0=ALU.mult, op1=ALU.mult), [xt, s, gbcF], [o_])
                p.dma("pool", out_d[i * 128:(i + 1) * 128, :], o_[:], [o_], ())
        p.barrier()
        counts = p.emit()
        _NC_CACHE['prog'] = p
    return nc, counts


_NC_CACHE = {}


def make_in_maps(inputs):
    f32 = np.float32
    x = np.asarray(inputs["x"], dtype=f32)
    mem = np.asarray(inputs["mem"], dtype=f32)
    shared = {k: np.ascontiguousarray(np.asarray(inputs[k], dtype=f32)) for k in
              ("norm_gain", "w_in", "w_gk_up", "b_gk", "gla_norm_gain", "b_f", "mem_norm_gain", "w_mem_kv", "w_branch", "w_out")}
    shared["final_gain"] = np.ascontiguousarray(np.asarray(inputs["final_gain"], dtype=f32).reshape(1, D))
    in_maps = []
    for c in range(8):
        b, half = c // 2, c % 2
        flags = np.zeros((128, 2), f32)
        flags[:, 0] = 0.0 if half == 1 else -30000.0
        flags[:, 1] = 1.0 if half == 1 else 0.0
        m = dict(shared)
        m["x"] = np.ascontiguousarray(x[b, half * TOK:(half + 1) * TOK])
        m["mem"] = np.ascontiguousarray(mem[b])
        m["flags"] = flags
        in_maps.append(m)
    return in_maps


def kernel(**inputs):
    if "nc" not in _NC_CACHE:
        _NC_CACHE["nc"] = build()[0]
    nc = _NC_CACHE["nc"]
    in_maps = make_in_maps(inputs)
    res = run_bass_kernel_spmd(nc, in_maps, core_ids=list(range(8)))
    out = np.empty((4, 4096, D), np.float32)
    for c in range(8):
        b, half = c // 2, c % 2
        out[b, half * TOK:(half + 1) * TOK] = np.asarray(res.results[c]["out"], dtype=np.float32)
    return out
```

```python
import contextlib
import numpy as np
import concourse.bass as bass
import concourse.mybir as mybir
from concourse.bass_utils import run_bass_kernel_spmd

F32 = mybir.dt.float32
BF16 = mybir.dt.bfloat16
AF = mybir.ActivationFunctionType
ALU = mybir.AluOpType

D = 2048
TOK = 2048
NT = 16
KT = 16
NG = 4
DEPTH = 2
INC = 12312
MEM = 256
O_GQ, O_GK, O_GV, O_GG, O_GD, O_FQ, O_FK, O_FV, O_FL, O_FG, O_MQ, O_MG, O_MERGE = (
    0, 512, 1024, 2048, 3072, 3088, 3600, 4112, 4624, 4632, 5144, 5656, 6168)
EPS = 1e-6
SAME_ENGINE_SYNC = True


class Buf:
    __slots__ = ("name", "w", "r")

    def __init__(self, name):
        self.name = name
        self.w = None
        self.r = {}


class V:
    def __init__(self, ap, name, buf=None):
        self.ap = ap
        self.buf = buf or Buf(name)

    def __getitem__(self, k):
        return self.ap[k]


def _b(x):
    return getattr(x, "buf", x)


class Prog:
    ENGS = ("pe", "act", "dve", "pool", "sp")
    CENGS = ("pe", "act", "dve", "pool")

    def __init__(self, nc, es):
        self.nc = nc
        self.stream = {k: [] for k in self.ENGS}
        self.ecount = {k: 0 for k in self.ENGS}
        self.known = {k: {} for k in self.ENGS}
        self.sems = {}
        for k in self.CENGS:
            self.sems[("e", k)] = es.enter_context(nc.semaphore("es_" + k))
        self.dkeys = {}
        self.dcount = {}
        self.dnext = {}
        for q, n in {"sp": 12, "pool": 8}.items():
            self.dkeys[q] = []
            for i in range(n):
                key = ("d", q, i)
                self.sems[key] = es.enter_context(nc.semaphore(f"ds_{q}{i}"))
                self.dkeys[q].append(key)
                self.dcount[key] = 0
            self.dnext[q] = 0
        self.cckey = ("c", "cc")
        self.sems[self.cckey] = es.enter_context(nc.semaphore("cc_sem"))
        self.dcount[self.cckey] = 0
        self.signaled = {k: set() for k in self.CENGS}

    def _dep(self, eng, tok):
        if tok is None:
            return
        key, idx = tok
        if key[0] == "e" and key[1] == eng:
            if eng == "pe" or not SAME_ENGINE_SYNC:
                return
        if self.known[eng].get(key, 0) >= idx:
            return
        self.known[eng][key] = idx
        if key[0] == "e":
            self.signaled[key[1]].add(idx)
        self.stream[eng].append(("w", key, idx))

    def _deps(self, eng, reads, writes):
        for b in reads:
            self._dep(eng, _b(b).w)
        for b in writes:
            b = _b(b)
            self._dep(eng, b.w)
            for k, v in b.r.items():
                self._dep(eng, (k, v))

    def _mark(self, tok, reads, writes):
        for b in writes:
            b = _b(b)
            b.w = tok
            b.r = {}
        for b in reads:
            b = _b(b)
            if b.r.get(tok[0], 0) < tok[1]:
                b.r[tok[0]] = tok[1]

    def op(self, eng, fn, reads=(), writes=()):
        self._deps(eng, reads, writes)
        self.ecount[eng] += 1
        tok = (("e", eng), self.ecount[eng])
        self.stream[eng].append(("i", fn, tok))
        self._mark(tok, reads, writes)
        return tok

    def dma(self, q, out, in_, reads=(), writes=()):
        i = self.dnext[q]
        self.dnext[q] = (i + 1) % len(self.dkeys[q])
        key = self.dkeys[q][i]
        if self.dcount[key] > 0:
            self._dep(q, (key, self.dcount[key]))
        self._deps(q, reads, writes)
        self.dcount[key] += 16
        tok = (key, self.dcount[key])
        self.stream[q].append(("d", (out, in_), tok))
        self._mark(tok, reads, writes)
        return tok

    def cc(self, fn, reads=(), writes=()):
        q = "pool"
        key = self.cckey
        self._deps(q, reads, writes)
        self.dcount[key] += 1
        tok = (key, self.dcount[key])
        self.stream[q].append(("c", fn, tok))
        self._mark(tok, reads, writes)
        return tok

    def barrier(self):
        for e in self.ENGS:
            for k in self.CENGS:
                if k != e and self.ecount[k] > 0:
                    self._dep(e, (("e", k), self.ecount[k]))
            for key, cnt in self.dcount.items():
                if cnt > 0:
                    self._dep(e, (key, cnt))

    def emit(self):
        nc = self.nc
        rank = {}
        for k, s in self.signaled.items():
            rank[k] = {idx: r + 1 for r, idx in enumerate(sorted(s))}
        prog = self

        def run(engname, e):
            for ent in prog.stream[engname]:
                if ent[0] == "w":
                    _, key, idx = ent
                    val = rank[key[1]][idx] if key[0] == "e" else idx
                    e.wait_ge(prog.sems[key], val)
                elif ent[0] == "i":
                    _, fn, tok = ent
                    ins = fn(e)
                    if tok[1] in prog.signaled[engname]:
                        ins.then_inc(prog.sems[tok[0]], 1)
                elif ent[0] == "d":
                    _, (out, in_), tok = ent
                    e.dma_start(out=out, in_=in_).then_inc(prog.sems[tok[0]], 16)
                elif ent[0] == "c":
                    _, fn, tok = ent
                    fn(e).then_inc(prog.sems[tok[0]])

        with nc.Block() as block:
            @block.sync
            def _(e):
                run("sp", e)

            @block.tensor
            def _(e):
                run("pe", e)

            @block.scalar
            def _(e):
                run("act", e)

            @block.vector
            def _(e):
                run("dve", e)

            @block.gpsimd
            def _(e):
                run("pool", e)
        return {k: len(v) for k, v in self.stream.items()}


def build(dbg=None, nlayers=DEPTH, stop_after=None, ncores=8, nocc=False):
    nc = bass.Bass("TRN2", target_bir_lowering=False)
    es = contextlib.ExitStack()
    dbg = dbg or []
    with es:
        p = Prog(nc, es)

        def din(name, shape, dt=F32):
            return nc.dram_tensor(name, list(shape), dt, kind="ExternalInput").ap()

        def dscr(name, shape, dt, force_internal=False):
            kind = "ExternalOutput" if (name in dbg and not force_internal) else "Internal"
            return nc.dram_tensor(name, list(shape), dt, kind=kind).ap()

        x_d = din("x", [TOK, D])
        mem_d = din("mem", [MEM, D])
        ng_d = din("norm_gain", [DEPTH, D])
        win_d = din("w_in", [DEPTH, D, INC])
        wup_d = din("w_gk_up", [DEPTH, 16, 512])
        bgk_d = din("b_gk", [DEPTH, 512])
        gng_d = din("gla_norm_gain", [DEPTH, 256])
        bf_d = din("b_f", [DEPTH, 8])
        mng_d = din("mem_norm_gain", [DEPTH, D])
        wkv_d = din("w_mem_kv", [DEPTH, D, 1024])
        wbr_d = din("w_branch", [DEPTH, D, D])
        wout_d = din("w_out", [DEPTH, D, D])
        fg_d = din("final_gain", [1, D])
        flags_d = din("flags", [128, 2])
        out_d = nc.dram_tensor("out", [TOK, D], F32, kind="ExternalOutput").ap()

        xres = [dscr("xresA", [TOK, D], F32), dscr("xresB", [TOK, D], F32)]
        gqT_d = dscr("gqT", [4, 128, TOK], BF16)
        gkT_d = dscr("gkT", [4, 128, TOK], BF16)
        ktok_d = dscr("ktok", [TOK, 512], BF16)
        vtok_d = dscr("vtok", [TOK, 1024], BF16)
        gp_d = dscr("gp", [TOK, 512], F32)
        kdec_d = dscr("kdec", [TOK, 512], BF16)
        fqT_d = dscr("fqT", [4, 128, TOK], BF16)
        fkT_d = dscr("fkT", [512, TOK], BF16, True)
        fva_d = [dscr(f"fva{i}", [TOK // 2, 1024], BF16, True) for i in range(2)]
        cinU_d = dscr("cinU", [128, 1024], F32, True)
        cinS_d = dscr("cinS", [128, 128], F32, True)
        fkT_all = dscr("fkT_all", [1024, TOK], BF16, True)
        fva_all = [dscr(f"fva_all{i}", [TOK, 1024], BF16, True) for i in range(2)]
        coutU_d = dscr("coutU", [256, 1024], F32, True)
        coutS_d = dscr("coutS", [256, 128], F32, True)
        yT_d = dscr("yT", [16, 128, TOK], BF16)
        wconv_d = dscr("wconv", [16, 128, KT * 512], BF16)
        dbg_omix = dscr("dbg_omix", [16, 128, TOK], BF16) if "dbg_omix" in dbg else None
        dbg_hT = dscr("dbg_hT", [16, 128, TOK], BF16) if "dbg_hT" in dbg else None
        exB = {n: Buf(n) for n in ("fkT", "fva", "cinU", "cinS", "fkT_all", "fva_all", "coutU", "coutS")}

        def sb(name, shape, dt):
            h = es.enter_context(nc.sbuf_tensor(name, list(shape), dt))
            return V(h, name)

        hT = sb("hT", [128, KT, TOK], BF16)
        omix = sb("omix", [128, 16, TOK], BF16)
        ident = sb("ident", [128, 128], BF16)
        maskG = sb("maskG", [128, 128], BF16)
        maskGf = sb("maskGf", [128, 128], F32)
        Lrev = sb("Lrev", [128, 128], F32)
        Ucum = sb("Ucum", [128, 128], F32)
        SU = sb("SU", [128, 128], F32)
        onesf = sb("onesf", [128, 128], F32)
        onesb = sb("onesb", [128, 128], BF16)
        neg16 = sb("neg16", [128, 1], F32)
        c_eps = sb("c_eps", [128, 1], F32)
        c_one = sb("c_one", [128, 1], F32)
        flags = sb("flags_sb", [128, 2], F32)
        gdT = sb("gdT", [32, TOK], BF16)
        memKT = sb("memKT", [128, 4, MEM], BF16)
        memV = sb("memV", [128, 2, 512], BF16)
        lfp = sb("lfp", [128, NT, 8], F32)
        SUF = sb("SUF", [128, NT, 8], F32)
        SUFo = sb("SUFo", [128, NT, 8], F32)
        Tn = sb("Tn", [128, NG, 8], F32)
        PREn = sb("PREn", [128, NG, 8], F32)
        bias_all = sb("bias_all", [128, NG, 32, 8], F32)
        Sst = sb("Sst", [128, 4, 256], F32)
        Sbf = sb("Sbf", [128, 4, 256], BF16)
        wupb = sb("wupb", [32, 512], BF16)
        gng_bc = sb("gng_bc", [128, 256], F32)
        bf_bc = sb("bf_bc", [128, 8], F32)
        small = sb("small", [128, 64], F32)

        ARENA_COLS = 12800
        arena_h = es.enter_context(nc.sbuf_tensor("arena", [128, ARENA_COLS], F32))

        class Arena:
            def __init__(self):
                self.off = 0
                self.gen = 0

            def reset(self):
                self.off = 0
                self.gen += 1

            def alloc(self, name, shape, dt, parts=128):
                n = int(np.prod(shape[1:]))
                ncol = (n * (2 if dt == BF16 else 4) + 3) // 4
                ncol = (ncol + 7) // 8 * 8
                assert self.off + ncol <= ARENA_COLS, (name, self.off, ncol)
                ap = arena_h[0:shape[0], self.off:self.off + ncol]
                self.off += ncol
                if dt == BF16:
                    ap = ap.bitcast(BF16)
                ap = ap[:, 0:n]
                if len(shape) == 3:
                    ap = ap.rearrange("p (a b) -> p a b", a=shape[1])
                elif len(shape) == 4:
                    ap = ap.rearrange("p (a b c) -> p a b c", a=shape[1], b=shape[2])
                return V(ap, f"{name}_{self.gen}")

        ar = Arena()

        psb = []
        for i in range(8):
            h = es.enter_context(nc.psum_tensor(f"psb{i}", [128, 512], F32))
            psb.append(V(h, f"psb{i}"))
        ps_rr = [0]

        def nextps():
            i = ps_rr[0]
            ps_rr[0] = (i + 1) % 8
            return psb[i]

        def psbf(ps):
            return ps.ap[:, :].bitcast(BF16)

        def A(eng, f, r=(), w=()):
            return p.op(eng, f, reads=r, writes=w)

        def mm(out, lhsT, rhs, start, stop, r, w):
            A("pe", lambda e, out=out, lhsT=lhsT, rhs=rhs, start=start, stop=stop:
              e.matmul(out, lhsT=lhsT, rhs=rhs, start=start, stop=stop), r, w)

        evac_rr = [0]

        def evac(out, in_, r, w, func=None, scale=1.0, eng=None):
            if func is None and eng is None:
                evac_rr[0] ^= 1
                eng = "act" if evac_rr[0] else "dve"
            if func is not None or eng == "act":
                f = func if func is not None else AF.Copy
                A("act", lambda e, out=out, in_=in_, f=f, scale=scale: e.activation(out=out, in_=in_, func=f, scale=scale), r, w)
            else:
                if scale == 1.0:
                    A("dve", lambda e, out=out, in_=in_: e.tensor_copy(out=out, in_=in_), r, w)
                else:
                    A("dve", lambda e, out=out, in_=in_, scale=scale: e.tensor_scalar(out=out, in0=in_, scalar1=scale, scalar2=None, op0=ALU.mult), r, w)

        def fill_tri(t, val, kind):
            A("pool", lambda e: e.memset(t[:], val), (), [t])
            if kind == "gt":
                kw = dict(pattern=[[-1, 128]], compare_op=ALU.is_gt, base=0, channel_multiplier=1)
            elif kind == "le":
                kw = dict(pattern=[[1, 128]], compare_op=ALU.is_gt, base=1, channel_multiplier=-1)
            else:
                kw = dict(pattern=[[-1, 128]], compare_op=ALU.is_equal, base=0, channel_multiplier=1)
            A("pool", lambda e: e.affine_select(out=t[:], in_=t[:], fill=0.0, **kw), [t], [t])

        fill_tri(maskGf, 1.0, "le")
        fill_tri(Lrev, -1.0 / 16.0, "gt")
        fill_tri(Ucum, -1.0 / 16.0, "le")
        fill_tri(SU, 1.0, "gt")
        fill_tri(onesf, 1.0, "eq")
        A("dve", lambda e: e.tensor_copy(out=ident[:], in_=onesf[:]), [onesf], [ident])
        A("dve", lambda e: e.tensor_copy(out=maskG[:], in_=maskGf[:]), [maskGf], [maskG])
        A("pool", lambda e: e.memset(onesf[:], 1.0), (), [onesf])
        A("pool", lambda e: e.memset(onesb[:], 1.0), (), [onesb])
        A("pool", lambda e: e.memset(neg16[:], -1.0 / 16.0), (), [neg16])
        A("pool", lambda e: e.memset(c_eps[:], EPS), (), [c_eps])
        A("pool", lambda e: e.memset(c_one[:], 1.0), (), [c_one])
        A("pool", lambda e: e.memset(gdT[:], 1.0), (), [gdT])
        p.dma("sp", flags[:], flags_d[:, :], (), [flags])
        p.barrier()

        def norm_transpose(src_fn, ntiles, gain_ap, dst, dst_is_hT=True, extra=None):
            ar.reset()
            xin = [ar.alloc(f"xin{i}", [128, D], F32) for i in range(2)]
            xs = [ar.alloc(f"xs{i}", [128, D], BF16) for i in range(2)]
            gbc = ar.alloc("gbc", [128, D], F32)
            junk = ar.alloc("junk", [128, D], BF16)
            st = [ar.alloc(f"nst{i}", [128, 4], F32) for i in range(2)]
            p.dma("sp", gbc[:], gain_ap.partition_broadcast(128), (), [gbc])
            def stats(i):
                xt, xb, s = xin[i % 2], xs[i % 2], st[i % 2]
                p.dma("sp", xt[:], src_fn(i), (), [xt])
                A("act", lambda e, xt=xt, s=s: e.activation(out=junk[:], in_=xt[:], func=AF.Square, accum_out=s[:, 0:1]), [xt], [junk, s])
                A("act", lambda e, s=s: e.activation(out=s[:, 1:2], in_=s[:, 0:1], func=AF.Sqrt, scale=1.0 / D, bias=c_eps[:]), [s, c_eps], [s])
                A("dve", lambda e, s=s: e.reciprocal(out=s[:, 2:3], in_=s[:, 1:2]), [s], [s])
                A("dve", lambda e, xt=xt, xb=xb, s=s: e.scalar_tensor_tensor(out=xb[:], in0=xt[:], scalar=s[:, 2:3], in1=gbc[:],
                                                                              op0=ALU.mult, op1=ALU.mult), [xt, s, gbc], [xb])

            def trans(i):
                xb = xs[i % 2]
                for g in range(4):
                    ps = nextps()
                    pv = psbf(ps)
                    for j in range(4):
                        kt = g * 4 + j
                        A("pe", lambda e, pv=pv, j=j, xb=xb, kt=kt: e.transpose(pv[:, j * 128:(j + 1) * 128], xb[:, kt * 128:(kt + 1) * 128], ident[:]),
                          [xb, ident], [ps])
                    evac(dst[:, g * 4:(g + 1) * 4, i * 128:(i + 1) * 128], pv[:, 0:512].rearrange("p (a b) -> p a b", a=4), [ps], [dst])

            stats(0)
            for i in range(ntiles):
                if i + 1 < ntiles:
                    stats(i + 1)
                trans(i)
                if extra is not None:
                    extra(i)
            p.barrier()

        def load_w(wsrc, c0, ncols, wb_t, c_dst=0):
            src = wsrc[:, c0:c0 + ncols].rearrange("(kt p) c -> p kt c", p=128)
            for hf in range(2):
                p.dma("pool", wb_t[:, hf * 8:(hf + 1) * 8, c_dst:c_dst + ncols], src[:, hf * 8:(hf + 1) * 8, :], (), [wb_t])

        def proj_F(wb_t, m0, msz, act_T, ntok, consume):
            for n in range(ntok // 512 if ntok >= 512 else 1):
                nn = min(512, ntok)
                ps = nextps()
                for kt in range(KT):
                    mm(ps[0:msz, 0:nn], wb_t[:, kt, m0:m0 + msz], act_T[:, kt, n * 512:n * 512 + nn], kt == 0, kt == KT - 1, [wb_t, act_T], [ps])
                consume(n, ps)

        def proj_T(wb_t, c0, ncols, act_T, ntiles, consume):
            for i in range(ntiles):
                ps = nextps()
                for kt in range(KT):
                    mm(ps[:, 0:ncols], act_T[:, kt, i * 128:(i + 1) * 128], wb_t[:, kt, c0:c0 + ncols], kt == 0, kt == KT - 1, [wb_t, act_T], [ps])
                consume(i, ps)

        for l in range(nlayers):
            x_src = x_d if l == 0 else xres[(l - 1) % 2]
            x_dst = xres[l % 2]
            win_l, wbr_l, wout_l, wkv_l = win_d[l], wbr_d[l], wout_d[l], wkv_d[l]

            conv_list = [(f, t, hf) for f in range(16) for t in range(4) for hf in range(2)]

            def conv_some(k, win_l=win_l, wbr_l=wbr_l, conv_list=conv_list):
                return
                for _ in range(k):
                    if not conv_list:
                        return
                    f, t, hf = conv_list.pop(0)
                    c0 = f * 128
                    srcw = wbr_l[:, c0:c0 + 128] if t == 0 else win_l[:, O_MERGE + (t - 1) * D + c0:O_MERGE + (t - 1) * D + c0 + 128]
                    src = srcw.rearrange("(kt p) c -> p kt c", p=128)[:, hf * 8:(hf + 1) * 8, :]
                    dst = wconv_d[f].rearrange("p (kt c) -> p kt c", c=512)[:, hf * 8:(hf + 1) * 8, t * 128:(t + 1) * 128]
                    p.dma("pool", dst, src, (), ())

            p.dma("pool", wupb[0:16, :], wup_d[l], (), [wupb])
            p.dma("pool", wupb[16:17, :], bgk_d[l:l + 1, :], (), [wupb])
            p.dma("sp", gng_bc[:], gng_d[l:l + 1, :].partition_broadcast(128), (), [gng_bc])
            p.dma("sp", bf_bc[:], bf_d[l:l + 1, :].partition_broadcast(128), (), [bf_bc])

            norm_transpose(lambda i: mem_d[i * 128:(i + 1) * 128, :], 2, mng_d[l:l + 1, :], hT)
            ar.reset()
            wb2 = [ar.alloc(f"wb{i}", [128, KT, 512], BF16) for i in range(2)]
            load_w(wkv_l, 0, 512, wb2[0])
            load_w(wkv_l, 512, 512, wb2[1])
            for h in range(4):
                def cons(n, ps, h=h):
                    evac(memKT[:, h, :], ps[:, 0:MEM], [ps], [memKT])
                proj_F(wb2[0], h * 128, 128, hT, MEM, cons)

            def cons(i, ps):
                evac(memV[:, i, :], ps[:, :], [ps], [memV])
            proj_T(wb2[1], 0, 512, hT, 2, cons)
            p.barrier()

            norm_transpose(lambda i: x_src[i * 128:(i + 1) * 128, :], NT, ng_d[l:l + 1, :], hT, extra=lambda i: conv_some(1))
            if dbg_hT is not None and l == 0:
                for kt in range(KT):
                    p.dma("sp", dbg_hT[kt], hT[:, kt, :], [hT], ())
                p.barrier()

            ar.reset()
            wb2 = [ar.alloc(f"wb{i}", [128, KT, 512], BF16) for i in range(2)]
            stg = [ar.alloc(f"stg{i}", [128, 4, 512], BF16) for i in range(3)]
            stv = [ar.alloc(f"stv{i}", [128, 8, 128], BF16) for i in range(2)]
            lft = ar.alloc("lft", [128, 16], F32)
            for t in stv:
                A("pool", lambda e, t=t: e.memset(t[:], 1.0), (), [t])
            groups = [("gq", O_GQ), ("gk", O_GK), ("gv0", O_GV), ("gv1", O_GV + 512), ("small", None),
                      ("fq", O_FQ), ("fk", O_FK), ("fv", O_FV)]

            def issue_load(gi):
                name, c0 = groups[gi]
                wt = wb2[gi % 2]
                if name == "small":
                    load_w(win_l, O_GD, 16, wt, 0)
                    load_w(win_l, O_FL, 8, wt, 16)
                else:
                    load_w(win_l, c0, 512, wt)
            issue_load(0)
            sg = [0]

            def F_to_dram(wt, dst3, scale=1.0):
                for n in range(NG):
                    s = stg[sg[0] % 3]
                    sg[0] += 1
                    for m in range(4):
                        ps = nextps()
                        for kt in range(KT):
                            mm(ps[:, :], wt[:, kt, m * 128:(m + 1) * 128], hT[:, kt, n * 512:(n + 1) * 512], kt == 0, kt == KT - 1, [wt, hT], [ps])
                        evac(s[:, m, :], ps[:, :], [ps], [s], scale=scale)
                    p.dma("sp", dst3[:, :, n * 512:(n + 1) * 512].rearrange("m p t -> p m t"), s[:], [s], ())

            def T_to_dram(wt, dst2, c_dst):
                for i4 in range(4):
                    s = stg[sg[0] % 3]
                    sg[0] += 1
                    for j in range(4):
                        i = i4 * 4 + j
                        ps = nextps()
                        for kt in range(KT):
                            mm(ps[:, :], hT[:, kt, i * 128:(i + 1) * 128], wt[:, kt, 0:512], kt == 0, kt == KT - 1, [wt, hT], [ps])
                        evac(s[:, j, :], ps[:, :], [ps], [s])
                    p.dma("sp", dst2[i4 * 512:(i4 + 1) * 512, c_dst:c_dst + 512].rearrange("(j p) c -> p j c", p=128), s[:], [s], ())

            for gi, (name, c0) in enumerate(groups):
                if gi + 1 < len(groups):
                    issue_load(gi + 1)
                wt = wb2[gi % 2]
                if name == "gq":
                    F_to_dram(wt, gqT_d, scale=128.0 ** -0.5)
                elif name == "gk":
                    F_to_dram(wt, gkT_d)
                    T_to_dram(wt, ktok_d, 0)
                elif name == "gv0":
                    T_to_dram(wt, vtok_d, 0)
                elif name == "gv1":
                    T_to_dram(wt, vtok_d, 512)
                elif name == "fq":
                    F_to_dram(wt, fqT_d)
                elif name == "fk":
                    F_to_dram(wt, fkT_d.rearrange("(m p) t -> m p t", p=128))
                elif name == "fv":
                    for i in range(NT):
                        s = stv[i % 2]
                        ps = nextps()
                        for kt in range(KT):
                            mm(ps[:, :], hT[:, kt, i * 128:(i + 1) * 128], wt[:, kt, 0:512], kt == 0, kt == KT - 1, [wt, hT], [ps])
                        pv = ps[:, :].rearrange("p (a b c) -> p a b c", a=4, b=2)
                        sv = s[:, :, :].rearrange("p (a b) c -> p a b c", b=2)
                        A("dve", lambda e, sv=sv, pv=pv: e.tensor_copy(out=sv[:, :, 0, 0:64], in_=pv[:, :, 0, :]), [ps], [s])
                        A("dve", lambda e, sv=sv, pv=pv: e.tensor_copy(out=sv[:, :, 1, 64:128], in_=pv[:, :, 1, :]), [ps], [s])
                        p.dma("sp", fva_d[i // 8][(i % 8) * 128:(i % 8 + 1) * 128, :], s[:, :, :].rearrange("p a b -> p (a b)"), [s], [exB["fva"]])
                elif name == "small":
                    def cons(n, ps):
                        evac(gdT[0:16, n * 512:(n + 1) * 512], ps[0:16, :], [ps], [gdT])
                    proj_F(wt, 0, 16, hT, TOK, cons)
                    for i in range(NT):
                        ps = nextps()
                        for kt in range(KT):
                            mm(ps[:, 0:8], hT[:, kt, i * 128:(i + 1) * 128], wt[:, kt, 16:24], kt == 0, kt == KT - 1, [wt, hT], [ps])
                        A("dve", lambda e, ps=ps: e.tensor_tensor(out=lft[:, 0:8], in0=ps[:, 0:8], in1=bf_bc[:], op=ALU.add), [ps, bf_bc], [lft])
                        A("act", lambda e: e.activation(out=lft[:, 8:16], in_=lft[:, 0:8], func=AF.Exp, scale=-1.0), [lft], [lft])
                        A("act", lambda e, i=i: e.activation(out=lfp[:, i, :], in_=lft[:, 8:16], func=AF.Ln, bias=c_one[:], scale=1.0), [lft, c_one], [lfp])
            p.barrier()
            if stop_after == "A2":
                break

            ar.reset()
            kt_t = [ar.alloc(f"ktok{i}", [128, 512], BF16) for i in range(2)]
            vt_t = [ar.alloc(f"vtok{i}", [128, 1024], BF16) for i in range(2)]
            t1 = [ar.alloc(f"t1{i}", [128, 512], F32) for i in range(2)]
            gp_t = [ar.alloc(f"gp{i}", [128, 512], F32) for i in range(2)]
            eR = [ar.alloc(f"eR{i}", [128, 512], F32) for i in range(2)]
            kd_t = [ar.alloc(f"kd{i}", [128, 512], BF16) for i in range(2)]
            dec = [ar.alloc(f"dec{i}", [128, 4], F32) for i in range(2)]
            A("pool", lambda e: e.memset(Sst[:], 0.0), (), [Sst])
            for i in range(NT):
                k_, v_, t_, g_, r_, d_, dc = kt_t[i % 2], vt_t[i % 2], t1[i % 2], gp_t[i % 2], eR[i % 2], kd_t[i % 2], dec[i % 2]
                p.dma("sp", k_[:], ktok_d[i * 128:(i + 1) * 128, :], (), [k_])
                p.dma("sp", v_[:], vtok_d[i * 128:(i + 1) * 128, :], (), [v_])
                ps = nextps()
                mm(ps[:, :], gdT[0:17, i * 128:(i + 1) * 128], wupb[0:17, :], True, True, [gdT, wupb], [ps])
                A("act", lambda e, t_=t_, ps=ps: e.activation(out=t_[:], in_=ps[:, :], func=AF.Exp, scale=-1.0), [ps], [t_])
                A("act", lambda e, t_=t_, g_=g_: e.activation(out=g_[:], in_=t_[:], func=AF.Ln, bias=c_one[:], scale=1.0), [t_, c_one], [g_])
                p.dma("pool", gp_d[i * 128:(i + 1) * 128, :], g_[:], [g_], ())
                ps2 = nextps()
                mm(ps2[:, :], Lrev[:], g_[:], True, True, [Lrev, g_], [ps2])
                A("act", lambda e, r_=r_, ps2=ps2: e.activation(out=r_[:], in_=ps2[:, :], func=AF.Exp), [ps2], [r_])
                A("dve", lambda e, d_=d_, k_=k_, r_=r_: e.tensor_tensor(out=d_[:], in0=k_[:], in1=r_[:], op=ALU.mult), [k_, r_], [d_])
                p.dma("pool", kdec_d[i * 128:(i + 1) * 128, :], d_[:], [d_], ())
                conv_some(1)
                ps3 = nextps()
                for h in range(4):
                    mm(ps3[:, h:h + 1], g_[:, h * 128:(h + 1) * 128], neg16[:], True, True, [g_, neg16], [ps3])
                A("act", lambda e, dc=dc, ps3=ps3: e.activation(out=dc[:], in_=ps3[:, 0:4], func=AF.Exp), [ps3], [dc])
                for hp in range(2):
                    ps4 = nextps()
                    for hh in range(2):
                        h = hp * 2 + hh
                        mm(ps4[:, hh * 256:(hh + 1) * 256], d_[:, h * 128:(h + 1) * 128], v_[:, h * 256:(h + 1) * 256], True, True, [d_, v_], [ps4])
                    for hh in range(2):
                        h = hp * 2 + hh
                        A("dve", lambda e, h=h, hh=hh, dc=dc, ps4=ps4: e.scalar_tensor_tensor(
                            out=Sst[:, h, :], in0=Sst[:, h, :], scalar=dc[:, h:h + 1], in1=ps4[:, hh * 256:(hh + 1) * 256],
                            op0=ALU.mult, op1=ALU.add), [Sst, dc, ps4], [Sst])
            p.dma("sp", cinU_d[:, :], Sst[:, :, :].rearrange("p a b -> p (a b)"), [Sst], [exB["cinU"]])

            rs_sb = ar.alloc("rs_sb", [128, NT, 8], F32)
            tot_sb = ar.alloc("tot_sb", [128, NT, 8], F32)
            acc = ar.alloc("acc", [128, 8], F32)
            psr, pst = nextps(), nextps()
            for j in range(NT):
                mm(psr[:, j * 8:(j + 1) * 8], SU[:], lfp[:, j, :], True, True, [SU, lfp], [psr])
                mm(pst[:, j * 8:(j + 1) * 8], onesf[:], lfp[:, j, :], True, True, [onesf, lfp], [pst])
            evac(rs_sb[:, :, :].rearrange("p a b -> p (a b)"), psr[:, 0:128], [psr], [rs_sb], eng="dve")
            evac(tot_sb[:, :, :].rearrange("p a b -> p (a b)"), pst[:, 0:128], [pst], [tot_sb], eng="dve")
            A("dve", lambda e: e.memset(acc[:], 0.0), (), [acc])
            A("dve", lambda e: e.memset(Tn[:, 3, :], 0.0), (), [Tn])
            for j in range(NT - 1, -1, -1):
                A("dve", lambda e, j=j: e.tensor_tensor(out=SUF[:, j, :], in0=rs_sb[:, j, :], in1=acc[:], op=ALU.add), [rs_sb, acc], [SUF])
                A("dve", lambda e, j=j: e.tensor_tensor(out=acc[:], in0=acc[:], in1=tot_sb[:, j, :], op=ALU.add), [acc, tot_sb], [acc])
                if j % 4 == 0 and j > 0:
                    n = j // 4 - 1
                    A("dve", lambda e, n=n: e.tensor_copy(out=Tn[:, n, :], in_=acc[:]), [acc], [Tn])
            for n in range(NG):
                A("dve", lambda e, n=n: e.tensor_tensor(out=PREn[:, n, :], in0=acc[:], in1=Tn[:, n, :], op=ALU.subtract), [acc, Tn], [PREn])
            p.dma("sp", cinS_d[:, :], SUF[:, :, :].rearrange("p a b -> p (a b)"), [SUF], [exB["cinS"]])
            p.barrier()

            ar.reset()
            wb2 = [ar.alloc(f"wbg{i}", [128, KT, 512], BF16) for i in range(2)]
            ggroups = [(O_GG, 0), (O_GG + 512, 4), (O_FG, 8), (O_MG, 12)]
            load_w(win_l, ggroups[0][0], 512, wb2[0])
            load_w(win_l, ggroups[1][0], 512, wb2[1])
            rg = [[2 * i, 2 * i + 1] for i in range(ncores // 2)]
            for src, dst, sn, dn in ((fkT_d, fkT_all, "fkT", "fkT_all"), (fva_d[0], fva_all[0], "fva", "fva_all"), (fva_d[1], fva_all[1], "fva", "fva_all"),
                                     (cinU_d, coutU_d, "cinU", "coutU"), (cinS_d, coutS_d, "cinS", "coutS")):
                if nocc:
                    continue
                p.cc(lambda e, src=src, dst=dst: e.collective_compute("AllGather", ALU.bypass, replica_groups=rg,
                                                                      ins=[src[:, :]], outs=[dst[:, :]]), [exB[sn]], [exB[dn]])
            for gi, (c0, ch0) in enumerate(ggroups):
                if gi >= 1 and gi + 1 < len(ggroups):
                    load_w(win_l, ggroups[gi + 1][0], 512, wb2[(gi + 1) % 2])
                for c in range(4):
                    def cons(n, ps, ch=ch0 + c):
                        evac(omix[:, ch, n * 512:(n + 1) * 512], ps[:, :], [ps], [omix], func=AF.Silu)
                    proj_F(wb2[gi % 2], c * 128, 128, hT, TOK, cons)
            p.barrier()
            if stop_after == "A3":
                break

            ar.reset()
            qT_t = [ar.alloc(f"qT{i}", [128, 4, 128], BF16) for i in range(2)]
            kT_t = [ar.alloc(f"kT{i}", [128, 4, 128], BF16) for i in range(2)]
            kd_t = [ar.alloc(f"kd{i}", [128, 512], BF16) for i in range(2)]
            vt_t = [ar.alloc(f"vtok{i}", [128, 1024], BF16) for i in range(2)]
            gp_t = [ar.alloc(f"gp{i}", [128, 512], F32) for i in range(2)]
            NS = 3
            E_t = [[ar.alloc(f"E{k}_{i}", [128, 128], F32) for i in range(4)] for k in range(NS)]
            Ei_t = [[ar.alloc(f"Ei{k}_{i}", [128, 128], F32) for i in range(4)] for k in range(NS)]
            qd_t = [[ar.alloc(f"qd{k}_{i}", [128, 128], BF16) for i in range(4)] for k in range(NS)]
            kdd_t = [[ar.alloc(f"kdd{k}_{i}", [128, 128], BF16) for i in range(4)] for k in range(NS)]
            at_t = [[ar.alloc(f"at{k}_{i}", [128, 128], BF16) for i in range(4)] for k in range(NS)]
            on_t = [ar.alloc(f"on{i}", [128, 1024], BF16) for i in range(2)]
            nst = [ar.alloc(f"gst{i}", [128, 16], F32) for i in range(2)]
            junkg = ar.alloc("junkg", [128, 256], BF16)
            Uin = ar.alloc("Uin", [128, 1024], F32)
            p.dma("sp", Uin[:], coutU_d[0:128, :], [exB["coutU"]], [Uin])
            A("dve", lambda e: e.tensor_scalar(out=Sst[:, :, :].rearrange("p a b -> p (a b)"), in0=Uin[:], scalar1=flags[:, 1:2], scalar2=None, op0=ALU.mult),
              [Uin, flags], [Sst])
            A("act", lambda e: e.activation(out=Sbf[:, :, :].rearrange("p a b -> p (a b)"), in_=Sst[:, :, :].rearrange("p a b -> p (a b)"), func=AF.Copy), [Sst], [Sbf])
            g_rr = [0]

            def gps():
                g_rr[0] = (g_rr[0] + 1) % 5
                return psb[3 + g_rr[0]]

            def gla_A1(i):
                q_, k_, g_ = qT_t[i % 2], kT_t[i % 2], gp_t[i % 2]
                tsl = slice(i * 128, (i + 1) * 128)
                p.dma("sp", q_[:], gqT_d[:, :, tsl].rearrange("m p t -> p m t"), (), [q_])
                p.dma("sp", k_[:], gkT_d[:, :, tsl].rearrange("m p t -> p m t"), (), [k_])
                p.dma("sp", g_[:], gp_d[tsl, :], (), [g_])
                pbs = []
                for h in range(4):
                    psb_ = gps()
                    mm(psb_[:, 0:128], g_[:, h * 128:(h + 1) * 128], Ucum[:], True, True, [g_, Ucum], [psb_])
                    E, Ei = E_t[i % NS][h], Ei_t[i % NS][h]
                    A("act", lambda e, E=E, psb_=psb_: e.activation(out=E[:], in_=psb_[:, 0:128], func=AF.Exp), [psb_], [E])
                    A("act", lambda e, Ei=Ei, psb_=psb_: e.activation(out=Ei[:], in_=psb_[:, 0:128], func=AF.Exp, scale=-1.0), [psb_], [Ei])
                for h in range(4):
                    E, Ei, qd, kdd = E_t[i % NS][h], Ei_t[i % NS][h], qd_t[i % NS][h], kdd_t[i % NS][h]
                    A("dve", lambda e, qd=qd, q_=q_, h=h, E=E: e.tensor_tensor(out=qd[:], in0=q_[:, h, :], in1=E[:], op=ALU.mult), [q_, E], [qd])
                    A("dve", lambda e, kdd=kdd, k_=k_, h=h, Ei=Ei: e.tensor_tensor(out=kdd[:], in0=k_[:, h, :], in1=Ei[:], op=ALU.mult), [k_, Ei], [kdd])

            def gla_A2(i):
                d_, v_ = kd_t[i % 2], vt_t[i % 2]
                tsl = slice(i * 128, (i + 1) * 128)
                p.dma("sp", d_[:], kdec_d[tsl, :], (), [d_])
                p.dma("sp", v_[:], vtok_d[tsl, :], (), [v_])
                for h in range(4):
                    qd, kdd, at = qd_t[i % NS][h], kdd_t[i % NS][h], at_t[i % NS][h]
                    psa = gps()
                    mm(psa[:, 0:128], kdd[:], qd[:], True, True, [kdd, qd], [psa])
                    A("dve", lambda e, at=at, psa=psa: e.tensor_tensor(out=at[:], in0=psa[:, 0:128], in1=maskGf[:], op=ALU.mult), [psa, maskGf], [at])

            def gla_B(i):
                d_, v_ = kd_t[i % 2], vt_t[i % 2]
                on, s = on_t[i % 2], nst[i % 2]
                tsl = slice(i * 128, (i + 1) * 128)
                pso = [psb[0], psb[1]]
                for h in range(4):
                    E, qd, at = E_t[i % NS][h], qd_t[i % NS][h], at_t[i % NS][h]
                    hs = slice(h * 128, (h + 1) * 128)
                    vs = slice(h * 256, (h + 1) * 256)
                    po = pso[h // 2]
                    pos = slice((h % 2) * 256, (h % 2 + 1) * 256)
                    mm(po[:, pos], qd[:], Sbf[:, h, :], True, False, [qd, Sbf], [po])
                    mm(po[:, pos], at[:], v_[:, vs], False, True, [at, v_], [po])
                    pss = gps()
                    mm(pss[:, 0:256], d_[:, hs], v_[:, vs], True, True, [d_, v_], [pss])
                    A("dve", lambda e, h=h, E=E, pss=pss: e.scalar_tensor_tensor(out=Sst[:, h, :], in0=Sst[:, h, :], scalar=E[:, 127:128], in1=pss[:, 0:256],
                                                                              op0=ALU.mult, op1=ALU.add), [Sst, E, pss], [Sst])
                    A("act", lambda e, h=h: e.activation(out=Sbf[:, h, :], in_=Sst[:, h, :], func=AF.Copy), [Sst], [Sbf])
                for h in range(4):
                    po = pso[h // 2]
                    pos = slice((h % 2) * 256, (h % 2 + 1) * 256)
                    A("act", lambda e, po=po, pos=pos, s=s, h=h: e.activation(out=junkg[:], in_=po[:, pos], func=AF.Square, accum_out=s[:, h:h + 1]), [po], [junkg, s])
                A("act", lambda e, s=s: e.activation(out=s[:, 4:8], in_=s[:, 0:4], func=AF.Sqrt, scale=1.0 / 256.0, bias=c_eps[:]), [s, c_eps], [s])
                A("dve", lambda e, s=s: e.reciprocal(out=s[:, 8:12], in_=s[:, 4:8]), [s], [s])
                for h in range(4):
                    po = pso[h // 2]
                    pos = slice((h % 2) * 256, (h % 2 + 1) * 256)
                    A("dve", lambda e, on=on, po=po, pos=pos, s=s, h=h: e.scalar_tensor_tensor(
                        out=on[:, h * 256:(h + 1) * 256], in0=po[:, pos], scalar=s[:, 8 + h:9 + h], in1=gng_bc[:], op0=ALU.mult, op1=ALU.mult),
                      [po, s, gng_bc], [on])
                pt = psb[2]
                ptv = psbf(pt)
                for c in range(8):
                    A("pe", lambda e, ptv=ptv, c=c, on=on: e.transpose(ptv[:, c * 128:(c + 1) * 128], on[:, c * 128:(c + 1) * 128], ident[:]), [on, ident], [pt])
                A("dve", lambda e, ptv=ptv, tsl=tsl: e.tensor_tensor(out=omix[:, 0:8, tsl], in0=ptv[:, :].rearrange("p (a b) -> p a b", a=8),
                                                                      in1=omix[:, 0:8, tsl], op=ALU.mult), [pt, omix], [omix])

            gla_A1(0)
            gla_A1(1)
            gla_A2(0)
            for i in range(NT):
                if i + 2 < NT:
                    gla_A1(i + 2)
                if i + 1 < NT:
                    gla_A2(i + 1)
                gla_B(i)
                conv_some(3)
            p.barrier()
            if stop_after == "B1a":
                break

            ar.reset()
            p.dma("sp", SUFo[:, :, :].rearrange("p a b -> p (a b)"), coutS_d[0:128, :], [exB["coutS"]], [SUFo])
            for n in range(NG):
                A("dve", lambda e, n=n: e.tensor_scalar(out=small[:, 0:8], in0=PREn[:, n, :], scalar1=-1.0, scalar2=flags[:, 0:1], op0=ALU.mult, op1=ALU.add),
                  [PREn, flags], [small])
                for j in range(NT):
                    A("dve", lambda e, n=n, j=j: e.tensor_tensor(out=bias_all[:, n, j, :], in0=small[:, 0:8], in1=SUFo[:, j, :], op=ALU.subtract),
                      [small, SUFo], [bias_all])
                    A("dve", lambda e, n=n, j=j: e.tensor_tensor(out=bias_all[:, n, 16 + j, :], in0=Tn[:, n, :], in1=SUF[:, j, :], op=ALU.subtract),
                      [Tn, SUF], [bias_all])
            QT = [ar.alloc(f"QT{i}", [128, 8, 512], BF16) for i in range(2)]
            for t in QT:
                A("pool", lambda e, t=t: e.memset(t[:], 0.0), (), [t])
            NKV = 5
            KTt = [ar.alloc(f"KTt{i}", [128, 4, 128], BF16) for i in range(NKV)]
            Vt = [ar.alloc(f"Vt{i}", [128, 8, 128], BF16) for i in range(NKV)]
            PT = [ar.alloc(f"PT{i}", [128, 512], BF16) for i in range(4)]
            rc = [ar.alloc(f"rc{i}", [128, 512], F32) for i in range(2)]
            tmpo = [ar.alloc(f"tmpo{i}", [128, 512], F32) for i in range(2)]
            acs = [ar.alloc(f"acs{i}", [128, 512], F32) for i in range(2)]

            kvc = [0]
            ptc = [0]
            LA = 3
            PF = 3
            for n in range(NG):
                qt = QT[n % 2]
                for par in range(2):
                    rsl = slice(par * 64, par * 64 + 64)
                    p.dma("sp", qt.ap[rsl, par::2, :], fqT_d[:, rsl, n * 512:(n + 1) * 512].rearrange("m p t -> p m t"), (), [qt])
                keys = [("o", j) for j in range(NT)] + [("s", j) for j in range(4 * n + 4)]
                nk = len(keys)
                for hb in range(2):
                    accs = [psb[k] for k in range(4)]
                    kv = {}

                    def load_kv(ki):
                        kind, j = keys[ki]
                        kt_ = KTt[kvc[0] % NKV]
                        vt_ = Vt[kvc[0] % NKV]
                        kvc[0] += 1
                        if kind == "o":
                            p.dma("sp", kt_[:], fkT_all[0:512, j * 128:(j + 1) * 128].rearrange("(m p) t -> p m t", p=128), [exB["fkT_all"]], [kt_])
                            p.dma("sp", vt_[:, :, :].rearrange("p a b -> p (a b)"), fva_all[j // 8][(j % 8) * 128:(j % 8 + 1) * 128, :], [exB["fva_all"]], [vt_])
                            kv[ki] = (kt_, vt_, j, 0)
                        else:
                            p.dma("sp", kt_[:], fkT_d[:, j * 128:(j + 1) * 128].rearrange("(m p) t -> p m t", p=128), [exB["fkT"]], [kt_])
                            p.dma("sp", vt_[:, :, :].rearrange("p a b -> p (a b)"), fva_d[j // 8][(j % 8) * 128:(j % 8 + 1) * 128, :], [exB["fva"]], [vt_])
                            kv[ki] = (kt_, vt_, 16 + j, max(0, j - 4 * n) * 128)

                    units = [(ki, hh) for ki in range(nk) for hh in range(4)]
                    ust = {}

                    def front(u):
                        ki, hh = units[u]
                        kind, j = keys[ki]
                        kt_, vt_, bj, q0 = kv[ki]
                        h = hb * 4 + hh
                        rows = slice((h % 2) * 64, (h % 2) * 64 + 64)
                        pr = h // 2
                        pss = psb[4 + (ptc[0] % 4)]
                        pt_ = PT[ptc[0] % 4]
                        ptc[0] += 1
                        mm(pss[:, q0:512], kt_[:, pr, :], qt[:, h, q0:512], True, True, [kt_, qt], [pss])
                        A("act", lambda e, pt_=pt_, pss=pss, q0=q0, n=n, bj=bj, h=h: e.activation(
                            out=pt_[:, q0:512], in_=pss[:, q0:512], func=AF.Exp, scale=0.125, bias=bias_all[:, n, bj, h:h + 1]),
                          [pss, bias_all], [pt_])
                        if kind == "s" and j >= 4 * n:
                            A("dve", lambda e, pt_=pt_, q0=q0: e.tensor_tensor(out=pt_[:, q0:q0 + 128], in0=pt_[:, q0:q0 + 128], in1=maskG[:], op=ALU.mult),
                              [pt_, maskG], [pt_])
                        ust[u] = pt_

                    def back(u):
                        ki, hh = units[u]
                        kt_, vt_, bj, q0 = kv[ki]
                        h = hb * 4 + hh
                        pt_ = ust.pop(u)
                        mm(accs[hh][:, q0:512], vt_[:, h, :], pt_[:, q0:512], ki == 0, ki == nk - 1, [vt_, pt_], [accs[hh]])

                    for ki in range(min(PF, nk)):
                        load_kv(ki)
                    for idx in range(len(units) + LA):
                        if idx < len(units):
                            ki, hh = units[idx]
                            if hh == 0 and ki + PF < nk:
                                load_kv(ki + PF)
                            front(idx)
                        if idx >= LA:
                            back(idx - LA)
                    for hh in range(4):
                        ac = acs[hh % 2]
                        A("dve", lambda e, ac=ac, acc_=accs[hh]: e.tensor_copy(out=ac[:], in_=acc_[:, :]), [accs[hh]], [ac])
                        h = hb * 4 + hh
                        orow = slice((h % 2) * 64, (h % 2) * 64 + 64)
                        srow = slice((1 - h % 2) * 64, (1 - h % 2) * 64 + 64)
                        rc_, tm_ = rc[hh % 2], tmpo[hh % 2]
                        A("dve", lambda e, rc_=rc_, ac=ac, orow=orow, srow=srow: e.reciprocal(out=rc_[orow, :], in_=ac[srow, :]), [ac], [rc_])
                        A("dve", lambda e, tm_=tm_, rc_=rc_, ac=ac, orow=orow: e.tensor_tensor(out=tm_[orow, :], in0=ac[orow, :], in1=rc_[orow, :], op=ALU.mult),
                          [ac, rc_], [tm_])
                        A("dve", lambda e, tm_=tm_, orow=orow, h=h, n=n: e.tensor_tensor(out=omix[orow, 8 + h // 2, n * 512:(n + 1) * 512], in0=tm_[orow, :],
                                                                                      in1=omix[orow, 8 + h // 2, n * 512:(n + 1) * 512], op=ALU.mult),
                          [tm_, omix], [omix])
            p.barrier()
            if stop_after == "B1b":
                break

            ar.reset()
            mqT = ar.alloc("mqT", [128, 4, TOK], BF16)
            wb2 = [ar.alloc(f"wb{i}", [128, KT, 512], BF16) for i in range(2)]
            load_w(win_l, O_MQ, 512, wb2[1])
            for c in range(4):
                def cons2(n, ps, c=c):
                    evac(mqT[:, c, n * 512:(n + 1) * 512], ps[:, :], [ps], [mqT])
                proj_F(wb2[1], c * 128, 128, hT, TOK, cons2)
            p.barrier()
            ar.off -= 2 * (KT * 512 // 2)
            PTm = [ar.alloc(f"PTm{i}", [128, 512], BF16) for i in range(4)]
            rc = [ar.alloc(f"rc{i}", [128, 512], F32) for i in range(2)]
            tmpo = [ar.alloc(f"tmpo{i}", [128, 512], F32) for i in range(2)]
            pc = [0]
            munits = [(n, h) for n in range(NG) for h in range(4)]
            mst = {}

            def mfront(u):
                n, h = munits[u]
                pts = []
                for mt in range(2):
                    pss = nextps()
                    pt_ = PTm[pc[0] % 4]
                    pc[0] += 1
                    mm(pss[:, :], memKT[:, h, mt * 128:(mt + 1) * 128], mqT[:, h, n * 512:(n + 1) * 512], True, True, [memKT, mqT], [pss])
                    A("act", lambda e, pt_=pt_, pss=pss: e.activation(out=pt_[:], in_=pss[:, :], func=AF.Exp, scale=128.0 ** -0.5), [pss], [pt_])
                    pts.append(pt_)
                mst[u] = pts

            def mback(u):
                n, h = munits[u]
                pts = mst.pop(u)
                pso_, psm = nextps(), nextps()
                for mt in range(2):
                    mm(pso_[:, :], memV[:, mt, h * 128:(h + 1) * 128], pts[mt][:], mt == 0, mt == 1, [memV, pts[mt]], [pso_])
                for mt in range(2):
                    mm(psm[:, :], onesb[:], pts[mt][:], mt == 0, mt == 1, [onesb, pts[mt]], [psm])
                rc_, tm_ = rc[h % 2], tmpo[h % 2]
                A("act", lambda e, rc_=rc_, psm=psm: e.activation(out=rc_[:], in_=psm[:, :], func=AF.Ln), [psm], [rc_])
                A("act", lambda e, rc_=rc_: e.activation(out=rc_[:], in_=rc_[:], func=AF.Exp, scale=-1.0), [rc_], [rc_])
                A("dve", lambda e, tm_=tm_, rc_=rc_, pso_=pso_: e.tensor_tensor(out=tm_[:], in0=pso_[:, :], in1=rc_[:], op=ALU.mult), [pso_, rc_], [tm_])
                A("dve", lambda e, tm_=tm_, h=h, n=n: e.tensor_tensor(out=omix[:, 12 + h, n * 512:(n + 1) * 512], in0=tm_[:],
                                                                       in1=omix[:, 12 + h, n * 512:(n + 1) * 512], op=ALU.mult), [tm_, omix], [omix])

            mfront(0)
            for u in range(len(munits)):
                if u + 1 < len(munits):
                    mfront(u + 1)
                mback(u)
            p.barrier()
            if dbg_omix is not None and l == 0:
                for c in range(16):
                    p.dma("sp", dbg_omix[c], omix[:, c, :], [omix], ())
                p.barrier()
            if stop_after == "B1c":
                break

            conv_some(1000)
            p.barrier()
            ar.reset()
            wbr_t = [ar.alloc(f"wbr{i}", [128, KT, 128], BF16) for i in range(2)]
            wmg_t = [ar.alloc(f"wmg{i}", [128, KT, 384], BF16) for i in range(2)]
            G_t = [ar.alloc(f"G{i}", [128, 512], F32) for i in range(4)]
            y_t = [ar.alloc(f"y{i}", [128, 512], F32) for i in range(2)]
            t_t = [ar.alloc(f"t{i}", [128, 512], F32) for i in range(2)]
            yb_t = [ar.alloc(f"yb{i}", [128, 512], BF16) for i in range(2)]
            branches = [(0, 8), (8, 12), (12, 16)]

            def load_f(f):
                load_w(wbr_l, f * 128, 128, wbr_t[f % 2])
                for br in range(3):
                    load_w(win_l, O_MERGE + br * D + f * 128, 128, wmg_t[f % 2], br * 128)
            load_f(0)
            gc = [0]
            for f in range(16):
                if f + 1 < 16:
                    load_f(f + 1)
                wr, wm = wbr_t[f % 2], wmg_t[f % 2]
                for n in range(NG):
                    ns = slice(n * 512, (n + 1) * 512)
                    Gs = []
                    for br in range(3):
                        ps = nextps()
                        for kt in range(KT):
                            mm(ps[:, :], wm[:, kt, br * 128:(br + 1) * 128], hT[:, kt, ns], kt == 0, kt == KT - 1, [wm, hT], [ps])
                        G = G_t[gc[0] % 4]
                        gc[0] += 1
                        A("act", lambda e, G=G, ps=ps: e.activation(out=G[:], in_=ps[:, :], func=AF.Sigmoid), [ps], [G])
                        Gs.append(G)
                    y, t, yb = y_t[n % 2], t_t[n % 2], yb_t[n % 2]
                    for br, (k0, k1) in enumerate(branches):
                        ps = nextps()
                        for kc in range(k0, k1):
                            mm(ps[:, :], wr[:, kc, :], omix[:, kc, ns], kc == k0, kc == k1 - 1, [wr, omix], [ps])
                        if br == 0:
                            A("dve", lambda e, y=y, ps=ps, G=Gs[0]: e.tensor_tensor(out=y[:], in0=ps[:, :], in1=G[:], op=ALU.mult), [ps, Gs[0]], [y])
                        else:
                            A("dve", lambda e, t=t, ps=ps, G=Gs[br]: e.tensor_tensor(out=t[:], in0=ps[:, :], in1=G[:], op=ALU.mult), [ps, Gs[br]], [t])
                            if br == 1:
                                A("dve", lambda e, y=y, t=t: e.tensor_tensor(out=y[:], in0=y[:], in1=t[:], op=ALU.add), [y, t], [y])
                            else:
                                A("dve", lambda e, y=y, t=t, yb=yb: e.tensor_tensor(out=yb[:], in0=y[:], in1=t[:], op=ALU.add), [y, t], [yb])
                    p.dma("sp", yT_d[f, :, ns], yb[:], [yb], ())
            p.barrier()

            ar.reset()
            wb2 = [ar.alloc(f"wb{i}", [128, KT, 512], BF16) for i in range(2)]
            xt_t = [ar.alloc(f"xt{i}", [128, 512], F32) for i in range(3)]
            xo_t = [ar.alloc(f"xo{i}", [128, 512], F32) for i in range(3)]
            yT = omix
            for f in range(16):
                p.dma("sp", yT[:, f, :], yT_d[f], (), [yT])
            load_w(wout_l, 0, 512, wb2[0])
            jobs = [(cg, i) for cg in range(4) for i in range(NT)]

            def xload(k):
                cg, i = jobs[k]
                xt = xt_t[k % 3]
                p.dma("sp", xt[:], x_src[i * 128:(i + 1) * 128, cg * 512:(cg + 1) * 512], (), [xt])
            xload(0)
            xload(1)
            for k, (cg, i) in enumerate(jobs):
                if i == 0 and cg + 1 < 4:
                    load_w(wout_l, (cg + 1) * 512, 512, wb2[(cg + 1) % 2])
                if k + 2 < len(jobs):
                    xload(k + 2)
                wt = wb2[cg % 2]
                cs = slice(cg * 512, (cg + 1) * 512)
                xt, xo = xt_t[k % 3], xo_t[k % 3]
                ps = nextps()
                for f in range(16):
                    mm(ps[:, :], yT[:, f, i * 128:(i + 1) * 128], wt[:, f, :], f == 0, f == 15, [yT, wt], [ps])
                A("dve", lambda e, xo=xo, ps=ps, xt=xt: e.tensor_tensor(out=xo[:], in0=ps[:, :], in1=xt[:], op=ALU.add), [ps, xt], [xo])
                p.dma("pool", x_dst[i * 128:(i + 1) * 128, cs], xo[:], [xo], ())
            p.barrier()

        if stop_after is None:
            ar.reset()
            xinF = [ar.alloc(f"xinF{i}", [128, D], F32) for i in range(2)]
            xoF = [ar.alloc(f"xoF{i}", [128, D], F32) for i in range(2)]
            gbcF = ar.alloc("gbcF", [128, D], F32)
            junkF = ar.alloc("junkF", [128, D], BF16)
            stF = [ar.alloc(f"nst{i}", [128, 4], F32) for i in range(2)]
            xf = xres[(nlayers - 1) % 2]
            p.dma("sp", gbcF[:], fg_d[0:1, :].partition_broadcast(128), (), [gbcF])
            for i in range(NT):
                xt, o_, s = xinF[i % 2], xoF[i % 2], stF[i % 2]
                p.dma("sp", xt[:], xf[i * 128:(i + 1) * 128, :], (), [xt])
                A("act", lambda e, xt=xt, s=s: e.activation(out=junkF[:], in_=xt[:], func=AF.Square, accum_out=s[:, 0:1]), [xt], [junkF, s])
                A("act", lambda e, s=s: e.activation(out=s[:, 1:2], in_=s[:, 0:1], func=AF.Sqrt, scale=1.0 / D, bias=c_eps[:]), [s, c_eps], [s])
                A("dve", lambda e, s=s: e.reciprocal(out=s[:, 2:3], in_=s[:, 1:2]), [s], [s])
                A("dve", lambda e, xt=xt, o_=o_, s=s: e.scalar_tensor_tensor(out=o_[:], in0=xt[:], scalar=s[:, 2:3], in1=gbcF[:],
                                                                              op0=ALU.mult, op1=ALU.mult), [xt, s, gbcF], [o_])
                p.dma("pool", out_d[i * 128:(i + 1) * 128, :], o_[:], [o_], ())
        p.barrier()
        counts = p.emit()
        _NC_CACHE['prog'] = p
    return nc, counts


_NC_CACHE = {}


def make_in_maps(inputs):
    f32 = np.float32
    x = np.asarray(inputs["x"], dtype=f32)
    mem = np.asarray(inputs["mem"], dtype=f32)
    shared = {k: np.ascontiguousarray(np.asarray(inputs[k], dtype=f32)) for k in
              ("norm_gain", "w_in", "w_gk_up", "b_gk", "gla_norm_gain", "b_f", "mem_norm_gain", "w_mem_kv", "w_branch", "w_out")}
    shared["final_gain"] = np.ascontiguousarray(np.asarray(inputs["final_gain"], dtype=f32).reshape(1, D))
    in_maps = []
    for c in range(8):
        b, half = c // 2, c % 2
        flags = np.zeros((128, 2), f32)
        flags[:, 0] = 0.0 if half == 1 else -30000.0
        flags[:, 1] = 1.0 if half == 1 else 0.0
        m = dict(shared)
        m["x"] = np.ascontiguousarray(x[b, half * TOK:(half + 1) * TOK])
        m["mem"] = np.ascontiguousarray(mem[b])
        m["flags"] = flags
        in_maps.append(m)
    return in_maps


def kernel(**inputs):
    if "nc" not in _NC_CACHE:
        _NC_CACHE["nc"] = build()[0]
    nc = _NC_CACHE["nc"]
    in_maps = make_in_maps(inputs)
    res = run_bass_kernel_spmd(nc, in_maps, core_ids=list(range(8)))
    out = np.empty((4, 4096, D), np.float32)
    for c in range(8):
        b, half = c // 2, c % 2
        out[b, half * TOK:(half + 1) * TOK] = np.asarray(res.results[c]["out"], dtype=np.float32)
    return out
```
